# Optimizing a Trainium2 kernel written in Bass

```python
import math
import jax
import jax.numpy as jnp
from jax import lax
import numpy as np

D_MODEL = 1024
BATCH = 4
SEQ = 8192
DEPTH = 2
DEC_BATCH = 32
DEC_SEQ = 1
PAST_LEN = 16384
PAGE_SIZE = 128

N_EVEN = (DEPTH + 1) // 2
N_ODD = DEPTH // 2

GLA_HEADS = 4
GLA_DK = 64
GLA_DV = 128
GLA_RANK = 16
GLA_TAU = 16.0
GLA_CHUNK = 64
NSA_HEADS = 8
NSA_KV_HEADS = 2
NSA_GROUP = NSA_HEADS // NSA_KV_HEADS
NSA_DH = 64
NSA_BRANCHES = 3
NSA_BLOCK = 64
NSA_TOP_N = 16
NSA_WINDOW = 512
NSA_CMP_HID = 64
Q_BLOCK = 128
SEL_Q_BLOCK = 64
M_HEADS = 4
M_DQK = 128
M_DV = 256
M_CHUNK = 64
D_FF = 2816
CONV_W = 3

EPS = 1e-6
TINY = 1e-30

EVEN_SPLITS = (GLA_HEADS * GLA_DK, GLA_HEADS * GLA_DK, GLA_HEADS * GLA_DV, GLA_HEADS * GLA_DV, GLA_RANK,
               NSA_HEADS * NSA_DH, NSA_BRANCHES * 2 * NSA_KV_HEADS * NSA_DH, NSA_HEADS * NSA_BRANCHES)
ODD_SPLITS = (M_HEADS * M_DQK, M_HEADS * M_DQK, M_HEADS * M_DV, M_HEADS * M_DV, M_HEADS, M_HEADS)
MIX_EVEN = GLA_HEADS * GLA_DV + NSA_HEADS * NSA_DH
MIX_ODD = M_HEADS * M_DV

kernel_name = 'gla_nsa_mlstm_convffn_step'


def _rmsnorm(x, g):
    xf = x.astype(jnp.float32)
    y = xf * lax.rsqrt(jnp.mean(xf * xf, axis=-1, keepdims=True) + EPS)
    return (y * g.astype(jnp.float32)).astype(x.dtype)


def _head_rmsnorm(x, g):
    x = x.astype(jnp.float32)
    return x * lax.rsqrt(jnp.mean(x * x, axis=-1, keepdims=True) + EPS) * g.astype(jnp.float32)


def _split(z, sizes):
    cuts = [int(c) for c in np.cumsum(sizes)[:-1]]
    return jnp.split(z, cuts, axis=-1)


def _to_chunks(a, nc, c):
    b = a.shape[0]
    a = a.reshape((b, nc, c) + a.shape[2:])
    return a.transpose((1, 0, 3, 2) + tuple(range(4, a.ndim)))


def _from_chunks(a):
    nc, b, h, c, d = a.shape
    return a.transpose(1, 0, 3, 2, 4).reshape(b, nc * c, h, d)


def _sweep(fn, qb, *arrs):
    b, n = arrs[0].shape[:2]
    if n <= qb or n % qb:
        return fn(jnp.int32(0), *arrs)
    nb = n // qb
    blocks = tuple(a.reshape((b, nb, qb) + a.shape[2:]).swapaxes(0, 1) for a in arrs)
    out = lax.map(lambda xs: fn(xs[0] * qb, *xs[1:]), (jnp.arange(nb, dtype=jnp.int32),) + blocks)
    out = out.swapaxes(0, 1)
    return out.reshape((b, n) + out.shape[3:])


def _gla_chunked(q, k, v, log_a, s0):
    t = q.shape[1]
    c = GLA_CHUNK if t % GLA_CHUNK == 0 else t
    nc = t // c
    tri = jnp.tril(jnp.ones((c, c), dtype=bool))[:, :, None]

    def step(s, inp):
        qc, kc, vc, la = inp
        cum = jnp.cumsum(la, axis=2)
        o_inter = jnp.einsum('bhcd,bhde->bhce', qc * jnp.exp(cum), s)
        diff = cum[:, :, :, None, :] - cum[:, :, None, :, :]
        decay = jnp.exp(jnp.where(tri, diff, -jnp.inf))
        att = jnp.einsum('bhid,bhijd,bhjd->bhij', qc, decay, kc)
        o = o_inter + jnp.einsum('bhij,bhje->bhie', att, vc)
        last = cum[:, :, -1:, :]
        s_new = jnp.exp(last[:, :, 0, :])[..., None] * s + jnp.einsum('bhcd,bhce->bhde', kc * jnp.exp(last - cum), vc)
        return s_new, o

    s_fin, o = lax.scan(step, s0, (_to_chunks(q, nc, c), _to_chunks(k, nc, c), _to_chunks(v, nc, c), _to_chunks(log_a, nc, c)))
    return _from_chunks(o), s_fin


def _mlstm_chunked(q, k, v, ig, lf, c0, n0, m0):
    t = q.shape[1]
    L = M_CHUNK if t % M_CHUNK == 0 else t
    nc = t // L
    tri = jnp.tril(jnp.ones((L, L), dtype=bool))

    def step(carry, inp):
        cm, nv, m = carry
        qc, kc, vc, ic, fc = inp
        cum = jnp.cumsum(fc, axis=-1)
        dlog = jnp.where(tri, cum[..., :, None] - cum[..., None, :] + ic[..., None, :], -jnp.inf)
        inter = cum + m[..., None]
        mi = jnp.maximum(inter, jnp.max(dlog, axis=-1))
        w = jnp.exp(dlog - mi[..., None])
        wi = jnp.exp(inter - mi)
        s = jnp.einsum('bhid,bhjd->bhij', qc, kc) * w
        num = wi[..., None] * jnp.einsum('bhid,bhde->bhie', qc, cm) + jnp.einsum('bhij,bhje->bhie', s, vc)
        qn = wi * jnp.einsum('bhid,bhd->bhi', qc, nv) + jnp.sum(s, axis=-1)
        h = num / jnp.maximum(jnp.abs(qn), jnp.exp(-mi))[..., None]
        last = cum[..., -1]
        gl = last[..., None] - cum + ic
        m_new = jnp.maximum(last + m, jnp.max(gl, axis=-1))
        wj = jnp.exp(gl - m_new[..., None])
        keep = jnp.exp(last + m - m_new)
        c_new = keep[..., None, None] * cm + jnp.einsum('bhl,bhld,bhle->bhde', wj, kc, vc)
        n_new = keep[..., None] * nv + jnp.einsum('bhl,bhld->bhd', wj, kc)
        return (c_new, n_new, m_new), h

    (cf, nf, mf), h = lax.scan(step, (c0, n0, m0), (_to_chunks(q, nc, L), _to_chunks(k, nc, L), _to_chunks(v, nc, L),
                                                   _to_chunks(ig, nc, L), _to_chunks(lf, nc, L)))
    return _from_chunks(h), cf, nf, mf


def _nsa_compress(rows, pe, w1, w2):
    b, l = rows.shape[:2]
    n = l // NSA_BLOCK
    blk = rows[:, :n * NSA_BLOCK].reshape((b, n, NSA_BLOCK) + rows.shape[2:])
    blk = blk + pe.transpose(1, 0, 2)[:, :, None, :]
    hid = jax.nn.gelu(jnp.einsum('bnsckd,csde->bncke', blk, w1))
    return jnp.einsum('bncke,ced->bnckd', hid, w2)


def _nsa_cmp_attend(q, kv_cmp, tpos):
    n = kv_cmp.shape[1]
    kc = kv_cmp.astype(jnp.float32)
    s = jnp.einsum('btkgd,bnkd->btkgn', q, kc[:, :, 0])
    vis = (((jnp.arange(n) + 1) * NSA_BLOCK - 1)[None, :] <= tpos[:, None])[None, :, None, None, :]
    s = jnp.where(vis, s, -jnp.inf)
    m = jnp.max(s, axis=-1, keepdims=True)
    m = jnp.where(jnp.isfinite(m), m, 0.0)
    e = jnp.exp(s - m)
    p = e / jnp.maximum(jnp.sum(e, axis=-1, keepdims=True), TINY)
    return jnp.einsum('btkgn,bnkd->btkgd', p, kc[:, :, 1]), p


def _nsa_select(p, tpos, n_blocks):
    imp = jnp.sum(p, axis=3)
    imp = jnp.pad(imp, ((0, 0), (0, 0), (0, 0), (0, n_blocks - imp.shape[-1])))
    blk = jnp.arange(n_blocks)[None, :]
    cur = (tpos // NSA_BLOCK)[:, None]
    forced = (blk == 0) | (blk == cur) | (blk == cur - 1)
    allowed = blk <= cur
    score = jnp.where(forced[None, :, None, :], jnp.inf, jnp.where(allowed[None, :, None, :], imp, -jnp.inf))
    idx = lax.top_k(score, NSA_TOP_N)[1]
    valid = idx <= cur[None, :, :, None]
    return idx, valid


def _nsa_sel_block(qc, tq, idx, valid, gather):
    b, qn = qc.shape[:2]
    kvb = gather(idx).astype(jnp.float32)
    kpos = idx[..., None] * NSA_BLOCK + jnp.arange(NSA_BLOCK)
    ok = valid[..., None] & (kpos <= tq[None, :, None, None, None])
    s = jnp.einsum('bqkgd,bqknsd->bqkgns', qc, kvb[..., 0, :])
    s = jnp.where(ok[:, :, :, None], s, -jnp.inf).reshape(b, qn, NSA_KV_HEADS, NSA_GROUP, -1)
    p = jax.nn.softmax(s, axis=-1)
    vals = kvb[..., 1, :].reshape(b, qn, NSA_KV_HEADS, -1, NSA_DH)
    return jnp.einsum('bqkgm,bqkmd->bqkgd', p, vals)


def _nsa_win_block(qc, p0, kwp, kw_pos0):
    qn = qc.shape[1]
    span = NSA_WINDOW + qn
    seg = lax.dynamic_slice_in_dim(kwp, p0 - kw_pos0, span, axis=1).astype(jnp.float32)
    kpos = p0 - NSA_WINDOW + jnp.arange(span)
    tq = p0 + jnp.arange(qn)
    ok = (kpos[None, :] >= kw_pos0) & (kpos[None, :] <= tq[:, None]) & (tq[:, None] - kpos[None, :] < NSA_WINDOW)
    s = jnp.einsum('bqkgd,bmkd->bqkgm', qc, seg[:, :, 0])
    s = jnp.where(ok[None, :, None, None, :], s, -jnp.inf)
    p = jax.nn.softmax(s, axis=-1)
    return jnp.einsum('bqkgm,bmkd->bqkgd', p, seg[:, :, 1])


def _nsa(q, kv3, gates, cmp_pe, cmp_w1, cmp_w2, past):
    b, t = q.shape[:2]
    q = q.astype(jnp.float32).reshape(b, t, NSA_KV_HEADS, NSA_GROUP, NSA_DH) * NSA_DH ** -0.5
    kv_c, kv_s, kv_w = kv3[:, :, 0], kv3[:, :, 1], kv3[:, :, 2]
    bi = jnp.arange(b)[:, None, None, None, None, None]
    hi = jnp.arange(NSA_KV_HEADS)[None, None, :, None, None, None]
    kvi = jnp.arange(2)
    offs = jnp.arange(NSA_BLOCK)
    if past is None:
        q_pos0 = 0
        cmp_rows = kv_c
        n_blocks = max(-(-t // NSA_BLOCK), NSA_TOP_N)
        sel_rows = jnp.pad(kv_s, ((0, 0), (0, n_blocks * NSA_BLOCK - t), (0, 0), (0, 0), (0, 0)))

        def gather(idx):
            rows = idx[..., None] * NSA_BLOCK + offs
            return sel_rows[bi, rows[..., None], kvi, hi]

        win_rows, kw_pos0, n_keep = kv_w, 0, min(NSA_WINDOW, t)
    else:
        cmp_pool, sel_pool, win_buf, page_table = past
        q_pos0 = PAST_LEN
        past_c = cmp_pool[page_table].reshape((b, PAST_LEN) + cmp_pool.shape[2:])
        cmp_rows = jnp.concatenate([past_c, kv_c], axis=1)
        n_blocks = max(-(-(PAST_LEN + t) // NSA_BLOCK), NSA_TOP_N)
        n_past_blk = PAST_LEN // NSA_BLOCK
        n_new_blk = -(-t // NSA_BLOCK)
        new_rows = jnp.pad(kv_s, ((0, 0), (0, n_new_blk * NSA_BLOCK - t), (0, 0), (0, 0), (0, 0)))

        def gather(idx):
            start = jnp.minimum(idx, n_past_blk - 1) * NSA_BLOCK
            phys = page_table[bi[..., 0, 0], start // PAGE_SIZE]
            rows = (start % PAGE_SIZE)[..., None] + offs
            old = sel_pool[phys[..., None, None], rows[..., None], kvi, hi]
            nrows = jnp.clip(idx - n_past_blk, 0, n_new_blk - 1)[..., None] * NSA_BLOCK + offs
            new = new_rows[bi, nrows[..., None], kvi, hi]
            return jnp.where((idx < n_past_blk)[..., None, None, None], old, new)

        win_rows = jnp.concatenate([win_buf, kv_w], axis=1)
        n_keep = win_buf.shape[1]
        kw_pos0 = PAST_LEN - n_keep
    tpos = q_pos0 + jnp.arange(t)
    kv_cmp = _nsa_compress(cmp_rows, cmp_pe, cmp_w1, cmp_w2)
    o_cmp, p_cmp = _nsa_cmp_attend(q, kv_cmp, tpos)
    idx, valid = _nsa_select(p_cmp, tpos, n_blocks)

    def sel_fn(off, qc, ic, vc):
        tq = q_pos0 + off + jnp.arange(qc.shape[1])
        return _nsa_sel_block(qc, tq, ic, vc, gather)

    o_sel = _sweep(sel_fn, SEL_Q_BLOCK, q, idx, valid)
    kwp = jnp.pad(win_rows, ((0, 0), (NSA_WINDOW, 0), (0, 0), (0, 0), (0, 0)))

    def win_fn(off, qc):
        return _nsa_win_block(qc, q_pos0 + off, kwp, kw_pos0)

    o_win = _sweep(win_fn, Q_BLOCK, q)
    o = gates[..., 0:1] * o_cmp + gates[..., 1:2] * o_sel + gates[..., 2:3] * o_win
    return o.reshape(b, t, NSA_HEADS * NSA_DH), kv_c, kv_s, win_rows[:, win_rows.shape[1] - n_keep:]


def _even_mixer(xn, w_in, w_out, gla_w_a2, gla_b_a, gla_norm, cmp_pe, cmp_w1, cmp_w2, gate_b, gla_state, past):
    b, t, _ = xn.shape
    z = (xn @ w_in).astype(jnp.float32)
    gq, gk, gv, gg, ga, nq, nkv, ngate = _split(z, EVEN_SPLITS)
    gq = gq.reshape(b, t, GLA_HEADS, GLA_DK) * GLA_DK ** -0.5
    gk = gk.reshape(b, t, GLA_HEADS, GLA_DK)
    gv = gv.reshape(b, t, GLA_HEADS, GLA_DV)
    log_a = jax.nn.log_sigmoid(ga @ gla_w_a2.astype(jnp.float32) + gla_b_a).reshape(b, t, GLA_HEADS, GLA_DK) / GLA_TAU
    o_gla, s_new = _gla_chunked(gq, gk, gv, log_a, gla_state.astype(jnp.float32))
    o_gla = (_head_rmsnorm(o_gla, gla_norm) * jax.nn.silu(gg).reshape(b, t, GLA_HEADS, GLA_DV)).reshape(b, t, -1)
    kv3 = nkv.astype(xn.dtype).reshape(b, t, NSA_BRANCHES, 2, NSA_KV_HEADS, NSA_DH)
    gates = jax.nn.sigmoid(ngate + gate_b).reshape(b, t, NSA_KV_HEADS, NSA_GROUP, NSA_BRANCHES)
    o_nsa, cmp_new, sel_new, win_new = _nsa(nq.reshape(b, t, NSA_HEADS, NSA_DH), kv3, gates, cmp_pe, cmp_w1, cmp_w2, past)
    y = jnp.concatenate([o_gla, o_nsa], axis=-1).astype(xn.dtype) @ w_out
    return y, s_new, cmp_new, sel_new, win_new


def _odd_mixer(xn, w_in, w_out, b_i, b_f, m_norm, c0, n0, m0):
    b, t, _ = xn.shape
    z = (xn @ w_in).astype(jnp.float32)
    q, k, v, o, ig, fg = _split(z, ODD_SPLITS)
    q = q.reshape(b, t, M_HEADS, M_DQK)
    k = k.reshape(b, t, M_HEADS, M_DQK) * M_DQK ** -0.5
    v = v.reshape(b, t, M_HEADS, M_DV)
    h, cf, nf, mf = _mlstm_chunked(q, k, v, ig + b_i, jax.nn.log_sigmoid(fg + b_f),
                                   c0.astype(jnp.float32), n0.astype(jnp.float32), m0.astype(jnp.float32))
    h = _head_rmsnorm(h, m_norm.reshape(M_HEADS, M_DV)) * jax.nn.sigmoid(o).reshape(b, t, M_HEADS, M_DV)
    return h.reshape(b, t, -1).astype(xn.dtype) @ w_out, cf, nf, mf


def _conv_ffn(xn, w_up, conv_w, conv_b, w_down, conv_state):
    t = xn.shape[1]
    gpre, up = jnp.split(xn @ w_up, 2, axis=-1)
    full = jnp.concatenate([conv_state.astype(gpre.dtype), gpre], axis=1)
    acc = conv_b + full[:, 0:t] * conv_w[0]
    for j in range(1, CONV_W):
        acc = acc + full[:, j:j + t] * conv_w[j]
    h = jax.nn.gelu(acc) * up
    return h @ w_down, full[:, full.shape[1] - (CONV_W - 1):]


def setup_inputs(seed: int = 0) -> dict:
    key = jax.random.key(seed)
    ks = iter(jax.random.split(key, 48))

    def nrm(shape, scale=1.0):
        return jax.random.normal(next(ks), shape, jnp.float32) * scale

    def gain(shape):
        return 1.0 + nrm(shape, 0.02)

    n_pages = PAST_LEN // PAGE_SIZE
    n_phys = (5 * DEC_BATCH * n_pages + 3) // 4
    w_buf = min(NSA_WINDOW, PAST_LEN)
    kv_pool = (N_EVEN, n_phys, PAGE_SIZE, 2, NSA_KV_HEADS, NSA_DH)
    x_prompt = nrm((BATCH, SEQ, D_MODEL))
    x_sample = nrm((DEC_BATCH, DEC_SEQ, D_MODEL))
    cache_cmp_kv = nrm(kv_pool)
    cache_sel_kv = nrm(kv_pool)
    cache_win_kv = nrm((N_EVEN, DEC_BATCH, w_buf, 2, NSA_KV_HEADS, NSA_DH))
    state_gla = nrm((N_EVEN, DEC_BATCH, GLA_HEADS, GLA_DK, GLA_DV), 0.3)
    state_mlstm_c = nrm((N_ODD, DEC_BATCH, M_HEADS, M_DQK, M_DV), 0.3)
    state_mlstm_n = nrm((N_ODD, DEC_BATCH, M_HEADS, M_DQK), 0.3)
    state_mlstm_m = nrm((N_ODD, DEC_BATCH, M_HEADS), 1.0)
    state_ffn_conv = nrm((DEPTH, DEC_BATCH, CONV_W - 1, D_FF))
    page_table = jax.random.permutation(next(ks), n_phys)[:DEC_BATCH * n_pages].reshape(DEC_BATCH, n_pages).astype(jnp.int32)
    p_even = sum(EVEN_SPLITS)
    p_odd = sum(ODD_SPLITS)
    return {
        'x_prompt': x_prompt,
        'x_sample': x_sample,
        'cache_cmp_kv': cache_cmp_kv,
        'cache_sel_kv': cache_sel_kv,
        'cache_win_kv': cache_win_kv,
        'state_gla': state_gla,
        'state_mlstm_c': state_mlstm_c,
        'state_mlstm_n': state_mlstm_n,
        'state_mlstm_m': state_mlstm_m,
        'state_ffn_conv': state_ffn_conv,
        'page_table': page_table,
        'norm_mix': gain((DEPTH, D_MODEL)),
        'norm_ffn': gain((DEPTH, D_MODEL)),
        'norm_final': gain((D_MODEL,)),
        'even_w_in': nrm((N_EVEN, D_MODEL, p_even), D_MODEL ** -0.5),
        'even_w_out': nrm((N_EVEN, MIX_EVEN, D_MODEL), MIX_EVEN ** -0.5),
        'gla_w_a2': nrm((N_EVEN, GLA_RANK, GLA_HEADS * GLA_DK), GLA_RANK ** -0.5),
        'gla_b_a': nrm((N_EVEN, GLA_HEADS * GLA_DK), 0.1),
        'gla_norm': gain((N_EVEN, GLA_DV)),
        'nsa_cmp_pe': nrm((N_EVEN, 2, NSA_BLOCK, NSA_DH), 0.1),
        'nsa_cmp_w1': nrm((N_EVEN, 2, NSA_BLOCK, NSA_DH, NSA_CMP_HID), (NSA_BLOCK * NSA_DH) ** -0.5),
        'nsa_cmp_w2': nrm((N_EVEN, 2, NSA_CMP_HID, NSA_DH), NSA_CMP_HID ** -0.5),
        'nsa_gate_b': nrm((N_EVEN, NSA_HEADS * NSA_BRANCHES), 0.1),
        'odd_w_in': nrm((N_ODD, D_MODEL, p_odd), D_MODEL ** -0.5),
        'odd_w_out': nrm((N_ODD, MIX_ODD, D_MODEL), MIX_ODD ** -0.5),
        'mlstm_b_i': nrm((N_ODD, M_HEADS), 0.1),
        'mlstm_b_f': jnp.linspace(3.0, 6.0, M_HEADS, dtype=jnp.float32)[None, :] + nrm((N_ODD, M_HEADS), 0.1),
        'mlstm_norm': gain((N_ODD, MIX_ODD)),
        'ffn_w_up': nrm((DEPTH, D_MODEL, 2 * D_FF), D_MODEL ** -0.5),
        'ffn_conv_w': nrm((DEPTH, CONV_W, D_FF), CONV_W ** -0.5),
        'ffn_conv_b': nrm((DEPTH, D_FF), 0.02),
        'ffn_w_down': nrm((DEPTH, D_FF, D_MODEL), D_FF ** -0.5),
    }


def reference(x_prompt, x_sample, cache_cmp_kv, cache_sel_kv, cache_win_kv, state_gla, state_mlstm_c, state_mlstm_n,
              state_mlstm_m, state_ffn_conv, page_table, norm_mix, norm_ffn, norm_final, even_w_in, even_w_out,
              gla_w_a2, gla_b_a, gla_norm, nsa_cmp_pe, nsa_cmp_w1, nsa_cmp_w2, nsa_gate_b, odd_w_in, odd_w_out,
              mlstm_b_i, mlstm_b_f, mlstm_norm, ffn_w_up, ffn_conv_w, ffn_conv_b, ffn_w_down):
    f32 = jnp.float32
    xp, xs = x_prompt, x_sample
    bp = xp.shape[0]
    cmp_p, cmp_s, sel_p, sel_s, win_p, win_s, gla_p, gla_s = [], [], [], [], [], [], [], []
    mc_p, mc_s, mn_p, mn_s, mm_p, mm_s, cv_p, cv_s = [], [], [], [], [], [], [], []
    for l in range(DEPTH):
        if l % 2 == 0:
            e = l // 2
            prm = (even_w_in[e], even_w_out[e], gla_w_a2[e], gla_b_a[e], gla_norm[e],
                   nsa_cmp_pe[e], nsa_cmp_w1[e], nsa_cmp_w2[e], nsa_gate_b[e])
            d, s_, c_, k_, w_ = _even_mixer(_rmsnorm(xp, norm_mix[l]), *prm,
                                            jnp.zeros((bp, GLA_HEADS, GLA_DK, GLA_DV), f32), None)
            xp = xp + d
            gla_p.append(s_); cmp_p.append(c_); sel_p.append(k_); win_p.append(w_)
            d, s_, c_, k_, w_ = _even_mixer(_rmsnorm(xs, norm_mix[l]), *prm, state_gla[e],
                                            (cache_cmp_kv[e], cache_sel_kv[e], cache_win_kv[e], page_table))
            xs = xs + d
            gla_s.append(s_); cmp_s.append(c_); sel_s.append(k_); win_s.append(w_)
        else:
            o = l // 2
            prm = (odd_w_in[o], odd_w_out[o], mlstm_b_i[o], mlstm_b_f[o], mlstm_norm[o])
            d, c_, n_, m_ = _odd_mixer(_rmsnorm(xp, norm_mix[l]), *prm,
                                       jnp.zeros((bp, M_HEADS, M_DQK, M_DV), f32),
                                       jnp.zeros((bp, M_HEADS, M_DQK), f32), jnp.zeros((bp, M_HEADS), f32))
            xp = xp + d
            mc_p.append(c_); mn_p.append(n_); mm_p.append(m_)
            d, c_, n_, m_ = _odd_mixer(_rmsnorm(xs, norm_mix[l]), *prm,
                                       state_mlstm_c[o], state_mlstm_n[o], state_mlstm_m[o])
            xs = xs + d
            mc_s.append(c_); mn_s.append(n_); mm_s.append(m_)
        fp = (ffn_w_up[l], ffn_conv_w[l], ffn_conv_b[l], ffn_w_down[l])
        d, cv = _conv_ffn(_rmsnorm(xp, norm_ffn[l]), *fp, jnp.zeros((bp, CONV_W - 1, D_FF), xp.dtype))
        xp = xp + d
        cv_p.append(cv)
        d, cv = _conv_ffn(_rmsnorm(xs, norm_ffn[l]), *fp, state_ffn_conv[l])
        xs = xs + d
        cv_s.append(cv)
    y_prompt = _rmsnorm(xp, norm_final)
    y_sample = _rmsnorm(xs, norm_final)
    return (y_prompt, y_sample,
            jnp.stack(cmp_p), jnp.stack(cmp_s), jnp.stack(sel_p), jnp.stack(sel_s),
            jnp.stack(win_p), jnp.stack(win_s), jnp.stack(gla_p), jnp.stack(gla_s),
            jnp.stack(mc_p), jnp.stack(mc_s), jnp.stack(mn_p), jnp.stack(mn_s),
            jnp.stack(mm_p), jnp.stack(mm_s), jnp.stack(cv_p), jnp.stack(cv_s))
```

```python
import contextlib
import numpy as np
import concourse.bass as bass
import concourse.mybir as mybir
from concourse.bass_utils import run_bass_kernel_spmd

F32 = mybir.dt.float32
BF16 = mybir.dt.bfloat16
AF = mybir.ActivationFunctionType
ALU = mybir.AluOpType
AX = mybir.AxisListType

ENGS = ("pe", "act", "dve", "pool", "sp")
D = 1024
DFF = 2816
NCH = 22
EPS = 1e-6
BIG = 30000.0


class Buf:
    __slots__ = ("w", "r_eng", "r_dma", "excl")

    def __init__(self):
        self.excl = False
        self.w = None
        self.r_eng = {}
        self.r_dma = []


class Rec:
    def __init__(self, nc):
        self.nc = nc
        self.ops = []
        self.eng_ops = {e: [] for e in ENGS}
        self.n_dma_sems = {"sp": 24, "pool": 16, "act": 4}
        self.dma_rr = {e: 0 for e in ENGS}
        self.dma_last = {}
        self.dma_uses = {}

    def op(self, eng, fn, reads=(), writes=(), dma=False):
        oid = len(self.ops)
        deps = set()
        xr = [b for b in reads if b.excl]
        if xr:
            reads = [b for b in reads if not b.excl]
            writes = list(writes) + xr
        for b in reads:
            if b.w is not None:
                deps.add(b.w)
        for b in writes:
            if b.w is not None:
                deps.add(b.w)
            deps.update(b.r_eng.values())
            deps.update(b.r_dma)
        o = {"id": oid, "eng": eng, "fn": fn, "dma": dma, "needed": False}
        if dma:
            slot = self.dma_rr[eng] % self.n_dma_sems[eng]
            self.dma_rr[eng] += 1
            key = (eng, slot)
            if key in self.dma_last:
                deps.add(self.dma_last[key])
            self.dma_last[key] = oid
            self.dma_uses[key] = self.dma_uses.get(key, 0) + 1
            o["slot"] = key
            o["val"] = 16 * self.dma_uses[key]
            o["needed"] = True
        if eng == "pe" and not dma:
            deps = {d for d in deps if not (self.ops[d]["eng"] == "pe" and not self.ops[d]["dma"])}
        deps.discard(oid)
        o["deps"] = deps
        for d in deps:
            self.ops[d]["needed"] = True
        for b in reads:
            if dma:
                b.r_dma.append(oid)
            else:
                b.r_eng[eng] = oid
        for b in writes:
            b.w = oid
            b.r_eng = {}
            b.r_dma = []
        self.ops.append(o)
        self.eng_ops[eng].append(o)
        return oid

    def emit(self):
        nc = self.nc
        cnt = {e: 0 for e in ENGS}
        for e in ENGS:
            for o in self.eng_ops[e]:
                if not o["dma"] and o["needed"]:
                    cnt[e] += 1
                    o["val"] = cnt[e]
        with contextlib.ExitStack() as st:
            esem = {e: st.enter_context(nc.semaphore("s_" + e)) for e in ENGS}
            dsem = {}
            for e, n in self.n_dma_sems.items():
                for i in range(n):
                    dsem[(e, i)] = st.enter_context(nc.semaphore("d_%s_%d" % (e, i)))
            block = st.enter_context(nc.Block())
            ops = self.ops

            def semval(o):
                if o["dma"]:
                    return dsem[o["slot"]], o["val"]
                return esem[o["eng"]], o["val"]

            def run(engname, handle):
                seen = {}
                for o in self.eng_ops[engname]:
                    need = {}
                    for d in o["deps"]:
                        s, v = semval(ops[d])
                        k = id(s)
                        if seen.get(k, 0) >= v:
                            continue
                        if k not in need or need[k][1] < v:
                            need[k] = (s, v)
                    for k, (s, v) in need.items():
                        handle.wait_ge(s, v)
                        seen[k] = v
                    ins = o["fn"](handle)
                    if o["dma"]:
                        ins.then_inc(dsem[o["slot"]], 16)
                    elif o["needed"]:
                        ins.then_inc(esem[engname], 1)
                if engname == "sp":
                    for key, n in self.dma_uses.items():
                        handle.wait_ge(dsem[key], 16 * n)
                    for e in ENGS:
                        if cnt[e] > 0:
                            handle.wait_ge(esem[e], cnt[e])

            @block.tensor
            def _(h):
                run("pe", h)

            @block.scalar
            def _(h):
                run("act", h)

            @block.vector
            def _(h):
                run("dve", h)

            @block.gpsimd
            def _(h):
                run("pool", h)

            @block.sync
            def _(h):
                run("sp", h)


class V:
    __slots__ = ("ap", "bufs")

    def __init__(self, ap, bufs):
        self.ap = ap
        self.bufs = bufs


class TB:
    def __init__(self, t):
        self.t = t
        self.buf = Buf()

    def __getitem__(self, idx):
        return V(self.t[idx], [self.buf])

    def v(self, idx, bufs):
        return V(self.t[idx], list(bufs))


def _bufs(*vs):
    out = []
    for v in vs:
        if isinstance(v, V):
            out.extend(v.bufs)
    return out


def _ap(v):
    return v.ap if isinstance(v, V) else v


class Prog:
    def __init__(self, T, parts=("gla", "nsa", "mlstm"), debug=False, samp=False, nphys=5120):
        self.T = T
        self.samp = samp
        self.NPHYS = nphys
        self.parts = parts
        self.debug = debug
        import os
        self.stage = int(os.environ.get("NSA_STAGE", "5"))
        self.skip = os.environ.get("SKIP", "").split(",")
        self.nc = bass.Bass("TRN2", target_bir_lowering=False)
        self.R = Rec(self.nc)
        self.st = contextlib.ExitStack()
        self.dram = {}
        self.rr = 0
        self.wrr = 0
        self.aoff = {}
        self.abufs = {}
        self.cur_phase = {}
        self.ARENA_B = 37 * 1024
        self.ARENA = 70 * 1024

    def din(self, name, shape, dt=F32):
        t = self.nc.dram_tensor(name, list(shape), dt, kind="ExternalInput")
        self.dram[name] = t
        return V(t.ap(), [Buf()])

    def dout(self, name, shape, dt=F32):
        t = self.nc.dram_tensor(name, list(shape), dt, kind="ExternalOutput")
        self.dram[name] = t
        return V(t.ap(), [Buf()])

    def sb(self, name, shape, dt=F32):
        return TB(self.st.enter_context(self.nc.sbuf_tensor(name, list(shape), dt)))

    def sba(self, phase, name, shape, dt=F32):
        n = 1
        for v in shape[1:]:
            n *= v
        nb = n * (2 if dt == BF16 else 4)
        nb = (nb + 63) // 64 * 64
        off = self.aoff.get(phase, 0)
        self.aoff[phase] = off + nb
        ar = self.arena_b if phase in ("KP", "KS") else self.arena
        cap = self.ARENA_B if phase in ("KP", "KS") else self.ARENA
        assert off + nb <= cap, (phase, name, off + nb)
        ap = ar.t[0:shape[0], off // 4:(off + nb) // 4]
        if dt != F32:
            ap = ap.bitcast(dt)
        ap = ap[:, 0:n]
        if len(shape) == 3:
            ap = ap.rearrange("p (a b) -> p a b", a=shape[1])
        elif len(shape) == 4:
            ap = ap.rearrange("p (a b c) -> p a b c", a=shape[1], b=shape[2])
        tb = TB(ap)
        self.abufs.setdefault(phase, []).append(tb.buf)
        return tb

    def phase(self, new):
        akey = "B" if new in ("KP", "KS") else "A"
        old = self.cur_phase.get(akey)
        if old == new:
            return
        self.cur_phase[akey] = new
        r_eng, r_dma = {}, []
        for b in self.abufs.get(old, []):
            cands = list(b.r_eng.items())
            if b.w is not None:
                o = self.R.ops[b.w]
                if o["dma"]:
                    r_dma.append(b.w)
                else:
                    cands.append((o["eng"], b.w))
            for e, oid in cands:
                if r_eng.get(e, -1) < oid:
                    r_eng[e] = oid
            r_dma.extend(b.r_dma)
        for b in self.abufs.get(new, []):
            b.w = None
            b.r_eng = dict(r_eng)
            b.r_dma = list(r_dma)

    def psum(self, name, shape, dt=F32):
        tb = TB(self.st.enter_context(self.nc.psum_tensor(name, list(shape), dt)))
        tb.buf.excl = True
        return tb

    def dma(self, out, in_, eng="sp", nc_ok=False):
        o, i = _ap(out), _ap(in_)
        if nc_ok:
            fn = lambda e: e.dma_start(out=o, in_=i, allow_slow_non_contiguous=True)
        else:
            fn = lambda e: e.dma_start(out=o, in_=i)
        self.R.op(eng, fn, reads=_bufs(in_), writes=_bufs(out), dma=True)

    def mm(self, out, lhsT, rhs, start=True, stop=True):
        o, l, r = _ap(out), _ap(lhsT), _ap(rhs)
        self.R.op("pe", lambda e: e.matmul(o, l, r, start=start, stop=stop, skip_group_check=True),
                  reads=_bufs(lhsT, rhs), writes=_bufs(out))

    def tr(self, out, in_, ident):
        o, i, d = _ap(out), _ap(in_), _ap(ident)
        self.R.op("pe", lambda e: e.transpose(o, i, d), reads=_bufs(in_, ident), writes=_bufs(out))

    def act(self, out, in_, func=None, bias=0.0, scale=1.0, accum=None):
        func = func or AF.Copy
        o, i, b, s, a = _ap(out), _ap(in_), _ap(bias), _ap(scale), _ap(accum)
        kw = {}
        if a is not None:
            kw["accum_out"] = a
        self.R.op("act", lambda e: e.activation(out=o, in_=i, func=func, bias=b, scale=s, **kw),
                  reads=_bufs(in_, bias, scale), writes=_bufs(out, accum))

    def ts(self, out, in0, s1, s2=None, op0=ALU.mult, op1=None, eng="dve", accum=None):
        o, i, a1, a2, ac = _ap(out), _ap(in0), _ap(s1), _ap(s2), _ap(accum)
        kw = {}
        if op1 is not None:
            kw["op1"] = op1
        if ac is not None:
            kw["accum_out"] = ac
        self.R.op(eng, lambda e: e.tensor_scalar(o, i, a1, a2, op0, **kw),
                  reads=_bufs(in0, s1, s2), writes=_bufs(out, accum))

    def tt(self, out, in0, in1, op, eng="dve"):
        o, a, b = _ap(out), _ap(in0), _ap(in1)
        self.R.op(eng, lambda e: e.tensor_tensor(o, a, b, op), reads=_bufs(in0, in1), writes=_bufs(out))

    def stt(self, out, in0, scalar, in1, op0, op1, eng="dve"):
        o, a, s, b = _ap(out), _ap(in0), _ap(scalar), _ap(in1)
        self.R.op(eng, lambda e: e.scalar_tensor_tensor(o, a, s, b, op0, op1),
                  reads=_bufs(in0, scalar, in1), writes=_bufs(out))

    def copy(self, out, in_, eng="dve"):
        o, i = _ap(out), _ap(in_)
        self.R.op(eng, lambda e: e.tensor_copy(o, i), reads=_bufs(in_), writes=_bufs(out))

    def memset(self, out, val, eng="dve"):
        o = _ap(out)
        self.R.op(eng, lambda e: e.memset(o, val), writes=_bufs(out))

    def recip(self, out, in_):
        o, i = _ap(out), _ap(in_)
        self.R.op("dve", lambda e: e.reciprocal(o, i), reads=_bufs(in_), writes=_bufs(out))

    def rsqrt(self, out, in_):
        self.act(out, in_, AF.Ln)
        self.act(out, out, AF.Exp, scale=-0.5)

    def evac(self, out, in_, scale=1.0):
        self.rr += 1
        if self.rr % 2 == 0:
            self.act(out, in_, AF.Copy, scale=scale)
        else:
            if scale == 1.0:
                self.copy(out, in_)
            else:
                self.ts(out, in_, scale, None, ALU.mult)

    def pp(self):
        self.prr = getattr(self, "prr", 0) + 1
        return self.pbanks[self.prr % len(self.pbanks)]

    def pa(self):
        self.par = getattr(self, "par", 0) + 1
        return self.paccs[self.par % len(self.paccs)]

    def ppb(self):
        self.prb = getattr(self, "prb", 0) + 1
        return self.pbf[self.prb % len(self.pbf)]

    def wslot(self):
        self.wrr += 1
        return self.wslots[self.wrr % len(self.wslots)]

    def wload(self, src, shape):
        slot = self.wslot()
        n = 1
        for s in shape[1:]:
            n *= s
        dst = slot.t[:, 0:n]
        if len(shape) == 3:
            dst = dst.rearrange("p (a b) -> p a b", a=shape[1])
        v = V(dst, [slot.buf])
        self.dma(v, src, eng="pool")
        return v

    def build(self):
        T = self.T
        NB = T // 512
        nc = self.nc
        P = self
        xp = P.din("xp", [T, D])
        gmix = [P.din("gmix%d" % l, [128, 8]) for l in range(2)]
        gffn = [P.din("gffn%d" % l, [128, 8]) for l in range(2)]
        gfin = P.din("gfin", [128, D])
        NE = 3624
        NT = T // 128
        BF = BF16
        identb_d = P.din("identb", [128, 128], BF)
        cmpb_d = P.din("cmpb", [NT, 128, 128], BF)
        eall_d = P.din("eall", [128, NT, 128], BF)
        causal_d = P.din("causal", [128, 512], BF)
        winlow_d = P.din("winlow", [128, 512], BF)
        candb_d = P.din("candb", [NT, 128, 128])
        forced_d = P.din("forced", [NT, 128, 128])
        allowed_d = P.din("allowed", [NT, 128, 128])
        w1d_d = P.din("w1d", [2, 2, 128, 4096])
        w2_d = P.din("w2", [128, 256])
        peT_d = P.din("peT", [128, 128])
        gateb_d = P.din("gateb", [128, 24])
        wE = P.din("wE", [D, NE])
        wEo = P.din("wEo", [D, D])
        NO = 3592
        wO = P.din("wO", [D, NO])
        mnorm_d = P.din("mnorm", [128, D])
        bif_d = P.din("bif", [128, 8])
        bicol_d = P.din("bicol", [4, 1])
        sel127_d = P.din("sel127", [128, 128])
        trilb_d = P.din("trilb", [128, 128])
        onehot_d = P.din("onehot", [4, 512])
        mc_out = P.dout("mlc", [4, 128, 256])
        mn_out = P.dout("mln", [128, 4])
        mm_out = P.dout("mlm", [1, 4])
        wOo = P.din("wOo", [D, D])
        wU = [P.din("wU%d" % l, [D, 2 * DFF]) for l in range(2)]
        wD = [P.din("wD%d" % l, [DFF, D]) for l in range(2)]
        cw = [P.din("cw%d" % l, [128, 4, NCH]) for l in range(2)]
        ident_d = P.din("ident", [128, 128])
        mtri_d = P.din("mtri", [128, 128])
        mstr_d = P.din("mstr", [128, 128])
        wa2_d = P.din("wa2", [16, 256])
        ba_d = P.din("ba", [128, 256])
        gnorm_d = P.din("gnorm", [128, 512])
        y_out = P.dout("y", [T, D])
        kvc_out = P.dout("kvc", [T, 256])
        kvs_out = P.dout("kvs", [T, 256])
        kvw_out = P.dout("kvw", [512, 256])
        gla_out = P.dout("glast", [256, 128])
        cv_out = [P.dout("cv%d" % l, [2, DFF]) for l in range(2)]
        if self.debug:
            dbg_out = P.dout("dbg", [T, D])
            dbg2_out = P.dout("dbg2", [T, D])

        self.arena = P.sb("arena", [128, self.ARENA // 4])
        self.arena_b = P.sb("arena_b", [128, self.ARENA_B // 4])
        P.phase("KP")
        x = P.sb("x", [128, 4, D])
        xnT = P.sb("xnT", [128, 8, 512], BF16)
        xn = [P.sb("xn%d" % i, [128, D], BF16) for i in range(2)]
        junk = P.sb("junk", [128, D], BF16)
        col = [P.sb("col%d" % i, [128, 8]) for i in range(4)]
        self.wslots = [P.sb("ws%d" % i, [128, 4096], BF16) for i in range(3)]
        self.pbanks = [P.psum("pb%d" % i, [128, 512]) for i in range(4)]
        self.paccs = [P.psum("pa%d" % i, [128, 512]) for i in range(2)]
        self.pbf = [P.psum("pbf%d" % i, [128, 1024], BF16) for i in range(2)]
        ident = P.sb("identb_s", [128, 128], BF16)
        identf = P.sb("identf", [128, 128])
        mtri = P.sb("mtrif", [128, 128])
        mstr = P.sb("mstrf", [128, 128])
        g_mix = [P.sb("g_mix%d" % l, [128, 8]) for l in range(2)]
        g_ffn = [P.sb("g_ffn%d" % l, [128, 8]) for l in range(2)]
        g_fin = P.sba("F", "g_fin", [128, D])
        cwt = [P.sb("cwt%d" % l, [128, 4, NCH]) for l in range(2)]
        halo = [P.sb("halo%d" % l, [128, NCH, 2]) for l in range(2)]
        hT = P.sba("F", "hT", [128, NCH, 512], BF16)
        gbuf = [P.sba("F", "gbuf%d" % i, [128, 514]) for i in range(2)]
        acc = [P.sba("F", "acc%d" % i, [128, 512]) for i in range(2)]
        mix = P.sb("mix", [128, 4, D], BF16)
        mixT = xnT
        self.colrr = 0

        def newcol():
            self.colrr += 1
            return col[self.colrr % len(col)]

        P.dma(ident[:], identb_d)
        P.dma(identf[:], ident_d)
        P.dma(mtri[:], mtri_d)
        P.dma(mstr[:], mstr_d)
        for l in range(2):
            P.dma(g_mix[l][:], gmix[l])
            P.dma(g_ffn[l][:], gffn[l])
            P.dma(cwt[l][:], cw[l])
            P.memset(halo[l][:], 0.0)
        P.memset(mix[:], 0.0)

        if "gla" in self.parts:
            wa2 = P.sb("wa2s", [16, 256], BF16)
            P.dma(wa2[:], wa2_d, eng="pool")
            ba = P.sb("bas", [128, 256])
            P.dma(ba[:], ba_d)
            gnorm = P.sb("gnorms", [128, 512])
            P.dma(gnorm[:], gnorm_d)
            S32 = [P.sb("S32_%d" % p, [128, 128]) for p in range(2)]
            Sbf = [P.sb("Sbf_%d" % p, [128, 128], BF16) for p in range(2)]
            for p in range(2):
                P.memset(S32[p][:], 0.0)
                P.memset(Sbf[p][:], 0.0)
            gqT = P.sba("E", "gqT", [128, 2, 512])
            gkT = P.sba("E", "gkT", [128, 2, 512])
            gaT = P.sba("E", "gaT", [16, 512], BF16)
            gk_tm = P.sba("E", "gk_tm", [128, 4, 256])
            gv_tm = P.sba("E", "gv_tm", [128, 4, 512], BF16)
            gsg = P.sba("E", "gsg", [128, 4, 512], BF16)
            spl = [P.sba("E", "spl%d" % i, [128, 256]) for i in range(2)]
            AT = [P.sba("E", "AT%d" % i, [128, 2, 128]) for i in range(2)]
            BT = [P.sba("E", "BT%d" % i, [128, 2, 128]) for i in range(2)]
            qd = [P.sba("E", "qd%d" % i, [128, 2, 128], BF16) for i in range(2)]
            kd = [P.sba("E", "kd%d" % i, [128, 2, 128], BF16) for i in range(2)]
            k2 = [P.sba("E", "k2%d" % i, [128, 256], BF16) for i in range(2)]
            erev = [P.sba("E", "erev%d" % i, [128, 256]) for i in range(2)]
            attT = [P.sba("E", "attT%d" % i, [128, 128], BF16) for i in range(2)]
            osb = [P.sba("E", "osb%d" % i, [128, 512]) for i in range(2)]

        if "nsa" in self.parts:
            nqT = P.sba("E", "nqT", [128, 4, 512], BF16)
            KselT = P.sba("KP", "KselT", [128, T], BF16)
            KwinT = P.sb("KwinT", [128, 1024], BF16)
            KcrT = P.sba("E", "KcrT", [128, 512], BF16)
            VcrT = P.sba("E", "VcrT", [128, 512], BF16)
            Vsel = P.sba("KP", "Vsel", [128, NT, 2, 65], BF16)
            Vwin = P.sb("Vwin", [128, 8, 2, 65], BF16)
            P.memset(Vsel[:], 1.0)
            P.memset(Vwin[:], 1.0, eng="pool")
            hidK = P.sb("hidK", [128, 128], BF16)
            hidV = P.sb("hidV", [128, 128], BF16)
            P.memset(hidK[:], 0.0)
            P.memset(hidV[:], 0.0)
            KcT = P.sb("KcT", [128, 128], BF16)
            Vc = P.sb("Vc", [128, 2, 64], BF16)
            eall = P.sb("eall_s", [128, NT, 128], mybir.dt.float8e4)
            P.dma(eall[:], eall_d, eng="pool")
            causal = P.sb("causal_s", [128, 512], BF16)
            P.dma(causal[:], causal_d)
            winlow = P.sb("winlow_s", [128, 512], BF16)
            P.dma(winlow[:], winlow_d)
            w2s = P.sb("w2s", [128, 256], BF16)
            P.dma(w2s[:], w2_d, eng="pool")
            peT = P.sb("peTs", [128, 128], BF16)
            P.dma(peT[:], peT_d, eng="pool")
            gateb = P.sb("gatebs", [128, 24])
            P.dma(gateb[:], gateb_d)
            hb = P.sb("hb", [128, 2])
            onesf = P.sb("onesf", [128, 128])
            P.memset(onesf[:], 1.0)
            gt = P.sba("E", "gt", [128, 4, 24])
            cmpb = [P.sba("E", "cmpb%d" % i, [128, 128], BF16) for i in range(2)]
            candb = [P.sba("E", "candb%d" % i, [128, 128]) for i in range(2)]
            forced = [P.sba("E", "forced%d" % i, [128, 128]) for i in range(2)]
            allowed = [P.sba("E", "allowed%d" % i, [128, 128]) for i in range(2)]
            eT = [P.sba("E", "eT%d" % i, [128, 512]) for i in range(2)]
            rT = [P.sba("E", "rT%d" % i, [128, 512]) for i in range(1)] * 2
            pTb = [P.sba("E", "pTb%d" % i, [128, 512], BF16) for i in range(2)]
            impT = [P.sba("E", "impT%d" % i, [128, 128]) for i in range(2)]
            cand = [P.sba("E", "cand%d" % i, [128, 128]) for i in range(2)]
            cand2 = [P.sba("E", "cand2%d" % i, [128, 128]) for i in range(2)]
            m8 = [P.sba("E", "m8%d" % i, [128, 16]) for i in range(2)]
            selbT = [P.sba("E", "selbT%d" % i, [128, 512], BF16) for i in range(2)]
            eTb = [P.sba("E", "eTb%d" % i, [128, 512], BF16) for i in range(3)]
            ocmp = [P.sba("E", "ocmp%d" % i, [128, 256]) for i in range(2)]
            nacc = [P.sba("E", "nacc%d" % i, [128, 256]) for i in range(2)]
            self.erb = 0
            for c in range(2 if "hb" not in self.skip else 0):
                ps = P.pp()
                for hf in range(2):
                    w = P.wload(V(w1d_d.ap[c, hf].rearrange("p (a b) -> p a b", a=32), w1d_d.bufs), [128, 32, 128])
                    for s_ in range(32):
                        ss = hf * 32 + s_
                        P.mm(ps[:, 0:1], V(w.ap[:, s_, :], w.bufs), peT[:, c * 64 + ss:c * 64 + ss + 1],
                             start=(ss == 0), stop=(ss == 63))
                P.copy(hb[:, c:c + 1], ps[:, 0:1])

        def nsa_compress(bt):
            if "compress" in self.skip:
                return
            n0 = bt * 8
            for c, (rows, hid) in enumerate(((KcrT, hidK), (VcrT, hidV))):
                ps = P.pp()
                for hf in range(2):
                    w = P.wload(V(w1d_d.ap[c, hf].rearrange("p (a b) -> p a b", a=32), w1d_d.bufs), [128, 32, 128])
                    for s_ in range(32):
                        ss = hf * 32 + s_
                        P.mm(ps[:, 0:8], V(w.ap[:, s_, :], w.bufs), V(rows.t[:, ss:512:64], [rows.buf]),
                             start=(ss == 0), stop=(ss == 63))
                P.act(hid[:, n0:n0 + 8], ps[:, 0:8], AF.Gelu_apprx_tanh, bias=hb[:, c:c + 1])
            psk = P.pp()
            P.mm(psk[:, 0:128], w2s[:, 0:128], hidK[:], True, True)
            P.evac(KcT[:], psk[:, 0:128])
            psv = P.pp()
            P.mm(psv[:, 0:128], hidV[:], w2s[:, 128:256], True, True)
            P.evac(Vc[:], V(psv.t[:, 0:128].rearrange("p (k d) -> p k d", k=2), [psv.buf]))

        def nsa_subtile(bt, s):
            if self.stage < 2:
                return
            qi = bt * 4 + s
            i2 = qi % 2
            P.dma(cmpb[i2][:], V(cmpb_d.ap[qi], cmpb_d.bufs))
            P.dma(candb[i2][:], V(candb_d.ap[qi], candb_d.bufs))
            P.dma(forced[i2][:], V(forced_d.ap[qi], forced_d.bufs))
            P.dma(allowed[i2][:], V(allowed_d.ap[qi], allowed_d.bufs))
            for k in range(2):
                kk = (qi * 2 + k) % 2
                qv = V(nqT.t[k * 64:(k + 1) * 64, :, s * 128:(s + 1) * 128], [nqT.buf])
                ps = P.pp()
                P.mm(ps[:, :], KcT[k * 64:(k + 1) * 64, :], qv, True, False)
                for g in range(4):
                    P.mm(ps[:, g * 128:(g + 1) * 128], ident[:], cmpb[i2][:], False, g == 3)
                e_ = eT[kk]
                P.act(e_[:], ps[:, :], AF.Exp)
                ps2 = P.pp()
                P.mm(ps2[:, :], onesf[:], e_[:], True, True)
                r_ = rT[kk]
                P.ts(r_[:], ps2[:, :], 1e-30, None, ALU.max)
                P.recip(r_[:], r_[:])
                P.tt(e_[:], e_[:], r_[:], ALU.mult)
                P.copy(pTb[kk][:], e_[:], eng="pool")
                im = impT[kk]
                P.tt(im[:], e_[:, 0:128], e_[:, 128:256], ALU.add, eng="pool")
                P.tt(im[:], im[:], e_[:, 256:384], ALU.add, eng="pool")
                P.tt(im[:], im[:], e_[:, 384:512], ALU.add, eng="pool")
                psi = P.pp()
                P.tr(psi[:, 0:128], im[:], identf[:])
                pso = P.pp()
                for g in range(4):
                    P.mm(pso[:, g * 64:(g + 1) * 64], pTb[kk][:, g * 128:(g + 1) * 128], Vc[:, k, :], True, True)
                P.act(ocmp[kk][:], pso[:, 0:256], AF.Copy)
                if self.stage < 3:
                    continue
                cd, cd2, m_ = cand[kk], cand2[kk], m8[kk]
                P.tt(cd[:], psi[:, 0:128], candb[i2][:], ALU.add)
                cda, cd2a, m1a, m2a = cd.t[:], cd2.t[:], m_.t[:, 0:8], m_.t[:, 8:16]
                P.R.op("dve", lambda e, o=m1a, i=cda: e.max(out=o, in_=i), reads=[cd.buf], writes=[m_.buf])
                P.R.op("dve", lambda e, o=cd2a, r=m1a, i=cda: e.match_replace(out=o, in_to_replace=r, in_values=i, imm_value=-1e30),
                       reads=[cd.buf, m_.buf], writes=[cd2.buf])
                P.R.op("dve", lambda e, o=m2a, i=cd2a: e.max(out=o, in_=i), reads=[cd2.buf], writes=[m_.buf])
                P.ts(cd2[:], cd[:], m_[:, 12:13], None, ALU.is_ge)
                P.tt(cd2[:], cd2[:], forced[i2][:], ALU.max)
                P.tt(cd2[:], cd2[:], allowed[i2][:], ALU.mult)
                P.ts(cd2[:], cd2[:], BIG, -BIG, ALU.mult, ALU.add)
                pst = P.pp()
                P.tr(pst[:, 0:128], cd2[:], identf[:])
                sb_ = selbT[kk]
                for g in range(4):
                    P.evac(sb_[:, g * 128:(g + 1) * 128], pst[:, 0:128])
                if self.stage < 4:
                    continue
                po = P.pa()
                for kt in range(qi + 1):
                    ps = P.pp()
                    P.mm(ps[:, :], KselT[k * 64:(k + 1) * 64, kt * 128:(kt + 1) * 128], qv, True, False)
                    P.mm(ps[:, :], eall[:, kt, :], sb_[:], False, kt != qi)
                    if kt == qi:
                        P.mm(ps[:, :], ident[:], causal[:], False, True)
                    self.erb += 1
                    eb = eTb[self.erb % 3]
                    P.act(eb[:], ps[:, :], AF.Exp)
                    for g in range(4):
                        P.mm(po[:, g * 65:(g + 1) * 65], eb[:, g * 128:(g + 1) * 128], Vsel[:, kt, k, :],
                             start=(kt == 0 and g == 0), stop=(kt == qi))
                c = newcol()
                na = nacc[kk]
                pov = V(po.t[:, 0:260].rearrange("p (g d) -> p g d", g=4), [po.buf])
                P.recip(c[:, 0:4], V(pov.ap[:, :, 64], pov.bufs))
                for g in range(4):
                    gc = k * 12 + g * 3
                    P.tt(c[:, 4 + g:5 + g], c[:, g:g + 1], gt[:, s, gc + 1:gc + 2], ALU.mult)
                    P.ts(na[:, g * 64:(g + 1) * 64], ocmp[kk][:, g * 64:(g + 1) * 64], gt[:, s, gc:gc + 1], None, ALU.mult)
                    P.stt(na[:, g * 64:(g + 1) * 64], V(pov.ap[:, g, 0:64], pov.bufs), c[:, 4 + g:5 + g],
                          na[:, g * 64:(g + 1) * 64], ALU.mult, ALU.add)
                if self.stage < 5:
                    continue
                pw = P.pa()
                k0 = max(0, qi - 4)
                for kt in range(k0, qi + 1):
                    ps = P.pp()
                    last = (kt == qi)
                    low = (kt == qi - 4)
                    P.mm(ps[:, :], KwinT[k * 64:(k + 1) * 64, (kt % 8) * 128:(kt % 8 + 1) * 128], qv, True, not (last or low))
                    if last:
                        P.mm(ps[:, :], ident[:], causal[:], False, True)
                    if low:
                        P.mm(ps[:, :], ident[:], winlow[:], False, True)
                    self.erb += 1
                    eb = eTb[self.erb % 3]
                    P.act(eb[:], ps[:, :], AF.Exp)
                    for g in range(4):
                        P.mm(pw[:, g * 65:(g + 1) * 65], eb[:, g * 128:(g + 1) * 128], Vwin[:, kt % 8, k, :],
                             start=(kt == k0 and g == 0), stop=(kt == qi))
                c = newcol()
                pwv = V(pw.t[:, 0:260].rearrange("p (g d) -> p g d", g=4), [pw.buf])
                P.recip(c[:, 0:4], V(pwv.ap[:, :, 64], pwv.bufs))
                for g in range(4):
                    gc = k * 12 + g * 3
                    P.tt(c[:, 4 + g:5 + g], c[:, g:g + 1], gt[:, s, gc + 2:gc + 3], ALU.mult)
                    f0 = 512 + (k * 4 + g) * 64
                    P.stt(mix[:, s, f0:f0 + 64], V(pwv.ap[:, g, 0:64], pwv.bufs), c[:, 4 + g:5 + g],
                          na[:, g * 64:(g + 1) * 64], ALU.mult, ALU.add)

        if "mlstm" in self.parts:
            mnorm = P.sb("mnorm_s", [128, D], BF16)
            P.dma(mnorm[:], mnorm_d, eng="pool")
            bif = P.sb("bif_s", [128, 8])
            P.dma(bif[:], bif_d)
            bicol = P.sb("bicol_s", [4, 1])
            P.dma(bicol[:], bicol_d)
            nbf = P.sb("nbf_s", [128, 4])
            P.ts(nbf[:], bif[:, 4:8], -1.0, None, ALU.mult)
            sel127 = P.sb("sel127_s", [128, 128])
            P.dma(sel127[:], sel127_d)
            trilb = P.sb("trilb_s", [128, 128])
            P.dma(trilb[:], trilb_d)
            onehot = P.sb("onehot_s", [4, 512])
            P.dma(onehot[:], onehot_d)
            C32 = P.sb("C32", [128, 4, 257])
            Cbf = P.sb("Cbf", [128, 4, 257], BF16)
            P.memset(C32[:], 0.0)
            P.memset(Cbf[:], 0.0)
            m_bc = P.sb("m_bc", [128, 4])
            P.memset(m_bc[:], 0.0)
            mqT = P.sba("O", "mqT", [128, 4, 512], BF16)
            mkT = P.sba("O", "mkT", [128, 4, 512], BF16)
            mk_tm = P.sba("O", "mk_tm", [128, 4, 512], BF16)
            v_aug = P.sba("O", "v_aug", [128, 4, 4, 257], BF16)
            og = P.sba("O", "og", [128, 4, D], BF16)
            icT = P.sba("O", "icT", [4, 512])
            if_tm = P.sba("O", "if_tm", [128, 4, 8])
            sc = [P.sba("O", "sc%d" % i, [128, 64]) for i in range(2)]
            a_row = [P.sba("O", "a_row%d" % i, [4, 128]) for i in range(2)]
            swT = [P.sba("O", "swT%d" % i, [128, 128], BF16) for i in range(2)]
            t1b = [P.sba("O", "t1b%d" % i, [128, 257]) for i in range(2)]
            numb = [P.sba("O", "numb%d" % i, [128, 257]) for i in range(2)]
            hhb = [P.sba("O", "hhb%d" % i, [128, 256]) for i in range(2)]
            kwb = [P.sba("O", "kwb%d" % i, [128, 128], BF16) for i in range(2)]

        def mlstm_subtile(s):
            i2 = s % 2
            c_ = sc[i2]
            sl = slice(s * 128, (s + 1) * 128)
            P.tt(c_[:, 0:4], if_tm[:, s, 0:4], bif[:, 0:4], ALU.add)
            P.act(c_[:, 4:8], if_tm[:, s, 4:8], AF.Exp, scale=-1.0)
            P.act(c_[:, 44:48], nbf[:], AF.Exp)
            P.tt(c_[:, 4:8], c_[:, 4:8], c_[:, 44:48], ALU.mult)
            P.act(c_[:, 4:8], c_[:, 4:8], AF.Ln, bias=1.0)
            psc = P.pp()
            P.mm(psc[:, 0:4], mtri[:], c_[:, 4:8], True, True)
            psr = P.pp()
            P.mm(psr[0:4, 0:128], c_[:, 4:8], mtri[:], True, True)
            P.copy(c_[:, 8:12], psc[:, 0:4])
            P.tt(c_[:, 12:16], c_[:, 0:4], c_[:, 8:12], ALU.add)
            ar = a_row[i2]
            P.tt(ar[:], psr[0:4, 0:128], icT[:, sl], ALU.add)
            for h in range(4):
                psm = P.pp()
                P.mm(psm[:, 0:128], onehot[:, h * 128:(h + 1) * 128], ar[:], True, False)
                P.mm(psm[:, 0:128], identf[:], trilb[:], False, True)
                P.R.op("dve", lambda e, o=c_.t[:, 16 + h:17 + h], i=psm.t[:, 0:128]: e.reduce_max(o, i, AX.X),
                       reads=[], writes=[c_.buf, psm.buf])
            P.tt(c_[:, 16:20], c_[:, 16:20], m_bc[:], ALU.max)
            P.act(c_[:, 20:24], c_[:, 12:16], AF.Exp)
            P.act(c_[:, 24:28], c_[:, 16:20], AF.Exp, scale=-1.0)
            P.tt(c_[:, 44:48], m_bc[:], c_[:, 16:20], ALU.subtract)
            P.act(c_[:, 28:32], c_[:, 44:48], AF.Exp)
            P.tt(c_[:, 44:48], c_[:, 8:12], c_[:, 16:20], ALU.subtract)
            P.act(c_[:, 32:36], c_[:, 44:48], AF.Exp)
            psb = P.pp()
            P.mm(psb[:, 0:12], sel127[:], c_[:, 8:20], True, True)
            P.copy(c_[:, 48:60], psb[:, 0:12])
            P.tt(c_[:, 44:48], c_[:, 12:16], c_[:, 56:60], ALU.subtract)
            P.act(c_[:, 36:40], c_[:, 44:48], AF.Exp)
            P.tt(c_[:, 44:48], m_bc[:], c_[:, 56:60], ALU.subtract)
            P.act(c_[:, 40:44], c_[:, 44:48], AF.Exp)
            P.tt(m_bc[:], c_[:, 56:60], c_[:, 48:52], ALU.subtract)
            for h in range(4):
                hi = (s * 4 + h) % 2
                ps = P.pp()
                P.mm(ps[:, 0:128], mkT[:, h, sl], mqT[:, h, sl], True, True)
                sw = swT[hi]
                P.stt(sw[:], ps[:, 0:128], c_[:, 20 + h:21 + h], mtri[:], ALU.mult, ALU.mult)
                p1 = P.pp()
                P.mm(p1[:, 0:257], sw[:], v_aug[:, s, h, :], True, True)
                p2 = P.pp()
                P.mm(p2[:, 0:257], mqT[:, h, sl], Cbf[:, h, :], True, True)
                t1, nu, hh = t1b[hi], numb[hi], hhb[hi]
                P.act(t1[:], p2[:, 0:257], AF.Copy, scale=c_[:, 28 + h:29 + h])
                P.stt(nu[:], p1[:, 0:257], c_[:, 24 + h:25 + h], t1[:], ALU.mult, ALU.add)
                cc = newcol()
                P.ts(cc[:, 3:4], nu[:, 256:257], -1.0, None, ALU.mult)
                P.tt(cc[:, 0:1], nu[:, 256:257], cc[:, 3:4], ALU.max)
                P.tt(cc[:, 0:1], cc[:, 0:1], c_[:, 32 + h:33 + h], ALU.max)
                P.recip(cc[:, 0:1], cc[:, 0:1])
                P.ts(hh[:], nu[:, 0:256], cc[:, 0:1], None, ALU.mult)
                P.act(junk[:, 0:256], hh[:], AF.Square, accum=cc[:, 1:2])
                P.ts(cc[:, 2:3], cc[:, 1:2], 1.0 / 256, EPS, ALU.mult, ALU.add)
                P.rsqrt(cc[:, 2:3], cc[:, 2:3])
                P.stt(mix[:, s, h * 256:(h + 1) * 256], hh[:], cc[:, 2:3], og[:, s, h * 256:(h + 1) * 256], ALU.mult, ALU.mult)
                kw = kwb[hi]
                P.ts(kw[:], mk_tm[:, s, h * 128:(h + 1) * 128], c_[:, 36 + h:37 + h], None, ALU.mult)
                pc = P.pp()
                P.mm(pc[:, 0:257], kw[:], v_aug[:, s, h, :], True, True)
                P.stt(C32[:, h, :], C32[:, h, :], c_[:, 40 + h:41 + h], pc[:, 0:257], ALU.mult, ALU.add)
                P.copy(Cbf[:, h, :], C32[:, h, :], eng="pool")

        def odd_layer(bt):
            P.phase("O")
            rmsnorm_T(g_mix[1])
            P.memset(V(v_aug.t[:, :, :, 256:257], [v_aug.buf]), 1.0)
            w = P.wload(wsrc(wO, 0, 512), [128, 8, 512])
            for h in range(4):
                proj_fm(w, h * 128, 128, lambda ps, h=h: P.evac(mqT[:, h, :], ps[:, :]))
            w = P.wload(wsrc(wO, 512, 512), [128, 8, 512])
            for h in range(4):
                proj_fm(w, h * 128, 128, lambda ps, h=h: P.evac(mkT[:, h, :], ps[:, :], scale=128 ** -0.5))
            w = P.wload(wsrc(wO, 1024, 512), [128, 8, 512])
            for s in range(4):
                proj_tm(w, 0, 512, s, lambda ps, s=s: P.evac(mk_tm[:, s, :], ps[:, :], scale=128 ** -0.5))
            for hf in range(2):
                w = P.wload(wsrc(wO, 1536 + hf * 512, 512), [128, 8, 512])
                for s in range(4):
                    proj_tm(w, 0, 512, s, lambda ps, s=s, hf=hf: P.evac(
                        V(v_aug.t[:, s, 2 * hf:2 * hf + 2, 0:256], [v_aug.buf]),
                        V(ps.t[:, :].rearrange("p (h e) -> p h e", h=2), [ps.buf])))
            for hf in range(2):
                w = P.wload(wsrc(wO, 2560 + hf * 512, 512), [128, 8, 512])
                for s in range(4):
                    def f(ps, s=s, hf=hf):
                        dst = og[:, s, hf * 512:(hf + 1) * 512]
                        P.act(dst, ps[:, :], AF.Sigmoid)
                        P.tt(dst, dst, mnorm[:, hf * 512:(hf + 1) * 512], ALU.mult, eng="pool")
                    proj_tm(w, 0, 512, s, f)
            w = P.wload(wsrc(wO, 3584, 8), [128, 8, 8])
            proj_fm(w, 0, 4, lambda ps: P.act(icT[:], ps[0:4, :], AF.Identity, bias=bicol[:, 0:1]))
            for s in range(4):
                proj_tm(w, 0, 8, s, lambda ps, s=s: P.copy(if_tm[:, s, :], ps[:, 0:8]))
            for s in range(4):
                mlstm_subtile(s)
            out_proj(wOo, 1)

        def rmsnorm_T(gain, bt_rows=4):
            for s in range(bt_rows):
                c = newcol()
                P.act(junk[:], x[:, s, :], AF.Square, accum=c[:, 0:1])
                P.ts(c[:, 1:2], c[:, 0:1], 1.0 / D, EPS, ALU.mult, ALU.add)
                P.rsqrt(c[:, 2:3], c[:, 1:2])
                xs = xn[s % 2]
                P.ts(xs[:], x[:, s, :], c[:, 2:3], None, ALU.mult)
                pt = P.ppb()
                for k in range(8):
                    P.tr(pt[:, k * 128:(k + 1) * 128], xs[:, k * 128:(k + 1) * 128], ident[:])
                for k in range(8):
                    if k % 2 == 0:
                        P.act(xnT[:, k, s * 128:(s + 1) * 128], pt[:, k * 128:(k + 1) * 128], AF.Copy, scale=gain[:, k:k + 1])
                    else:
                        P.ts(xnT[:, k, s * 128:(s + 1) * 128], pt[:, k * 128:(k + 1) * 128], gain[:, k:k + 1], None, ALU.mult)

        def proj_fm(w, c0, n, out_fn):
            ps = P.pp()
            for k in range(8):
                P.mm(ps[0:n, :], V(w.ap[:, k, c0:c0 + n], w.bufs), xnT[:, k, :], start=(k == 0), stop=(k == 7))
            out_fn(ps)

        def proj_tm(w, c0, n, s, out_fn):
            ps = P.pp()
            for k in range(8):
                P.mm(ps[:, 0:n], xnT[:, k, s * 128:(s + 1) * 128], V(w.ap[:, k, c0:c0 + n], w.bufs),
                     start=(k == 0), stop=(k == 7))
            out_fn(ps)

        def wsrc(wd, c0, n):
            return V(wd.ap.rearrange("(k p) c -> p k c", p=128)[:, :, c0:c0 + n], wd.bufs)

        def out_proj(wd, l):
            for s in range(4):
                pt = P.ppb()
                for k in range(8):
                    P.tr(pt[:, k * 128:(k + 1) * 128], mix[:, s, k * 128:(k + 1) * 128], ident[:])
                P.evac(mixT[:, :, s * 128:(s + 1) * 128], V(pt.t[:, :].rearrange("p (k t) -> p k t", k=8), [pt.buf]))
            for half in range(2):
                w = P.wload(wsrc(wd, half * 512, 512), [128, 8, 512])
                for s in range(4):
                    ps = P.pp()
                    for k in range(8):
                        P.mm(ps[:, :], mixT[:, k, s * 128:(s + 1) * 128], V(w.ap[:, k, :], w.bufs),
                             start=(k == 0), stop=(k == 7))
                    P.tt(x[:, s, half * 512:(half + 1) * 512], x[:, s, half * 512:(half + 1) * 512], ps[:, :], ALU.add)

        def ffn(l, last):
            P.phase("F")
            rmsnorm_T(g_ffn[l])
            for j in range(11):
                w = P.wload(wsrc(wU[l], j * 512, 512), [128, 8, 512])
                for i in range(2):
                    c = 2 * j + i
                    psg = P.pp()
                    for k in range(8):
                        P.mm(psg[:, :], V(w.ap[:, k, i * 128:(i + 1) * 128], w.bufs), xnT[:, k, :], start=(k == 0), stop=(k == 7))
                    psu = P.pp()
                    for k in range(8):
                        P.mm(psu[:, :], V(w.ap[:, k, 256 + i * 128:256 + (i + 1) * 128], w.bufs), xnT[:, k, :],
                             start=(k == 0), stop=(k == 7))
                    gb = gbuf[c % 2]
                    ac = acc[c % 2]
                    P.act(gb[:, 2:514], psg[:, :], AF.Copy)
                    P.copy(gb[:, 0:2], halo[l][:, c, :], eng="pool")
                    P.copy(halo[l][:, c, :], gb[:, 512:514], eng="pool")
                    P.ts(ac[:], gb[:, 2:514], cwt[l][:, 2, c:c + 1], cwt[l][:, 3, c:c + 1], ALU.mult, ALU.add)
                    P.stt(ac[:], gb[:, 1:513], cwt[l][:, 1, c:c + 1], ac[:], ALU.mult, ALU.add)
                    P.stt(ac[:], gb[:, 0:512], cwt[l][:, 0, c:c + 1], ac[:], ALU.mult, ALU.add)
                    P.act(ac[:], ac[:], AF.Gelu_apprx_tanh)
                    P.tt(hT[:, c, :], ac[:], psu[:, :], ALU.mult)
            for q in range(8):
                w = P.wload(V(wD[l].ap.rearrange("(c p) n -> p c n", p=128)[:, :, q * 128:(q + 1) * 128], wD[l].bufs),
                            [128, NCH, 128])
                for s in range(4):
                    ps = P.pp()
                    for c in range(NCH):
                        P.mm(ps[:, 0:128], hT[:, c, s * 128:(s + 1) * 128], V(w.ap[:, c, :], w.bufs),
                             start=(c == 0), stop=(c == NCH - 1))
                    P.tt(x[:, s, q * 128:(q + 1) * 128], x[:, s, q * 128:(q + 1) * 128], ps[:, 0:128], ALU.add)
            if last:
                for r in range(2):
                    P.dma(V(cv_out[l].ap[r, :].rearrange("(c p) -> p c", p=128), cv_out[l].bufs), halo[l][:, :, r], nc_ok=True)

        def gla_subtile(s):
            i2 = s % 2
            ps = P.pp()
            P.mm(ps[:, 0:256], gaT[0:16, s * 128:(s + 1) * 128], wa2[:], True, True)
            sp_ = spl[i2]
            P.tt(sp_[:], ps[:, 0:256], ba[:], ALU.add)
            P.act(sp_[:], sp_[:], AF.Exp, scale=-1.0)
            P.act(sp_[:], sp_[:], AF.Ln, bias=1.0)
            psc = P.pp()
            for p in range(2):
                P.mm(psc[:, p * 128:(p + 1) * 128], sp_[:, p * 128:(p + 1) * 128], mtri[:], True, True)
            psr = P.pp()
            P.mm(psr[:, 0:256], mstr[:], sp_[:], True, True)
            a_, b_ = AT[i2], BT[i2]
            pv = V(psc.t[:, 0:256].rearrange("p (c t) -> p c t", c=2), [psc.buf])
            P.act(a_[:], pv, AF.Exp, scale=-1.0 / 16.0)
            P.act(b_[:], pv, AF.Exp, scale=1.0 / 16.0)
            P.act(erev[i2][:], psr[:, 0:256], AF.Exp, scale=-1.0 / 16.0)
            q_, k_ = qd[i2], kd[i2]
            P.stt(q_[:], gqT[:, :, s * 128:(s + 1) * 128], 0.125, a_[:], ALU.mult, ALU.mult)
            P.tt(k_[:], gkT[:, :, s * 128:(s + 1) * 128], b_[:], ALU.mult)
            P.tt(k2[i2][:], gk_tm[:, s, :], erev[i2][:], ALU.mult)
            pso = P.pa()
            for h in range(4):
                p, off = h // 2, (h % 2) * 64
                psa = P.pp()
                P.mm(psa[:, 0:128], k_[off:off + 64, p, :], q_[off:off + 64, p, :], True, True)
                at = attT[h % 2]
                P.tt(at[:], psa[:, 0:128], mtri[:], ALU.mult)
                P.mm(pso[:, h * 128:(h + 1) * 128], at[:], gv_tm[:, s, h * 128:(h + 1) * 128], True, False)
                P.mm(pso[:, h * 128:(h + 1) * 128], q_[off:off + 64, p, :], Sbf[p][off:off + 64, :], False, True)
            o_ = osb[i2]
            c = newcol()
            for h in range(4):
                P.act(o_[:, h * 128:(h + 1) * 128], pso[:, h * 128:(h + 1) * 128], AF.Square, accum=c[:, h:h + 1])
            P.ts(c[:, 4:8], c[:, 0:4], 1.0 / 128, EPS, ALU.mult, ALU.add)
            P.rsqrt(c[:, 4:8], c[:, 4:8])
            for h in range(4):
                P.stt(mix[:, s, h * 128:(h + 1) * 128], pso[:, h * 128:(h + 1) * 128], c[:, 4 + h:5 + h],
                      gsg[:, s, h * 128:(h + 1) * 128], ALU.mult, ALU.mult)
            for p in range(2):
                pss = P.pp()
                P.mm(pss[:, 0:256], k2[i2][:, p * 128:(p + 1) * 128], gv_tm[:, s, p * 256:(p + 1) * 256], True, True)
                for hh in range(2):
                    r0 = hh * 64
                    P.stt(S32[p][r0:r0 + 64, :], S32[p][r0:r0 + 64, :], a_[r0:r0 + 64, p, 127:128],
                          pss[r0:r0 + 64, hh * 128:(hh + 1) * 128], ALU.mult, ALU.add)
                P.copy(Sbf[p][:], S32[p][:], eng="pool")

        def even_layer(bt):
            nsa = "nsa" in self.parts
            P.phase("E")
            rmsnorm_T(g_mix[0])
            w = P.wload(wsrc(wE, 0, 512), [128, 8, 512])
            for c in range(2):
                proj_fm(w, c * 128, 128, lambda ps, c=c: P.evac(gqT[:, c, :], ps[:, :]))
                proj_fm(w, 256 + c * 128, 128, lambda ps, c=c: P.evac(gkT[:, c, :], ps[:, :]))
            if nsa and "g12" not in self.skip:
                w = P.wload(wsrc(wE, 512, 512), [128, 8, 512])
                for g in range(4):
                    proj_fm(w, g * 128, 128, lambda ps, g=g: P.evac(nqT[:, g, :], ps[:, :], scale=0.125))
                w = P.wload(wsrc(wE, 1024, 512), [128, 8, 512])
                proj_fm(w, 0, 128, lambda ps: P.evac(KcrT[:], ps[:, :]))
                proj_fm(w, 128, 128, lambda ps: P.evac(KselT[:, bt * 512:(bt + 1) * 512], ps[:, :]))
                proj_fm(w, 256, 128, lambda ps: P.evac(KwinT[:, (bt % 2) * 512:(bt % 2 + 1) * 512], ps[:, :]))
                proj_fm(w, 384, 128, lambda ps: P.evac(VcrT[:], ps[:, :]))
            w = P.wload(wsrc(wE, 1536, 296), [128, 8, 296])
            proj_fm(w, 280, 16, lambda ps: P.evac(gaT[:, :], ps[0:16, :]))
            for s in range(4):
                def f(ps, s=s):
                    P.evac(gk_tm[:, s, :], ps[:, 0:256])
                    if nsa and "gate" not in self.skip:
                        P.tt(gt[:, s, :], ps[:, 256:280], gateb[:], ALU.add)
                        P.act(gt[:, s, :], gt[:, s, :], AF.Exp, scale=-1.0)
                        P.ts(gt[:, s, :], gt[:, s, :], 1.0, None, ALU.add)
                        P.recip(gt[:, s, :], gt[:, s, :])
                proj_tm(w, 0, 280, s, f)
            w = P.wload(wsrc(wE, 1832, 512), [128, 8, 512])
            for s in range(4):
                proj_tm(w, 0, 512, s, lambda ps, s=s: P.evac(gv_tm[:, s, :], ps[:, :]))
            w = P.wload(wsrc(wE, 2344, 512), [128, 8, 512])
            for s in range(4):
                def f(ps, s=s):
                    P.act(gsg[:, s, :], ps[:, :], AF.Silu)
                    P.tt(gsg[:, s, :], gsg[:, s, :], gnorm[:], ALU.mult, eng="pool")
                proj_tm(w, 0, 512, s, f)
            w = P.wload(wsrc(wE, 2856, 512), [128, 8, 512])
            w2 = P.wload(wsrc(wE, 3368, 256), [128, 8, 256])
            for s in range(4):
                t0 = bt * 512 + s * 128
                qi = bt * 4 + s
                def f(ps, t0=t0, s=s, qi=qi):
                    o_ = osb[s % 2]
                    P.evac(o_[:], ps[:, :])
                    P.dma(V(kvc_out.ap[t0:t0 + 128, :], kvc_out.bufs), o_[:, 0:256])
                    P.dma(V(kvs_out.ap[t0:t0 + 128, :], kvs_out.bufs), o_[:, 256:512])
                    if nsa and "vcopy" not in self.skip:
                        P.copy(Vsel[:, qi, :, 0:64], V(o_.t[:, 384:512].rearrange("p (k d) -> p k d", k=2), [o_.buf]), eng="pool")
                proj_tm(w, 0, 512, s, f)
                if (nsa and "g7all" not in self.skip) or t0 >= T - 512:
                    def f2(ps, t0=t0, s=s, qi=qi):
                        e_ = erev[s % 2]
                        P.evac(e_[:], ps[:, 0:256])
                        if t0 >= T - 512:
                            r0 = t0 - (T - 512)
                            P.dma(V(kvw_out.ap[r0:r0 + 128, :], kvw_out.bufs), e_[:])
                        if nsa and "vcopy" not in self.skip:
                            P.copy(Vwin[:, qi % 8, :, 0:64], V(e_.t[:, 128:256].rearrange("p (k d) -> p k d", k=2), [e_.buf]), eng="pool")
                    proj_tm(w2, 0, 256, s, f2)
            if nsa:
                nsa_compress(bt)
                for s in range(4):
                    nsa_subtile(bt, s)
            if "gla" in self.parts:
                for s in range(4):
                    gla_subtile(s)
            if self.debug:
                for s in range(4):
                    t0 = bt * 512 + s * 128
                    P.dma(V(dbg2_out.ap[t0:t0 + 128, :], dbg2_out.bufs), mix[:, s, :], eng="pool")
            out_proj(wEo, 0)
            if self.debug:
                for s in range(4):
                    t0 = bt * 512 + s * 128
                    P.dma(V(dbg_out.ap[t0:t0 + 128, :], dbg_out.bufs), x[:, s, :])


        if self.samp:
            I32 = mybir.dt.int32
            NPG = 128
            xs_d = P.din("xs", [4, D])
            sgla_d = P.din("sgla", [4, 4, 64, 128])
            smc_d = P.din("smc", [4, 4, 128, 256])
            smn_d = P.din("smn", [4, 4, 128])
            smm_d = P.din("smm", [4, 4])
            scv_d = P.din("scv", [2, 4, 2, DFF])
            swin_d = P.din("swin", [4, 512, 256])
            cpool_d = P.din("cpool", [self.NPHYS * 128, 256])
            spool_d = P.din("spool", [self.NPHYS * 128, 256])
            ptT_d = P.din("ptT", [128, 4], I32)
            ptrep_d = P.din("ptrep", [128, 4, 128], I32)
            riota_d = P.din("riota", [128, 128])
            jcol_d = P.din("jcol", [128, 1])
            eye4_d = P.din("eye4", [4, 4])
            cmask_d = P.din("cmask", [128, 16])
            hc_d = P.din("hc", [2, 512], BF16)
            colsel_d = P.din("colsel", [4, 16], BF16)
            mnew_d = P.din("mnew", [128, 8], BF16)
            candbs_d = P.din("candbs", [2, 256])
            forceds_d = P.din("forceds", [2, 256])
            cwrep_d = [P.din("cwrep%d" % l, [4, 4, DFF]) for l in range(2)]
            gfin4_d = gfin
            ys_out = P.dout("ys", [4, D])
            kvcs_out = P.dout("kvcs", [4, 256])
            kvss_out = P.dout("kvss", [4, 256])
            kvws_out = P.dout("kvws", [4, 512, 256])
            glas_out = P.dout("glas", [4, 4, 64, 128])
            mlcs_out = P.dout("mlcs", [4, 4, 128, 256])
            mlns_out = P.dout("mlns", [4, 4, 128])
            mlms_out = P.dout("mlms", [4, 4])
            cvs_out = P.dout("cvs", [2, 4, 2, DFF])

            SA = lambda name, shape, dt=F32: P.sba("S", name, shape, dt)
            SB = lambda name, shape, dt=F32: P.sba("KS", name, shape, dt)
            eye4 = SA("eye4s", [4, 4])
            cmask = SA("cmasks", [128, 4, 4])
            colsel = SA("colsels", [4, 4, 4], BF16)
            mnew = SA("mnews", [128, 8], BF16)
            s_hT = SA("s_hT", [128, NCH, 4], BF16)
            s_gfin = SA("s_gfin", [4, D])
            hc = SA("hcs", [2, 4, 128], BF16)
            candbs = SA("candbss", [2, 256])
            forceds = SA("forcedss", [2, 256])
            riota = SA("riotas", [128, 128])
            jcol = SA("jcols", [128, 1])
            ptT = SA("ptTs", [128, 4], I32)
            ptTf = SA("ptTf", [128, 8])
            ptrep = SA("ptreps", [128, 4, 128], I32)
            ptrf = SA("ptrf", [128, 128])
            idxCf = SA("idxCf", [128, 128])
            idxC = SA("idxC", [128, 4, 128], I32)
            idxS = SA("idxS", [128, 4, 128], I32)
            s_gq = SA("s_gq", [4, 256])
            s_gk = SA("s_gk", [4, 256])
            s_gv = SA("s_gv", [4, 512], BF16)
            s_gsg = SA("s_gsg", [4, 512])
            s_nkv = SA("s_nkv", [4, 768])
            s_gt = SA("s_gt", [4, 24])
            s_sp = SA("s_sp", [4, 256])
            s_A = SA("s_A", [4, 256])
            s_AT = SA("s_AT", [128, 2, 4])
            s_gqT = SA("s_gqT", [128, 2, 4])
            s_qd = SA("s_qd", [128, 2, 4], BF16)
            s_qm = SA("s_qm", [128, 4, 2, 4], BF16)
            s_gaT = SA("s_gaT", [16, 4], BF16)
            s_nqT = SA("s_nqT", [128, 4, 4], BF16)
            S0f = SB("S0f", [128, 4, 2, 128])
            S0b = SB("S0b", [128, 4, 2, 128], BF16)
            s_km = SA("s_km", [4, 256], BF16)
            s_o = SA("s_o", [4, 512])
            s_tmp = SA("s_tmp", [4, 512])
            Gb = [SA("Gb%d" % i, [128, 256], BF16) for i in range(4)]
            XTb = [SA("XTb%d" % i, [128, 256], BF16) for i in range(3)]
            hidKs = SA("hidKs", [128, 128, 2], BF16)
            hidVs = SA("hidVs", [128, 128, 2], BF16)
            KcTs = SA("KcTs", [128, 256], BF16)
            Vcs = SA("Vcs", [128, 2, 128], BF16)
            eTs = SA("eTs", [128, 2, 8])
            rbs = SA("rbs", [128, 8])
            pTbs = SA("pTbs", [128, 2, 8], BF16)
            impTs = SA("impTs", [128, 2, 2])
            crow = SA("crow", [2, 256])
            crow2 = SA("crow2", [2, 256])
            m8s = SA("m8s", [2, 16])
            selbb = SA("selbb", [2, 256], BF16)
            mcol4 = SA("mcol4", [128, 2, 128, 4], BF16)
            Vaug = [SA("Vaug%d" % i, [128, 2, 65], BF16) for i in range(10)]
            ebs = [SA("ebs%d" % i, [128, 32], BF16) for i in range(4)]
            s_nkvb = SA("s_nkvb", [4, 768], BF16)
            Gnew = SA("Gnew", [128, 256], BF16)
            ounb = SA("ounb", [4, 3, 2, 65])
            Rexp = SB("Rexp", [4, 3, 2, 256], BF16)
            Rtm = SB("Rtm", [4, 3, 512])
            cst = SA("cst", [4, 2, 256])
            cwj = SA("cwj", [4, 4, 256])
            gpre = SA("gpre", [4, 256])
            accs = SA("accs", [4, 256])
            hs = SA("hs", [4, DFF], BF16)
            SB = lambda name, shape, dt=F32: P.sba("KS", name, shape, dt)
            C0f = [SB("C0f%d" % i, [128, 4, 257]) for i in range(2)]
            C0b = [SB("C0b%d" % i, [128, 4, 257], BF16) for i in range(2)]
            s_mq = SA("s_mq", [4, 512])
            s_mk = SA("s_mk", [4, 512])
            s_mv = SA("s_mv", [4, 4, 257], BF16)
            s_og = SB("s_og", [4, D])
            s_if = SA("s_if", [4, 8])
            s_mqT = SA("s_mqT", [128, 4, 4], BF16)
            s_mqm = SA("s_mqm", [128, 4, 4, 4], BF16)
            s_sc = SA("s_sc", [4, 64])
            s_m0 = SA("s_m0", [4, 4])
            s_dg = SA("s_dg", [4, 4, 8])
            s_bc = SA("s_bc", [128, 16])
            s_kmb = SA("s_kmb", [4, 512], BF16)
            s_kw = SA("s_kw", [4, 512], BF16)
            s_num = SB("s_num", [4, 4, 257])

        def s_rmsnorm(gain):
            c = newcol()
            P.act(junk[0:4, :], x[0:4, 0, :], AF.Square, accum=c[0:4, 0:1])
            P.ts(c[0:4, 1:2], c[0:4, 0:1], 1.0 / D, EPS, ALU.mult, ALU.add)
            P.rsqrt(c[0:4, 2:3], c[0:4, 1:2])
            xs_ = xn[0]
            P.ts(xs_[0:4, :], x[0:4, 0, :], c[0:4, 2:3], None, ALU.mult)
            pt = P.ppb()
            for k in range(8):
                P.tr(pt[:, k * 128:k * 128 + 4], xs_[0:4, k * 128:(k + 1) * 128], ident[0:4, 0:4])
            for k in range(8):
                P.act(xnT[:, k, 0:4], pt[:, k * 128:k * 128 + 4], AF.Copy, scale=gain[:, k:k + 1])

        def s_fm(w, c0, n, out_fn):
            ps = P.pp()
            for k in range(8):
                P.mm(ps[0:n, 0:4], V(w.ap[:, k, c0:c0 + n], w.bufs), xnT[:, k, 0:4], start=(k == 0), stop=(k == 7))
            out_fn(ps)

        def s_tm(w, c0, n, out_fn):
            ps = P.pp()
            for k in range(8):
                P.mm(ps[0:4, 0:n], xnT[:, k, 0:4], V(w.ap[:, k, c0:c0 + n], w.bufs), start=(k == 0), stop=(k == 7))
            out_fn(ps)

        def s_out_proj(wd):
            pt = P.ppb()
            for k in range(8):
                P.tr(pt[:, k * 128:k * 128 + 4], mix[0:4, 0, k * 128:(k + 1) * 128], ident[0:4, 0:4])
            for k in range(8):
                P.evac(mixT[:, k, 0:4], pt[:, k * 128:k * 128 + 4])
            for half in range(2):
                w = P.wload(wsrc(wd, half * 512, 512), [128, 8, 512])
                ps = P.pp()
                for k in range(8):
                    P.mm(ps[0:4, :], mixT[:, k, 0:4], V(w.ap[:, k, :], w.bufs), start=(k == 0), stop=(k == 7))
                P.tt(x[0:4, 0, half * 512:(half + 1) * 512], x[0:4, 0, half * 512:(half + 1) * 512], ps[0:4, :], ALU.add)

        def s_ffn(l):
            s_rmsnorm(g_ffn[l])
            for j in range(11):
                w = P.wload(wsrc(wU[l], j * 512, 512), [128, 8, 512])
                js = slice(j * 256, (j + 1) * 256)
                P.dma(cst[:], V(scv_d.ap[l, :, :, js], scv_d.bufs))
                P.dma(cwj[:], V(cwrep_d[l].ap[:, :, js], cwrep_d[l].bufs))
                ps = P.pp()
                for k in range(8):
                    P.mm(ps[0:4, :], xnT[:, k, 0:4], V(w.ap[:, k, :], w.bufs), start=(k == 0), stop=(k == 7))
                P.copy(gpre[:], ps[0:4, 0:256])
                P.dma(V(cvs_out.ap[l, :, 0, js], cvs_out.bufs), cst[:, 1, :])
                P.dma(V(cvs_out.ap[l, :, 1, js], cvs_out.bufs), gpre[:])
                P.tt(accs[:], gpre[:], cwj[:, 2, :], ALU.mult)
                P.tt(accs[:], accs[:], cwj[:, 3, :], ALU.add)
                P.tt(cst[:, 0, :], cst[:, 0, :], cwj[:, 0, :], ALU.mult)
                P.tt(accs[:], accs[:], cst[:, 0, :], ALU.add)
                P.tt(cst[:, 1, :], cst[:, 1, :], cwj[:, 1, :], ALU.mult)
                P.tt(accs[:], accs[:], cst[:, 1, :], ALU.add)
                P.act(accs[:], accs[:], AF.Gelu_apprx_tanh)
                P.tt(hs[:, js], accs[:], ps[0:4, 256:512], ALU.mult)
            pt = P.ppb()
            for c in range(NCH):
                P.tr(pt[:, c * 4:(c + 1) * 4], hs[0:4, c * 128:(c + 1) * 128], ident[0:4, 0:4])
            P.evac(s_hT[:, :, :], V(pt.t[:, 0:NCH * 4].rearrange("p (c t) -> p c t", c=NCH), [pt.buf]))
            for q in range(8):
                w = P.wload(V(wD[l].ap.rearrange("(c p) n -> p c n", p=128)[:, :, q * 128:(q + 1) * 128], wD[l].bufs),
                            [128, NCH, 128])
                ps = P.pp()
                for c in range(NCH):
                    P.mm(ps[0:4, 0:128], s_hT[:, c, :], V(w.ap[:, c, :], w.bufs), start=(c == 0), stop=(c == NCH - 1))
                P.tt(x[0:4, 0, q * 128:(q + 1) * 128], x[0:4, 0, q * 128:(q + 1) * 128], ps[0:4, 0:128], ALU.add)

        def s_gather(pool_d, idx, b, r):
            self.grr = getattr(self, "grr", 0) + 1
            g = Gb[self.grr % 4]
            o_, i_, ix = g.t[:], pool_d.ap, idx.t[:, b, r:r + 1]
            self.R.op("pool", lambda e: e.indirect_dma_start(out=o_, out_offset=None, in_=i_,
                                                             in_offset=bass.IndirectOffsetOnAxis(ap=ix, axis=0)),
                      reads=[idx.buf] + pool_d.bufs, writes=[g.buf], dma=True)
            return g

        def s_even():
            s_rmsnorm(g_mix[0])
            w = P.wload(wsrc(wE, 0, 512), [128, 8, 512])
            for c in range(2):
                s_fm(w, c * 128, 128, lambda ps, c=c: P.copy(s_gqT[:, c, :], ps[:, 0:4]))
            s_tm(w, 0, 512, lambda ps: (P.copy(s_gq[:], ps[0:4, 0:256]), P.act(s_gk[:], ps[0:4, 256:512], AF.Copy)))
            w = P.wload(wsrc(wE, 512, 512), [128, 8, 512])
            for g in range(4):
                s_fm(w, g * 128, 128, lambda ps, g=g: P.evac(s_nqT[:, g, :], ps[:, 0:4], scale=0.125))
            w = P.wload(wsrc(wE, 1536, 296), [128, 8, 296])
            s_fm(w, 280, 16, lambda ps: P.evac(s_gaT[:, :], ps[0:16, 0:4]))
            def fg_(ps):
                P.tt(s_gt[:], ps[0:4, 256:280], gateb[0:4, :], ALU.add)
                P.act(s_gt[:], s_gt[:], AF.Exp, scale=-1.0)
                P.ts(s_gt[:], s_gt[:], 1.0, None, ALU.add)
                P.recip(s_gt[:], s_gt[:])
            s_tm(w, 0, 280, fg_)
            w = P.wload(wsrc(wE, 1832, 512), [128, 8, 512])
            s_tm(w, 0, 512, lambda ps: P.evac(s_gv[:], ps[0:4, :]))
            w = P.wload(wsrc(wE, 2344, 512), [128, 8, 512])
            def fs_(ps):
                P.act(s_gsg[:], ps[0:4, :], AF.Silu)
                P.tt(s_gsg[:], s_gsg[:], gnorm[0:4, :], ALU.mult)
            s_tm(w, 0, 512, fs_)
            w = P.wload(wsrc(wE, 2856, 512), [128, 8, 512])
            s_tm(w, 0, 512, lambda ps: P.copy(s_nkv[:, 0:512], ps[0:4, :]))
            w = P.wload(wsrc(wE, 3368, 256), [128, 8, 256])
            s_tm(w, 0, 256, lambda ps: P.copy(s_nkv[:, 512:768], ps[0:4, 0:256]))
            P.dma(kvcs_out, s_nkv[:, 0:256])
            P.dma(kvss_out, s_nkv[:, 256:512])
            for b in range(4):
                P.dma(V(kvws_out.ap[b, 0:511, :], kvws_out.bufs), V(swin_d.ap[b, 1:512, :], swin_d.bufs))
                P.dma(V(kvws_out.ap[b, 511:512, :], kvws_out.bufs), s_nkv[b:b + 1, 512:768])
            if "s_gla" not in self.skip:
                s_gla()
            if "s_nsa" not in self.skip:
                s_nsa()
            s_out_proj(wEo)

        def s_gla():
            ps = P.pp()
            P.mm(ps[0:4, 0:256], s_gaT[0:16, 0:4], wa2[:], True, True)
            P.tt(s_sp[:], ps[0:4, 0:256], ba[0:4, :], ALU.add)
            P.act(s_sp[:], s_sp[:], AF.Exp, scale=-1.0)
            P.act(s_sp[:], s_sp[:], AF.Ln, bias=1.0)
            P.act(s_A[:], s_sp[:], AF.Exp, scale=-1.0 / 16.0)
            pst = P.pp()
            for p in range(2):
                P.tr(pst[:, p * 4:(p + 1) * 4], s_A[0:4, p * 128:(p + 1) * 128], identf[0:4, 0:4])
            P.copy(s_AT[:], V(pst.t[:, 0:8].rearrange("p (c t) -> p c t", c=2), [pst.buf]))
            P.stt(s_qd[:], s_gqT[:], 0.125, s_AT[:], ALU.mult, ALU.mult)
            for b in range(4):
                for p in range(2):
                    P.tt(s_qm[:, b, p, :], s_qd[:, p, :], cmask[:, b, :], ALU.mult)
                    P.dma(S0f[:, b, p, :], V(sgla_d.ap[b, 2 * p:2 * p + 2].rearrange("h d e -> (h d) e"), sgla_d.bufs))
            P.copy(S0b[:], S0f[:], eng="pool")
            psoh = [P.pa(), P.pa()]
            for h in range(4):
                p, off = h // 2, (h % 2) * 64
                for b in range(4):
                    P.mm(psoh[h % 2][0:4, p * 128:(p + 1) * 128], s_qm[off:off + 64, b, p, :], S0b[off:off + 64, b, p, :],
                         start=(b == 0), stop=(b == 3))
            P.tt(s_tmp[:, 0:256], s_gq[:], s_gk[:], ALU.mult)
            c = newcol()
            tv = s_tmp.t[:, 0:256].rearrange("p (h d) -> p h d", h=4)
            co = c.t[0:4, 0:4]
            P.R.op("dve", lambda e: e.reduce_sum(co, tv, AX.X), reads=[s_tmp.buf], writes=[c.buf])
            P.ts(c[0:4, 0:4], c[0:4, 0:4], 0.125, None, ALU.mult)
            for h in range(4):
                P.stt(s_o[:, h * 128:(h + 1) * 128], s_gv[:, h * 128:(h + 1) * 128], c[0:4, h:h + 1],
                      psoh[h % 2][0:4, (h // 2) * 128:(h // 2 + 1) * 128], ALU.mult, ALU.add)
            c2 = newcol()
            for h in range(4):
                P.act(s_tmp[:, h * 128:(h + 1) * 128], s_o[:, h * 128:(h + 1) * 128], AF.Square, accum=c2[0:4, h:h + 1])
            P.ts(c2[0:4, 4:8], c2[0:4, 0:4], 1.0 / 128, EPS, ALU.mult, ALU.add)
            P.rsqrt(c2[0:4, 4:8], c2[0:4, 4:8])
            for h in range(4):
                P.stt(mix[0:4, 0, h * 128:(h + 1) * 128], s_o[:, h * 128:(h + 1) * 128], c2[0:4, 4 + h:5 + h],
                      s_gsg[:, h * 128:(h + 1) * 128], ALU.mult, ALU.mult)
            for b in range(4):
                P.ts(s_km[:], s_gk[:], eye4[:, b:b + 1], None, ALU.mult)
                for p in range(2):
                    pss = P.pp()
                    P.mm(pss[:, 0:256], s_km[0:4, p * 128:(p + 1) * 128], s_gv[0:4, p * 256:(p + 1) * 256], True, True)
                    for hh in range(2):
                        r0 = hh * 64
                        P.stt(S0f[r0:r0 + 64, b, p, :], S0f[r0:r0 + 64, b, p, :], s_AT[r0:r0 + 64, p, b:b + 1],
                              pss[r0:r0 + 64, hh * 128:(hh + 1) * 128], ALU.mult, ALU.add)
                    P.dma(V(glas_out.ap[b, 2 * p:2 * p + 2].rearrange("h d e -> (h d) e"), glas_out.bufs), S0f[:, b, p, :])

        def s_nsa():
            P.copy(ptTf[:, 0:4], ptT[:])
            P.ts(ptTf[:, 4:8], ptTf[:, 0:4], 128.0, None, ALU.mult)
            P.copy(ptrf[:], V(ptrep.t[:, 0, :], [ptrep.buf]))
            for b in range(4):
                P.ts(idxCf[:], riota[:], ptTf[:, 4 + b:5 + b], None, ALU.add)
                P.copy(idxC[:, b, :], idxCf[:])
                P.copy(ptrf[:], V(ptrep.t[:, b, :], [ptrep.buf]))
                P.ts(idxCf[:], ptrf[:], 128.0, jcol[:, 0:1], ALU.mult, ALU.add)
                P.copy(idxS[:, b, :], idxCf[:])
            P.memset(Rtm[:], 0.0)
            P.copy(s_nkvb[:], s_nkv[:])
            for va in Vaug:
                P.memset(va[:], 1.0, eng="pool")
            for b in range(4):
                psC = P.pa()
                first = True
                for hf in range(2):
                    wk = P.wload(V(w1d_d.ap[0, hf].rearrange("p (a b) -> p a b", a=32), w1d_d.bufs), [128, 32, 128])
                    wv = P.wload(V(w1d_d.ap[1, hf].rearrange("p (a b) -> p a b", a=32), w1d_d.bufs), [128, 32, 128])
                    for s_ in range(32):
                        ss = hf * 32 + s_
                        for nl in range(2):
                            g = s_gather(cpool_d, idxC, b, nl * 64 + ss)
                            pt = P.ppb()
                            P.tr(pt[:, 0:128], g[:, 0:128], ident[:])
                            P.tr(pt[:, 128:256], g[:, 128:256], ident[:])
                            self.xrr = getattr(self, "xrr", 0) + 1
                            xt = XTb[self.xrr % 3]
                            P.evac(xt[:], pt[:, 0:256])
                            last = (ss == 63)
                            P.mm(psC[:, nl * 128:(nl + 1) * 128], V(wk.ap[:, s_, :], wk.bufs), xt[:, 0:128], first, last)
                            first = False
                            P.mm(psC[:, (2 + nl) * 128:(3 + nl) * 128], V(wv.ap[:, s_, :], wv.bufs), xt[:, 128:256], False, last)
                for nl in range(2):
                    P.act(V(hidKs.t[:, :, nl], [hidKs.buf]), psC[:, nl * 128:(nl + 1) * 128], AF.Gelu_apprx_tanh, bias=hb[:, 0:1])
                    P.act(V(hidVs.t[:, :, nl], [hidVs.buf]), psC[:, (2 + nl) * 128:(3 + nl) * 128], AF.Gelu_apprx_tanh, bias=hb[:, 1:2])
                hk2 = V(hidKs.t[:, :, :].rearrange("p a b -> p (a b)"), [hidKs.buf])
                hv2 = hidVs.t[:, :, :].rearrange("p a b -> p (a b)")
                psk = P.pp()
                P.mm(psk[:, 0:256], w2s[:, 0:128], hk2, True, True)
                P.evac(KcTs[:], psk[:, 0:256])
                for t_ in range(2):
                    psv = P.pp()
                    P.mm(psv[:, 0:128], V(hv2[:, t_ * 128:(t_ + 1) * 128], [hidVs.buf]), w2s[:, 128:256], True, True)
                    P.evac(Vcs[:, t_, :], psv[:, 0:128])
                for k in range(2):
                    pse = P.pp()
                    for t_ in range(2):
                        P.mm(pse[:, t_ * 4:(t_ + 1) * 4], KcTs[k * 64:(k + 1) * 64, t_ * 128:(t_ + 1) * 128],
                             V(s_nqT.t[k * 64:(k + 1) * 64, :, b], [s_nqT.buf]), True, True)
                    P.act(eTs[:, :, k * 4:(k + 1) * 4], V(pse.t[:, 0:8].rearrange("p (t c) -> p t c", t=2), [pse.buf]), AF.Exp)
                ps2 = P.pp()
                for t_ in range(2):
                    P.mm(ps2[:, 0:8], onesf[:], eTs[:, t_, :], t_ == 0, t_ == 1)
                P.ts(rbs[:], ps2[:, 0:8], 1e-30, None, ALU.max)
                P.recip(rbs[:], rbs[:])
                for t_ in range(2):
                    P.tt(eTs[:, t_, :], eTs[:, t_, :], rbs[:], ALU.mult)
                P.copy(pTbs[:], eTs[:], eng="pool")
                iv = eTs.t[:, :, :].rearrange("p t (k g) -> p (t k) g", k=2)
                io = impTs.t[:, :, :].rearrange("p t k -> p (t k)")
                P.R.op("dve", lambda e, io=io, iv=iv: e.reduce_sum(io, iv, AX.X), reads=[eTs.buf], writes=[impTs.buf])
                pso = P.pp()
                for k in range(2):
                    for t_ in range(2):
                        P.mm(pso[0:4, k * 64:(k + 1) * 64], pTbs[:, t_, k * 4:(k + 1) * 4], Vcs[:, t_, k * 64:(k + 1) * 64],
                             t_ == 0, t_ == 1)
                c = newcol()
                for k in range(2):
                    for g in range(4):
                        P.ts(Rexp[:, 0, k, g * 64:(g + 1) * 64], pso[0:4, k * 64:(k + 1) * 64], eye4[:, g:g + 1], None, ALU.mult)
                psr = P.pp()
                for t_ in range(2):
                    P.tr(psr[0:2, t_ * 128:(t_ + 1) * 128], impTs[:, t_, :], identf[:])
                P.tt(crow[:], psr[0:2, 0:256], candbs[:], ALU.add)
                ca, c2a, m1a, m2a = crow.t[:], crow2.t[:], m8s.t[:, 0:8], m8s.t[:, 8:16]
                P.R.op("dve", lambda e, o=m1a, i=ca: e.max(out=o, in_=i), reads=[crow.buf], writes=[m8s.buf])
                P.R.op("dve", lambda e, o=c2a, r=m1a, i=ca: e.match_replace(out=o, in_to_replace=r, in_values=i, imm_value=-1e30),
                       reads=[crow.buf, m8s.buf], writes=[crow2.buf])
                P.R.op("dve", lambda e, o=m2a, i=c2a: e.max(out=o, in_=i), reads=[crow2.buf], writes=[m8s.buf])
                P.ts(crow2[:], crow[:], m8s[:, 12:13], None, ALU.is_ge)
                P.tt(crow2[:], crow2[:], forceds[:], ALU.max)
                P.ts(crow2[:], crow2[:], BIG, -BIG, ALU.mult, ALU.add)
                P.copy(selbb[:], crow2[:])
                for k in range(2):
                    psm = P.pp()
                    for nl in range(2):
                        P.mm(psm[:, 0:128], hc[:, k * 2 + nl, :], V(selbb.t[:, nl:256:2], [selbb.buf]), nl == 0, nl == 1)
                    for g in range(4):
                        P.evac(V(mcol4.t[:, k, :, g], [mcol4.buf]), psm[:, 0:128])
                for br in (1, 2):
                    npg = 128 if br == 1 else 4
                    po = P.pa()
                    firstv = True
                    GRP = 8
                    ntile = npg + 1
                    for t0_ in range(0, ntile, GRP):
                        tl = list(range(t0_, min(ntile, t0_ + GRP)))
                        psk_ = [P.pp(), P.pp()]
                        tiles = []
                        for ti, pg in enumerate(tl):
                            if pg < npg:
                                if br == 1:
                                    g = s_gather(spool_d, idxS, b, pg)
                                else:
                                    self.grr += 1
                                    g = Gb[self.grr % 4]
                                    P.dma(g[:], V(swin_d.ap[b, pg * 128:(pg + 1) * 128, :], swin_d.bufs), eng="pool")
                            else:
                                g = Gnew
                                P.memset(Gnew[:], 0.0)
                                P.dma(Gnew[0:1, :], s_nkvb[b:b + 1, br * 256:(br + 1) * 256])
                            pt = P.ppb()
                            P.tr(pt[:, 0:128], g[:, 0:128], ident[:])
                            self.xrr += 1
                            xt = XTb[self.xrr % 3]
                            P.evac(xt[:, 0:128], pt[:, 0:128])
                            self.vrr = getattr(self, "vrr", 0) + 1
                            va = Vaug[self.vrr % len(Vaug)]
                            P.copy(va[:, :, 0:64], V(g.t[:, 128:256].rearrange("p (k d) -> p k d", k=2), [g.buf]), eng="pool")
                            tiles.append(va)
                            for k in range(2):
                                cs = slice(ti * 4, ti * 4 + 4)
                                qv = V(s_nqT.t[k * 64:(k + 1) * 64, :, b], [s_nqT.buf])
                                extra = []
                                if br == 1 and pg < npg:
                                    extra.append((ident[:], V(mcol4.t[:, k, pg, :], [mcol4.buf])))
                                if pg == npg:
                                    extra.append((ident[:], mnew[:, 0:4]))
                                if br == 2 and pg == 0:
                                    extra.append((ident[:], mnew[:, 4:8]))
                                P.mm(psk_[k][:, cs], xt[k * 64:(k + 1) * 64, 0:128], qv, True, len(extra) == 0)
                                for ei, (l_, r_) in enumerate(extra):
                                    P.mm(psk_[k][:, cs], l_, r_, False, ei == len(extra) - 1)
                        nt_ = len(tl)
                        ebk = []
                        for k in range(2):
                            self.err = getattr(self, "err", 0) + 1
                            eb = ebs[self.err % len(ebs)]
                            P.act(eb[:, 0:nt_ * 4], psk_[k][:, 0:nt_ * 4], AF.Exp)
                            ebk.append(eb)
                        for ti, pg in enumerate(tl):
                            for k in range(2):
                                cs = slice(ti * 4, ti * 4 + 4)
                                lastv = (pg == npg)
                                P.mm(po[0:4, k * 65:(k + 1) * 65], ebk[k][:, cs], tiles[ti][:, k, :], firstv, lastv)
                                firstv = False
                    P.copy(ounb[:, br, :, :], V(po.t[0:4, 0:130].rearrange("p (k d) -> p k d", k=2), [po.buf]))
                    c = newcol()
                    P.recip(c[0:4, 0:2], V(ounb.t[:, br, :, 64], [ounb.buf]))
                    for k in range(2):
                        for g in range(4):
                            P.ts(Rexp[:, br, k, g * 64:(g + 1) * 64], ounb[:, br, k, 0:64], c[0:4, k:k + 1], eye4[:, g:g + 1],
                                 ALU.mult, ALU.mult)
                for br in range(3):
                    psR = P.pp()
                    for k in range(2):
                        P.mm(psR[0:4, k * 256:(k + 1) * 256], colsel[:, b, :], Rexp[:, br, k, :], True, True)
                    P.tt(Rtm[:, br, :], Rtm[:, br, :], psR[0:4, :], ALU.add)
            for k in range(2):
                for g in range(4):
                    gc = k * 12 + g * 3
                    f0 = (k * 4 + g) * 64
                    P.ts(s_tmp[:, f0:f0 + 64], Rtm[:, 0, f0:f0 + 64], s_gt[:, gc:gc + 1], None, ALU.mult)
                    P.stt(s_tmp[:, f0:f0 + 64], Rtm[:, 1, f0:f0 + 64], s_gt[:, gc + 1:gc + 2], s_tmp[:, f0:f0 + 64], ALU.mult, ALU.add)
                    P.stt(mix[0:4, 0, 512 + f0:512 + f0 + 64], Rtm[:, 2, f0:f0 + 64], s_gt[:, gc + 2:gc + 3], s_tmp[:, f0:f0 + 64],
                          ALU.mult, ALU.add)

        def s_odd():
            s_rmsnorm(g_mix[1])
            w = P.wload(wsrc(wO, 0, 512), [128, 8, 512])
            s_tm(w, 0, 512, lambda ps: P.copy(s_mq[:], ps[0:4, :]))
            for h in range(4):
                s_fm(w, h * 128, 128, lambda ps, h=h: P.evac(s_mqT[:, h, :], ps[:, 0:4]))
            w = P.wload(wsrc(wO, 512, 512), [128, 8, 512])
            s_tm(w, 0, 512, lambda ps: P.act(s_mk[:], ps[0:4, :], AF.Copy, scale=128 ** -0.5))
            for hf in range(2):
                w = P.wload(wsrc(wO, 1536 + hf * 512, 512), [128, 8, 512])
                s_tm(w, 0, 512, lambda ps, hf=hf: P.evac(s_mv[:, 2 * hf:2 * hf + 2, 0:256],
                                                         V(ps.t[0:4, :].rearrange("p (h e) -> p h e", h=2), [ps.buf])))
            P.memset(V(s_mv.t[:, :, 256:257], [s_mv.buf]), 1.0)
            for hf in range(2):
                w = P.wload(wsrc(wO, 2560 + hf * 512, 512), [128, 8, 512])
                def f(ps, hf=hf):
                    dst = s_og[:, hf * 512:(hf + 1) * 512]
                    P.act(dst, ps[0:4, :], AF.Sigmoid)
                    P.tt(dst, dst, mnorm[0:4, hf * 512:(hf + 1) * 512], ALU.mult)
                s_tm(w, 0, 512, f)
            w = P.wload(wsrc(wO, 3584, 8), [128, 8, 8])
            s_tm(w, 0, 8, lambda ps: P.copy(s_if[:], ps[0:4, 0:8]))
            P.dma(s_m0[:], smm_d)
            c_ = s_sc
            P.tt(c_[:, 0:4], s_if[:, 0:4], bif[0:4, 0:4], ALU.add)
            P.act(c_[:, 4:8], s_if[:, 4:8], AF.Exp, scale=-1.0)
            P.act(c_[:, 36:40], nbf[0:4, :], AF.Exp)
            P.tt(c_[:, 4:8], c_[:, 4:8], c_[:, 36:40], ALU.mult)
            P.act(c_[:, 4:8], c_[:, 4:8], AF.Ln, bias=1.0)
            P.tt(c_[:, 8:12], s_m0[:], c_[:, 4:8], ALU.subtract)
            P.tt(c_[:, 12:16], c_[:, 8:12], c_[:, 0:4], ALU.max)
            P.tt(c_[:, 36:40], c_[:, 0:4], c_[:, 12:16], ALU.subtract)
            P.act(c_[:, 16:20], c_[:, 36:40], AF.Exp)
            P.tt(c_[:, 36:40], c_[:, 8:12], c_[:, 12:16], ALU.subtract)
            P.act(c_[:, 20:24], c_[:, 36:40], AF.Exp)
            P.act(c_[:, 24:28], c_[:, 12:16], AF.Exp, scale=-1.0)
            P.dma(mlms_out, c_[:, 12:16])
            P.tt(s_tmp[:], s_mq[:], s_mk[:], ALU.mult)
            tv = s_tmp.t[:, :].rearrange("p (h d) -> p h d", h=4)
            co = c_.t[:, 28:32]
            P.R.op("dve", lambda e: e.reduce_sum(co, tv, AX.X), reads=[s_tmp.buf], writes=[c_.buf])
            P.tt(c_[:, 32:36], c_[:, 28:32], c_[:, 16:20], ALU.mult)
            P.memset(s_num[:], 0.0)
            for h in range(4):
                P.ts(s_kw[:, h * 128:(h + 1) * 128], s_mk[:, h * 128:(h + 1) * 128], c_[:, 16 + h:17 + h], None, ALU.mult)
            for b in range(4):
                P.ts(s_dg[:, b, 0:4], c_[:, 20:24], eye4[:, b:b + 1], None, ALU.mult)
            psb = P.pp()
            P.mm(psb[:, 0:16], onesf[0:4, :], V(s_dg.t[:, :, 0:4], [s_dg.buf]), True, True)
            P.copy(s_bc[:], psb[:, 0:16])
            for b in range(4):
                cf, cb = C0f[b % 2], C0b[b % 2]
                for h in range(4):
                    P.dma(cf[:, h, 0:256], V(smc_d.ap[b, h], smc_d.bufs))
                    P.dma(cf[:, h, 256:257], V(smn_d.ap[b, h, :].rearrange("(d o) -> d o", o=1), smn_d.bufs), nc_ok=True)
                P.copy(cb[:], cf[:], eng="pool")
                P.ts(s_kmb[:], s_kw[:], eye4[:, b:b + 1], None, ALU.mult)
                for h in range(4):
                    P.tt(s_mqm[:, b, h, :], s_mqT[:, h, :], cmask[:, b, :], ALU.mult)
                    pn = P.pp()
                    P.mm(pn[0:4, 0:257], s_mqm[:, b, h, :], cb[:, h, :], True, True)
                    P.tt(s_num[:, h, :], s_num[:, h, :], pn[0:4, 0:257], ALU.add)
                    pc = P.pp()
                    P.mm(pc[:, 0:257], s_kmb[0:4, h * 128:(h + 1) * 128], s_mv[0:4, h, :], True, True)
                    P.stt(cf[:, h, :], cf[:, h, :], s_bc[:, b * 4 + h:b * 4 + h + 1], pc[:, 0:257], ALU.mult, ALU.add)
                    P.dma(V(mlcs_out.ap[b, h], mlcs_out.bufs), cf[:, h, 0:256])
                    P.dma(V(mlns_out.ap[b, h, :].rearrange("(d o) -> d o", o=1), mlns_out.bufs), cf[:, h, 256:257], nc_ok=True)
            for h in range(4):
                nu = V(s_num.t[:, h, :], [s_num.buf])
                P.ts(nu, nu, c_[:, 20 + h:21 + h], None, ALU.mult)
                P.stt(nu, s_mv[:, h, :], c_[:, 32 + h:33 + h], nu, ALU.mult, ALU.add)
                cc = newcol()
                P.ts(cc[0:4, 3:4], s_num[:, h, 256:257], -1.0, None, ALU.mult)
                P.tt(cc[0:4, 0:1], s_num[:, h, 256:257], cc[0:4, 3:4], ALU.max)
                P.tt(cc[0:4, 0:1], cc[0:4, 0:1], c_[:, 24 + h:25 + h], ALU.max)
                P.recip(cc[0:4, 0:1], cc[0:4, 0:1])
                P.ts(s_o[:, 0:256], s_num[:, h, 0:256], cc[0:4, 0:1], None, ALU.mult)
                P.act(s_tmp[:, 0:256], s_o[:, 0:256], AF.Square, accum=cc[0:4, 1:2])
                P.ts(cc[0:4, 2:3], cc[0:4, 1:2], 1.0 / 256, EPS, ALU.mult, ALU.add)
                P.rsqrt(cc[0:4, 2:3], cc[0:4, 2:3])
                P.stt(mix[0:4, 0, h * 256:(h + 1) * 256], s_o[:, 0:256], cc[0:4, 2:3], s_og[:, h * 256:(h + 1) * 256], ALU.mult, ALU.mult)
            s_out_proj(wOo)

        def sample_pass():
            P.phase("S")
            P.phase("KS")
            P.dma(eye4[:], eye4_d)
            P.dma(cmask[:], V(cmask_d.ap.rearrange("p (a b) -> p a b", a=4), cmask_d.bufs))
            P.dma(colsel[:], V(colsel_d.ap.rearrange("p (a b) -> p a b", a=4), colsel_d.bufs))
            P.dma(mnew[:], mnew_d)
            P.dma(hc[:], V(hc_d.ap.rearrange("p (a b) -> p a b", a=4), hc_d.bufs))
            P.dma(candbs[:], candbs_d)
            P.dma(forceds[:], forceds_d)
            P.dma(riota[:], riota_d)
            P.dma(jcol[:], jcol_d)
            P.dma(ptT[:], ptT_d)
            P.dma(ptrep[:], ptrep_d)
            P.dma(s_gfin[:], V(gfin.ap[0:4, :], gfin.bufs))
            P.dma(x[0:4, 0, :], xs_d)
            s_even()
            if "s_ffn" not in self.skip:
                s_ffn(0)
            if "s_odd" not in self.skip:
                s_odd()
            if "s_ffn" not in self.skip:
                s_ffn(1)
            c = newcol()
            P.act(junk[0:4, :], x[0:4, 0, :], AF.Square, accum=c[0:4, 0:1])
            P.ts(c[0:4, 1:2], c[0:4, 0:1], 1.0 / D, EPS, ALU.mult, ALU.add)
            P.rsqrt(c[0:4, 2:3], c[0:4, 1:2])
            P.stt(s_og[:], x[0:4, 0, :], c[0:4, 2:3], s_gfin[:], ALU.mult, ALU.mult)
            P.dma(ys_out, s_og[:])

        for bt in range(NB):
            for s in range(4):
                t0 = bt * 512 + s * 128
                P.dma(x[:, s, :], V(xp.ap[t0:t0 + 128, :], xp.bufs))
            even_layer(bt)
            ffn(0, bt == NB - 1)
            if "mlstm" in self.parts:
                odd_layer(bt)
                if self.debug and bt == NB - 1:
                    pass
            ffn(1, bt == NB - 1)
            P.dma(g_fin[:], gfin)
            for s in range(4):
                t0 = bt * 512 + s * 128
                c = newcol()
                P.act(junk[:], x[:, s, :], AF.Square, accum=c[:, 0:1])
                P.ts(c[:, 1:2], c[:, 0:1], 1.0 / D, EPS, ALU.mult, ALU.add)
                P.rsqrt(c[:, 2:3], c[:, 1:2])
                o_ = acc[s % 2]
                for hf in range(2):
                    P.stt(o_[:], x[:, s, hf * 512:(hf + 1) * 512], c[:, 2:3], g_fin[:, hf * 512:(hf + 1) * 512], ALU.mult, ALU.mult)
                    P.dma(V(y_out.ap[t0:t0 + 128, hf * 512:(hf + 1) * 512], y_out.bufs), o_[:])
        if "gla" in self.parts:
            for p in range(2):
                P.dma(V(gla_out.ap[p * 128:(p + 1) * 128, :], gla_out.bufs), S32[p][:])
        if "mlstm" in self.parts:
            for h in range(4):
                P.dma(V(mc_out.ap[h], mc_out.bufs), C32[:, h, 0:256])
            P.dma(mn_out, V(C32.t[:, :, 256], [C32.buf]), nc_ok=True)
            P.dma(mm_out, m_bc[0:1, :])
        if self.samp:
            sample_pass()
        with self.st:
            self.R.emit()
        return self.nc


def _even_cols():
    gq, gk, gv, gg, ga = 0, 256, 512, 1024, 1536
    nq, nkv, ng = 1552, 2064, 2832
    cols = []
    cols += list(range(gq, gq + 256)) + list(range(gk, gk + 256))
    for g in range(4):
        for k in range(2):
            h = k * 4 + g
            cols += list(range(nq + h * 64, nq + h * 64 + 64))
    for br in range(3):
        cols += list(range(nkv + br * 256, nkv + br * 256 + 128))
    cols += list(range(nkv + 128, nkv + 256))
    cols += list(range(gk, gk + 256)) + list(range(ng, ng + 24)) + list(range(ga, ga + 16))
    cols += list(range(gv, gv + 512))
    cols += list(range(gg, gg + 512))
    cols += list(range(nkv, nkv + 512))
    cols += list(range(nkv + 512, nkv + 768))
    return np.array(cols)


def nsa_consts(T):
    import ml_dtypes
    bf = ml_dtypes.bfloat16
    NT = T // 128
    n = np.arange(128)
    tl = np.arange(128)
    cmpb = np.zeros((NT, 128, 128), np.float32)
    candb = np.zeros((NT, 128, 128), np.float32)
    forced = np.zeros((NT, 128, 128), np.float32)
    allowed = np.zeros((NT, 128, 128), np.float32)
    for qi in range(NT):
        t = 128 * qi + tl
        cur = t // 64
        cmpb[qi] = np.where(64 * n[:, None] + 63 <= t[None, :], 0.0, -BIG)
        candb[qi] = np.where((n[None, :] >= 1) & (n[None, :] <= cur[:, None] - 2), 0.0, -BIG)
        forced[qi] = ((n[None, :] == 0) | (n[None, :] == cur[:, None]) | (n[None, :] == cur[:, None] - 1)).astype(np.float32)
        allowed[qi] = (n[None, :] <= cur[:, None]).astype(np.float32)
    eall = np.zeros((128, NT, 128), np.float32)
    for kt in range(NT):
        if 2 * kt < 128:
            eall[2 * kt, kt, 0:64] = 1.0
        if 2 * kt + 1 < 128:
            eall[2 * kt + 1, kt, 64:128] = 1.0
    j = np.arange(128)
    causal = np.where(j[:, None] <= tl[None, :], 0.0, -BIG).astype(np.float32)
    winlow = np.where(j[:, None] > tl[None, :], 0.0, -BIG).astype(np.float32)
    return {
        "identb": np.eye(128, dtype=np.float32).astype(bf),
        "cmpb": cmpb.astype(bf), "eall": eall.astype(bf),
        "causal": np.tile(causal, (1, 4)).astype(bf), "winlow": np.tile(winlow, (1, 4)).astype(bf),
        "candb": candb, "forced": forced, "allowed": allowed,
    }


def _up_cols():
    cols = []
    for j in range(11):
        cols += list(range(256 * j, 256 * j + 256)) + list(range(DFF + 256 * j, DFF + 256 * j + 256))
    return np.array(cols)


def _rep(v, n=128):
    v = np.asarray(v, np.float32).reshape(1, -1)
    return np.ascontiguousarray(np.broadcast_to(v, (n, v.shape[1])))


def prep_shared(inp, T):
    f = lambda a: np.ascontiguousarray(np.asarray(a, np.float32))
    sh = {}
    for l in range(2):
        sh["gmix%d" % l] = f(np.asarray(inp["norm_mix"][l]).reshape(8, 128).T)
        sh["gffn%d" % l] = f(np.asarray(inp["norm_ffn"][l]).reshape(8, 128).T)
        sh["wU%d" % l] = f(np.asarray(inp["ffn_w_up"][l])[:, _up_cols()])
        sh["wD%d" % l] = f(inp["ffn_w_down"][l])
        cwl = np.concatenate([np.asarray(inp["ffn_conv_w"][l]), np.asarray(inp["ffn_conv_b"][l])[None]], 0)
        sh["cw%d" % l] = f(cwl.reshape(4, NCH, 128).transpose(2, 0, 1))
    sh["gfin"] = _rep(inp["norm_final"])
    sh["wE"] = f(np.asarray(inp["even_w_in"][0])[:, _even_cols()])
    sh["wEo"] = f(inp["even_w_out"][0])
    wo = np.asarray(inp["odd_w_in"][0], np.float32)
    ocols = list(range(0, 512)) + list(range(512, 1024)) + list(range(512, 1024)) + list(range(1024, 3072)) + list(range(3072, 3080))
    sh["wO"] = f(wo[:, ocols])
    sh["mnorm"] = _rep(inp["mlstm_norm"][0])
    sh["bif"] = _rep(np.concatenate([np.asarray(inp["mlstm_b_i"][0]), np.asarray(inp["mlstm_b_f"][0])]))
    sh["bicol"] = f(np.asarray(inp["mlstm_b_i"][0]).reshape(4, 1))
    s127 = np.zeros((128, 128), np.float32); s127[127, :] = 1.0
    sh["sel127"] = s127
    ii = np.arange(128)
    sh["trilb"] = np.where(ii[None, :] <= ii[:, None], 0.0, -BIG).astype(np.float32)
    oh = np.zeros((4, 4, 128), np.float32)
    for h in range(4):
        oh[h, h, :] = 1.0
    sh["onehot"] = f(oh.reshape(4, 512))
    sh["wOo"] = f(inp["odd_w_out"][0])
    sh["ident"] = np.eye(128, dtype=np.float32)
    i = np.arange(128)
    sh["mtri"] = (i[:, None] <= i[None, :]).astype(np.float32)
    sh["mstr"] = (i[:, None] > i[None, :]).astype(np.float32)
    sh["wa2"] = f(inp["gla_w_a2"][0])
    sh["ba"] = _rep(inp["gla_b_a"][0])
    sh["gnorm"] = _rep(np.tile(np.asarray(inp["gla_norm"][0]), 4))
    w1 = np.asarray(inp["nsa_cmp_w1"][0], np.float32)
    w1bd = np.zeros((2, 128, 64, 128), np.float32)
    for k in range(2):
        w1bd[:, k * 64:(k + 1) * 64, :, k * 64:(k + 1) * 64] = w1.transpose(0, 2, 1, 3)
    sh["w1d"] = f(w1bd.reshape(2, 128, 2, 32 * 128).transpose(0, 2, 1, 3))
    w2 = np.asarray(inp["nsa_cmp_w2"][0], np.float32)
    w2bd = np.zeros((128, 2, 128), np.float32)
    for k in range(2):
        w2bd[k * 64:(k + 1) * 64, :, k * 64:(k + 1) * 64] = w2.transpose(1, 0, 2)
    sh["w2"] = f(w2bd.reshape(128, 256))
    pe = np.asarray(inp["nsa_cmp_pe"][0], np.float32)
    peT = np.concatenate([pe[0].T, pe[1].T], axis=1)
    sh["peT"] = f(np.concatenate([peT, peT], axis=0))
    sh["gateb"] = _rep(inp["nsa_gate_b"][0])
    sh.update(nsa_consts(T))
    return sh


def sample_consts():
    import ml_dtypes
    bf = ml_dtypes.bfloat16
    c = {}
    c["riota"] = np.ascontiguousarray(np.broadcast_to(np.arange(128, dtype=np.float32)[None, :], (128, 128)))
    c["jcol"] = np.arange(128, dtype=np.float32).reshape(128, 1)
    c["eye4"] = np.eye(4, dtype=np.float32)
    cm = np.zeros((128, 4, 4), np.float32)
    for b in range(4):
        cm[:, b, b] = 1.0
    c["cmask"] = cm.reshape(128, 16)
    hc = np.zeros((2, 2, 2, 128), np.float32)
    for k in range(2):
        hc[k, k, 0, 0:64] = 1.0
        hc[k, k, 1, 64:128] = 1.0
    c["hc"] = hc.reshape(2, 512).astype(bf)
    cs = np.zeros((4, 4, 4), np.float32)
    for b in range(4):
        cs[:, b, b] = 1.0
    c["colsel"] = cs.reshape(4, 16).astype(bf)
    mn = np.zeros((128, 8), np.float32)
    mn[1:, 0:4] = -BIG
    mn[0, 4:8] = -BIG
    c["mnew"] = mn.astype(bf)
    cb = np.full((2, 256), -BIG, np.float32)
    cb[:, 1:255] = 0.0
    c["candbs"] = cb
    fo = np.zeros((2, 256), np.float32)
    fo[:, 0] = 1.0
    fo[:, 255] = 1.0
    c["forceds"] = fo
    return c


def prep_sample(inp, c):
    f = lambda a: np.ascontiguousarray(np.asarray(a, np.float32))
    b0 = 4 * c
    m = {}
    m["xs"] = f(np.asarray(inp["x_sample"])[b0:b0 + 4, 0, :])
    m["sgla"] = f(np.asarray(inp["state_gla"])[0, b0:b0 + 4])
    m["smc"] = f(np.asarray(inp["state_mlstm_c"])[0, b0:b0 + 4])
    m["smn"] = f(np.asarray(inp["state_mlstm_n"])[0, b0:b0 + 4])
    m["smm"] = f(np.asarray(inp["state_mlstm_m"])[0, b0:b0 + 4])
    m["scv"] = f(np.asarray(inp["state_ffn_conv"])[:, b0:b0 + 4])
    m["swin"] = f(np.asarray(inp["cache_win_kv"])[0, b0:b0 + 4].reshape(4, 512, 256))
    pt = np.asarray(inp["page_table"])[b0:b0 + 4].astype(np.int32)
    m["ptT"] = np.ascontiguousarray(pt.T)
    m["ptrep"] = np.ascontiguousarray(np.broadcast_to(pt[None, :, :], (128, 4, 128)))
    return m


def prep_sample_shared(inp):
    f = lambda a: np.ascontiguousarray(np.asarray(a, np.float32))
    sh = {}
    cp = np.asarray(inp["cache_cmp_kv"])
    sp = np.asarray(inp["cache_sel_kv"])
    nphys = cp.shape[1]
    sh["cpool"] = f(cp[0].reshape(nphys * 128, 256))
    sh["spool"] = f(sp[0].reshape(nphys * 128, 256))
    for l in range(2):
        cwl = np.concatenate([np.asarray(inp["ffn_conv_w"][l]), np.asarray(inp["ffn_conv_b"][l])[None]], 0)
        sh["cwrep%d" % l] = f(np.broadcast_to(cwl[None], (4, 4, DFF)))
    sh.update(sample_consts())
    return sh, nphys


_CACHE = {}


def get_prog(T, parts, debug=False, samp=False, nphys=5120):
    key = (T, tuple(parts), debug, samp, nphys)
    if key not in _CACHE:
        p = Prog(T, parts, debug, samp, nphys)
        p.build()
        _CACHE[key] = p
    return _CACHE[key]


def run(inp, T, parts=("gla", "nsa", "mlstm"), n_cores=8, debug=False, samp=False):
    sh = prep_shared(inp, T)
    nphys = 5120
    if samp:
        ssh, nphys = prep_sample_shared(inp)
        sh.update(ssh)
    prog = get_prog(T, parts, debug, samp, nphys)
    xpr = np.asarray(inp["x_prompt"], np.float32)
    in_maps = []
    for c in range(n_cores):
        m = dict(sh)
        m["xp"] = np.ascontiguousarray(xpr[c % xpr.shape[0], :T])
        if samp:
            m.update(prep_sample(inp, c))
        m = {k: v for k, v in m.items() if k in prog.dram}
        in_maps.append(m)
    res = run_bass_kernel_spmd(prog.nc, in_maps, core_ids=list(range(n_cores)))
    return res.results


def kernel(**inputs):
    T = 8192
    r = run(inputs, T, samp=True)
    f = np.float32
    cat = lambda key, cores: np.concatenate([np.asarray(r[c][key], f) for c in cores], 0)
    P4, A8 = range(4), range(8)
    y_p = np.stack([r[c]["y"] for c in P4]).astype(f)
    y_s = cat("ys", A8).reshape(32, 1, 1024)
    kvshape = lambda a, n: a.reshape(1, a.shape[0], n, 2, 2, 64)
    cmp_p = kvshape(np.stack([r[c]["kvc"] for c in P4]).astype(f), T)
    sel_p = kvshape(np.stack([r[c]["kvs"] for c in P4]).astype(f), T)
    win_p = kvshape(np.stack([r[c]["kvw"] for c in P4]).astype(f), 512)
    cmp_s = kvshape(cat("kvcs", A8), 1)
    sel_s = kvshape(cat("kvss", A8), 1)
    win_s = kvshape(cat("kvws", A8), 512)
    gla_p = np.stack([r[c]["glast"].reshape(4, 64, 128) for c in P4]).astype(f)[None]
    gla_s = cat("glas", A8)[None]
    mc_p = np.stack([r[c]["mlc"] for c in P4]).astype(f)[None]
    mc_s = cat("mlcs", A8)[None]
    mn_p = np.stack([r[c]["mln"].T for c in P4]).astype(f)[None]
    mn_s = cat("mlns", A8)[None]
    mm_p = np.stack([r[c]["mlm"][0] for c in P4]).astype(f)[None]
    mm_s = cat("mlms", A8)[None]
    cv_p = np.stack([np.stack([r[c]["cv%d" % l] for c in P4]) for l in range(2)]).astype(f)
    cv_s = np.concatenate([np.asarray(r[c]["cvs"], f) for c in A8], 1)
    return (y_p, y_s, cmp_p, cmp_s, sel_p, sel_s, win_p, win_s, gla_p, gla_s,
            mc_p, mc_s, mn_p, mn_s, mm_p, mm_s, cv_p, cv_s)
```

```python
import contextlib
import numpy as np
import concourse.bass as bass
import concourse.mybir as mybir
from concourse.bass_utils import run_bass_kernel_spmd

F32 = mybir.dt.float32
BF16 = mybir.dt.bfloat16
AF = mybir.ActivationFunctionType
ALU = mybir.AluOpType
AX = mybir.AxisListType

ENGS = ("pe", "act", "dve", "pool", "sp")
D = 1024
DFF = 2816
NCH = 22
EPS = 1e-6
BIG = 30000.0
import os as _os
SAME_ENG_RELAX = int(_os.environ.get("SAME_ENG_RELAX", "0"))


class Buf:
    __slots__ = ("w", "r_eng", "r_dma", "excl")

    def __init__(self):
        self.excl = False
        self.w = None
        self.r_eng = {}
        self.r_dma = []


class Rec:
    def __init__(self, nc):
        self.nc = nc
        self.ops = []
        self.eng_ops = {e: [] for e in ENGS}
        self.n_dma_sems = {"sp": 24, "pool": 16, "act": 4}
        self.dma_rr = {e: 0 for e in ENGS}
        self.dma_last = {}
        self.dma_uses = {}

    def op(self, eng, fn, reads=(), writes=(), dma=False):
        oid = len(self.ops)
        deps = set()
        xr = [b for b in reads if b.excl]
        if xr:
            reads = [b for b in reads if not b.excl]
            writes = list(writes) + xr
        for b in reads:
            if b.w is not None:
                deps.add(b.w)
        for b in writes:
            if b.w is not None:
                deps.add(b.w)
            deps.update(b.r_eng.values())
            deps.update(b.r_dma)
        o = {"id": oid, "eng": eng, "fn": fn, "dma": dma, "needed": False, "pos": len(self.eng_ops[eng])}
        if dma:
            slot = self.dma_rr[eng] % self.n_dma_sems[eng]
            self.dma_rr[eng] += 1
            key = (eng, slot)
            if key in self.dma_last:
                deps.add(self.dma_last[key])
            self.dma_last[key] = oid
            self.dma_uses[key] = self.dma_uses.get(key, 0) + 1
            o["slot"] = key
            o["val"] = 16 * self.dma_uses[key]
            o["needed"] = True
        if eng == "pe" and not dma:
            deps = {d for d in deps if not (self.ops[d]["eng"] == "pe" and not self.ops[d]["dma"])}
        if SAME_ENG_RELAX and not dma:
            pos = len(self.eng_ops[eng])
            deps = {d for d in deps if not (self.ops[d]["eng"] == eng and not self.ops[d]["dma"]
                                            and pos - self.ops[d]["pos"] >= SAME_ENG_RELAX)}
        deps.discard(oid)
        o["deps"] = deps
        for d in deps:
            self.ops[d]["needed"] = True
        for b in reads:
            if dma:
                b.r_dma.append(oid)
            else:
                b.r_eng[eng] = oid
        for b in writes:
            b.w = oid
            b.r_eng = {}
            b.r_dma = []
        self.ops.append(o)
        self.eng_ops[eng].append(o)
        return oid

    def emit(self):
        nc = self.nc
        cnt = {e: 0 for e in ENGS}
        for e in ENGS:
            for o in self.eng_ops[e]:
                if not o["dma"] and o["needed"]:
                    cnt[e] += 1
                    o["val"] = cnt[e]
        with contextlib.ExitStack() as st:
            esem = {e: st.enter_context(nc.semaphore("s_" + e)) for e in ENGS}
            dsem = {}
            for e, n in self.n_dma_sems.items():
                for i in range(n):
                    dsem[(e, i)] = st.enter_context(nc.semaphore("d_%s_%d" % (e, i)))
            block = st.enter_context(nc.Block())
            ops = self.ops

            def semval(o):
                if o["dma"]:
                    return dsem[o["slot"]], o["val"]
                return esem[o["eng"]], o["val"]

            def run(engname, handle):
                seen = {}
                for o in self.eng_ops[engname]:
                    need = {}
                    for d in o["deps"]:
                        s, v = semval(ops[d])
                        k = id(s)
                        if seen.get(k, 0) >= v:
                            continue
                        if k not in need or need[k][1] < v:
                            need[k] = (s, v)
                    for k, (s, v) in need.items():
                        handle.wait_ge(s, v)
                        seen[k] = v
                    ins = o["fn"](handle)
                    if o["dma"]:
                        ins.then_inc(dsem[o["slot"]], 16)
                    elif o["needed"]:
                        ins.then_inc(esem[engname], 1)
                if engname == "sp":
                    for key, n in self.dma_uses.items():
                        handle.wait_ge(dsem[key], 16 * n)
                    for e in ENGS:
                        if cnt[e] > 0:
                            handle.wait_ge(esem[e], cnt[e])

            @block.tensor
            def _(h):
                run("pe", h)

            @block.scalar
            def _(h):
                run("act", h)

            @block.vector
            def _(h):
                run("dve", h)

            @block.gpsimd
            def _(h):
                run("pool", h)

            @block.sync
            def _(h):
                run("sp", h)


class V:
    __slots__ = ("ap", "bufs")

    def __init__(self, ap, bufs):
        self.ap = ap
        self.bufs = bufs


class TB:
    def __init__(self, t):
        self.t = t
        self.buf = Buf()

    def __getitem__(self, idx):
        return V(self.t[idx], [self.buf])

    def v(self, idx, bufs):
        return V(self.t[idx], list(bufs))


def _bufs(*vs):
    out = []
    for v in vs:
        if isinstance(v, V):
            out.extend(v.bufs)
    return out


def _ap(v):
    return v.ap if isinstance(v, V) else v


class Prog:
    def __init__(self, T, parts=("gla", "nsa", "mlstm"), debug=False, samp=False, nphys=5120):
        self.T = T
        self.samp = samp
        self.NPHYS = nphys
        self.parts = parts
        self.debug = debug
        import os
        self.stage = int(os.environ.get("NSA_STAGE", "5"))
        self.skip = os.environ.get("SKIP", "").split(",")
        self.nc = bass.Bass("TRN2", target_bir_lowering=False)
        self.R = Rec(self.nc)
        self.st = contextlib.ExitStack()
        self.dram = {}
        self.rr = 0
        self.wrr = 0
        self.aoff = {}
        self.wcast = []
        self.abufs = {}
        self.cur_phase = {}
        self.ARENA_B = 37 * 1024
        self.ARENA = 70 * 1024

    def din(self, name, shape, dt=F32):
        t = self.nc.dram_tensor(name, list(shape), dt, kind="ExternalInput")
        self.dram[name] = t
        return V(t.ap(), [Buf()])

    def din_w(self, name, shape, rows_per_dma=128):
        src = self.din(name, shape)
        t = self.nc.dram_tensor(name + "_bf", list(shape), BF16, kind="Internal")
        dst = V(t.ap(), [Buf()])
        self.wcast.append((dst, src, shape, rows_per_dma))
        return dst

    def emit_wcasts(self):
        for dst, src, shape, rpd in self.wcast:
            if len(shape) == 2:
                for r0 in range(0, shape[0], rpd):
                    r1 = min(shape[0], r0 + rpd)
                    self.dma(V(dst.ap[r0:r1, :], dst.bufs), V(src.ap[r0:r1, :], src.bufs), eng="pool", max_last=8192)
            else:
                for a in range(shape[0]):
                    for b in range(shape[1]):
                        self.dma(V(dst.ap[a, b], dst.bufs), V(src.ap[a, b], src.bufs), eng="pool", max_last=8192)

    def dout(self, name, shape, dt=F32):
        t = self.nc.dram_tensor(name, list(shape), dt, kind="ExternalOutput")
        self.dram[name] = t
        return V(t.ap(), [Buf()])

    def sb(self, name, shape, dt=F32):
        return TB(self.st.enter_context(self.nc.sbuf_tensor(name, list(shape), dt)))

    def sba(self, phase, name, shape, dt=F32):
        n = 1
        for v in shape[1:]:
            n *= v
        nb = n * (2 if dt == BF16 else 4)
        nb = (nb + 63) // 64 * 64
        off = self.aoff.get(phase, 0)
        self.aoff[phase] = off + nb
        ar = self.arena_b if phase in ("KP", "KS") else self.arena
        cap = self.ARENA_B if phase in ("KP", "KS") else self.ARENA
        assert off + nb <= cap, (phase, name, off + nb)
        ap = ar.t[0:shape[0], off // 4:(off + nb) // 4]
        if dt != F32:
            ap = ap.bitcast(dt)
        ap = ap[:, 0:n]
        if len(shape) == 3:
            ap = ap.rearrange("p (a b) -> p a b", a=shape[1])
        elif len(shape) == 4:
            ap = ap.rearrange("p (a b c) -> p a b c", a=shape[1], b=shape[2])
        tb = TB(ap)
        self.abufs.setdefault(phase, []).append(tb.buf)
        return tb

    def phase(self, new):
        akey = "B" if new in ("KP", "KS") else "A"
        old = self.cur_phase.get(akey)
        if old == new:
            return
        self.cur_phase[akey] = new
        r_eng, r_dma = {}, []
        for b in self.abufs.get(old, []):
            cands = list(b.r_eng.items())
            if b.w is not None:
                o = self.R.ops[b.w]
                if o["dma"]:
                    r_dma.append(b.w)
                else:
                    cands.append((o["eng"], b.w))
            for e, oid in cands:
                if r_eng.get(e, -1) < oid:
                    r_eng[e] = oid
            r_dma.extend(b.r_dma)
        for b in self.abufs.get(new, []):
            b.w = None
            b.r_eng = dict(r_eng)
            b.r_dma = list(r_dma)

    def psum(self, name, shape, dt=F32):
        tb = TB(self.st.enter_context(self.nc.psum_tensor(name, list(shape), dt)))
        tb.buf.excl = True
        return tb

    def dma(self, out, in_, eng="sp", nc_ok=False, max_last=None):
        o, i = _ap(out), _ap(in_)
        if max_last is not None:
            fn = lambda e: e.dma_start(out=o, in_=i, max_dma_last_dim=max_last)
        elif nc_ok:
            fn = lambda e: e.dma_start(out=o, in_=i, allow_slow_non_contiguous=True)
        else:
            fn = lambda e: e.dma_start(out=o, in_=i)
        self.R.op(eng, fn, reads=_bufs(in_), writes=_bufs(out), dma=True)

    def mm(self, out, lhsT, rhs, start=True, stop=True):
        o, l, r = _ap(out), _ap(lhsT), _ap(rhs)
        self.R.op("pe", lambda e: e.matmul(o, l, r, start=start, stop=stop, skip_group_check=True),
                  reads=_bufs(lhsT, rhs), writes=_bufs(out))

    def tr(self, out, in_, ident):
        o, i, d = _ap(out), _ap(in_), _ap(ident)
        self.R.op("pe", lambda e: e.transpose(o, i, d), reads=_bufs(in_, ident), writes=_bufs(out))

    def act(self, out, in_, func=None, bias=0.0, scale=1.0, accum=None):
        func = func or AF.Copy
        o, i, b, s, a = _ap(out), _ap(in_), _ap(bias), _ap(scale), _ap(accum)
        kw = {}
        if a is not None:
            kw["accum_out"] = a
        self.R.op("act", lambda e: e.activation(out=o, in_=i, func=func, bias=b, scale=s, **kw),
                  reads=_bufs(in_, bias, scale), writes=_bufs(out, accum))

    def ts(self, out, in0, s1, s2=None, op0=ALU.mult, op1=None, eng="dve", accum=None):
        o, i, a1, a2, ac = _ap(out), _ap(in0), _ap(s1), _ap(s2), _ap(accum)
        kw = {}
        if op1 is not None:
            kw["op1"] = op1
        if ac is not None:
            kw["accum_out"] = ac
        self.R.op(eng, lambda e: e.tensor_scalar(o, i, a1, a2, op0, **kw),
                  reads=_bufs(in0, s1, s2), writes=_bufs(out, accum))

    def tt(self, out, in0, in1, op, eng="dve"):
        o, a, b = _ap(out), _ap(in0), _ap(in1)
        self.R.op(eng, lambda e: e.tensor_tensor(o, a, b, op), reads=_bufs(in0, in1), writes=_bufs(out))

    def stt(self, out, in0, scalar, in1, op0, op1, eng="dve"):
        o, a, s, b = _ap(out), _ap(in0), _ap(scalar), _ap(in1)
        self.R.op(eng, lambda e: e.scalar_tensor_tensor(o, a, s, b, op0, op1),
                  reads=_bufs(in0, scalar, in1), writes=_bufs(out))

    def copy(self, out, in_, eng="dve"):
        o, i = _ap(out), _ap(in_)
        self.R.op(eng, lambda e: e.tensor_copy(o, i), reads=_bufs(in_), writes=_bufs(out))

    def memset(self, out, val, eng="dve"):
        o = _ap(out)
        self.R.op(eng, lambda e: e.memset(o, val), writes=_bufs(out))

    def recip(self, out, in_):
        o, i = _ap(out), _ap(in_)
        self.R.op("dve", lambda e: e.reciprocal(o, i), reads=_bufs(in_), writes=_bufs(out))

    def rsqrt(self, out, in_):
        self.act(out, in_, AF.Ln)
        self.act(out, out, AF.Exp, scale=-0.5)

    def evac(self, out, in_, scale=1.0):
        self.rr += 1
        if self.rr % 2 == 0:
            self.act(out, in_, AF.Copy, scale=scale)
        else:
            if scale == 1.0:
                self.copy(out, in_)
            else:
                self.ts(out, in_, scale, None, ALU.mult)

    def pp(self):
        self.prr = getattr(self, "prr", 0) + 1
        return self.pbanks[self.prr % len(self.pbanks)]

    def pa(self):
        self.par = getattr(self, "par", 0) + 1
        return self.paccs[self.par % len(self.paccs)]

    def ppb(self):
        self.prb = getattr(self, "prb", 0) + 1
        return self.pbf[self.prb % len(self.pbf)]

    def wslot(self):
        self.wrr += 1
        return self.wslots[self.wrr % len(self.wslots)]

    def wload(self, src, shape):
        slot = self.wslot()
        n = 1
        for s in shape[1:]:
            n *= s
        dst = slot.t[:, 0:n]
        if len(shape) == 3:
            dst = dst.rearrange("p (a b) -> p a b", a=shape[1])
        v = V(dst, [slot.buf])
        self.dma(v, src, eng="sp")
        return v

    def build(self):
        T = self.T
        NB = T // 512
        nc = self.nc
        P = self
        xp = P.din("xp", [T, D])
        gmix = [P.din("gmix%d" % l, [128, 8]) for l in range(2)]
        gffn = [P.din("gffn%d" % l, [128, 8]) for l in range(2)]
        gfin = P.din("gfin", [128, D])
        NE = 3624
        NT = T // 128
        BF = BF16
        identb_d = P.din("identb", [128, 128], BF)
        cmpb_d = P.din("cmpb", [NT, 128, 128], BF)
        eall_d = P.din("eall", [128, NT, 128], BF)
        causal_d = P.din("causal", [128, 512], BF)
        winlow_d = P.din("winlow", [128, 512], BF)
        candb_d = P.din("candb", [NT, 128, 128])
        forced_d = P.din("forced", [NT, 128, 128])
        allowed_d = P.din("allowed", [NT, 128, 128])
        w1d_d = P.din_w("w1d", [2, 2, 128, 4096])
        w2_d = P.din("w2", [128, 256])
        peT_d = P.din("peT", [128, 128])
        gateb_d = P.din("gateb", [128, 24])
        wE = P.din_w("wE", [D, NE])
        wEo = P.din_w("wEo", [D, D])
        NO = 3592
        wO = P.din_w("wO", [D, NO])
        mnorm_d = P.din("mnorm", [128, D])
        bif_d = P.din("bif", [128, 8])
        bicol_d = P.din("bicol", [4, 1])
        sel127_d = P.din("sel127", [128, 128])
        trilb_d = P.din("trilb", [128, 128])
        onehot_d = P.din("onehot", [4, 512])
        mc_out = P.dout("mlc", [4, 128, 256])
        mn_out = P.dout("mln", [128, 4])
        mm_out = P.dout("mlm", [1, 4])
        wOo = P.din_w("wOo", [D, D])
        wU = [P.din_w("wU%d" % l, [D, 2 * DFF]) for l in range(2)]
        wD = [P.din_w("wD%d" % l, [DFF, D]) for l in range(2)]
        cw = [P.din("cw%d" % l, [128, 4, NCH]) for l in range(2)]
        ident_d = P.din("ident", [128, 128])
        mtri_d = P.din("mtri", [128, 128])
        mstr_d = P.din("mstr", [128, 128])
        wa2_d = P.din("wa2", [16, 256])
        ba_d = P.din("ba", [128, 256])
        gnorm_d = P.din("gnorm", [128, 512])
        y_out = P.dout("y", [T, D])
        kvc_out = P.dout("kvc", [T, 256])
        kvs_out = P.dout("kvs", [T, 256])
        kvw_out = P.dout("kvw", [512, 256])
        gla_out = P.dout("glast", [256, 128])
        cv_out = [P.dout("cv%d" % l, [2, DFF]) for l in range(2)]
        if self.debug:
            dbg_out = P.dout("dbg", [T, D])
            dbg2_out = P.dout("dbg2", [T, D])

        self.arena = P.sb("arena", [128, self.ARENA // 4])
        self.arena_b = P.sb("arena_b", [128, self.ARENA_B // 4])
        P.phase("KP")
        x = P.sb("x", [128, 4, D])
        xnT = P.sb("xnT", [128, 8, 512], BF16)
        xn = [P.sb("xn%d" % i, [128, D], BF16) for i in range(2)]
        junk = P.sb("junk", [128, D], BF16)
        col = [P.sb("col%d" % i, [128, 8]) for i in range(4)]
        self.wslots = [P.sb("ws%d" % i, [128, 4096], BF16) for i in range(3)]
        self.pbanks = [P.psum("pb%d" % i, [128, 512]) for i in range(4)]
        self.paccs = [P.psum("pa%d" % i, [128, 512]) for i in range(2)]
        self.pbf = [P.psum("pbf%d" % i, [128, 1024], BF16) for i in range(2)]
        ident = P.sb("identb_s", [128, 128], BF16)
        identf = P.sb("identf", [128, 128])
        mtri = P.sb("mtrif", [128, 128])
        mstr = P.sb("mstrf", [128, 128])
        g_mix = [P.sb("g_mix%d" % l, [128, 8]) for l in range(2)]
        g_ffn = [P.sb("g_ffn%d" % l, [128, 8]) for l in range(2)]
        g_fin = P.sba("F", "g_fin", [128, D])
        cwt = [P.sb("cwt%d" % l, [128, 4, NCH]) for l in range(2)]
        halo = [P.sb("halo%d" % l, [128, NCH, 2]) for l in range(2)]
        hT = P.sba("F", "hT", [128, NCH, 512], BF16)
        gbuf = [P.sba("F", "gbuf%d" % i, [128, 514]) for i in range(2)]
        acc = [P.sba("F", "acc%d" % i, [128, 512]) for i in range(2)]
        mix = P.sb("mix", [128, 4, D], BF16)
        mixT = xnT
        self.colrr = 0

        def newcol():
            self.colrr += 1
            return col[self.colrr % len(col)]

        P.emit_wcasts()
        P.dma(ident[:], identb_d)
        P.dma(identf[:], ident_d)
        P.dma(mtri[:], mtri_d)
        P.dma(mstr[:], mstr_d)
        for l in range(2):
            P.dma(g_mix[l][:], gmix[l])
            P.dma(g_ffn[l][:], gffn[l])
            P.dma(cwt[l][:], cw[l])
            P.memset(halo[l][:], 0.0)
        P.memset(mix[:], 0.0)

        if "gla" in self.parts:
            wa2 = P.sb("wa2s", [16, 256], BF16)
            P.dma(wa2[:], wa2_d, eng="pool")
            ba = P.sb("bas", [128, 256])
            P.dma(ba[:], ba_d)
            gnorm = P.sb("gnorms", [128, 512])
            P.dma(gnorm[:], gnorm_d)
            S32 = [P.sb("S32_%d" % p, [128, 128]) for p in range(2)]
            Sbf = [P.sb("Sbf_%d" % p, [128, 128], BF16) for p in range(2)]
            for p in range(2):
                P.memset(S32[p][:], 0.0)
                P.memset(Sbf[p][:], 0.0)
            gqT = P.sba("E", "gqT", [128, 2, 512])
            gkT = P.sba("E", "gkT", [128, 2, 512])
            gaT = P.sba("E", "gaT", [16, 512], BF16)
            gk_tm = P.sba("E", "gk_tm", [128, 4, 256])
            gv_tm = P.sba("E", "gv_tm", [128, 4, 512], BF16)
            gsg = P.sba("E", "gsg", [128, 4, 512], BF16)
            spl = [P.sba("E", "spl%d" % i, [128, 256]) for i in range(2)]
            AT = [P.sba("E", "AT%d" % i, [128, 2, 128]) for i in range(2)]
            BT = [P.sba("E", "BT%d" % i, [128, 2, 128]) for i in range(2)]
            qd = [P.sba("E", "qd%d" % i, [128, 2, 128], BF16) for i in range(2)]
            kd = [P.sba("E", "kd%d" % i, [128, 2, 128], BF16) for i in range(2)]
            k2 = [P.sba("E", "k2%d" % i, [128, 256], BF16) for i in range(2)]
            erev = [P.sba("E", "erev%d" % i, [128, 256]) for i in range(2)]
            attT = [P.sba("E", "attT%d" % i, [128, 128], BF16) for i in range(2)]
            osb = [P.sba("E", "osb%d" % i, [128, 512]) for i in range(2)]

        if "nsa" in self.parts:
            nqT = P.sba("E", "nqT", [128, 4, 512], BF16)
            KselT = P.sba("KP", "KselT", [128, T], BF16)
            KwinT = P.sb("KwinT", [128, 1024], BF16)
            KcrT = P.sba("E", "KcrT", [128, 512], BF16)
            VcrT = P.sba("E", "VcrT", [128, 512], BF16)
            Vsel = P.sba("KP", "Vsel", [128, NT, 2, 65], BF16)
            Vwin = P.sb("Vwin", [128, 8, 2, 65], BF16)
            P.memset(Vsel[:], 1.0)
            P.memset(Vwin[:], 1.0, eng="pool")
            hidK = P.sb("hidK", [128, 128], BF16)
            hidV = P.sb("hidV", [128, 128], BF16)
            P.memset(hidK[:], 0.0)
            P.memset(hidV[:], 0.0)
            KcT = P.sb("KcT", [128, 128], BF16)
            Vc = P.sb("Vc", [128, 2, 64], BF16)
            eall = P.sb("eall_s", [128, NT, 128], mybir.dt.float8e4)
            P.dma(eall[:], eall_d, eng="pool")
            causal = P.sb("causal_s", [128, 512], BF16)
            P.dma(causal[:], causal_d)
            winlow = P.sb("winlow_s", [128, 512], BF16)
            P.dma(winlow[:], winlow_d)
            w2s = P.sb("w2s", [128, 256], BF16)
            P.dma(w2s[:], w2_d, eng="pool")
            peT = P.sb("peTs", [128, 128], BF16)
            P.dma(peT[:], peT_d, eng="pool")
            gateb = P.sb("gatebs", [128, 24])
            P.dma(gateb[:], gateb_d)
            hb = P.sb("hb", [128, 2])
            onesf = P.sb("onesf", [128, 128])
            P.memset(onesf[:], 1.0)
            gt = P.sba("E", "gt", [128, 4, 24])
            cmpb = [P.sba("E", "cmpb%d" % i, [128, 128], BF16) for i in range(2)]
            candb = [P.sba("E", "candb%d" % i, [128, 128]) for i in range(2)]
            forced = [P.sba("E", "forced%d" % i, [128, 128]) for i in range(2)]
            allowed = [P.sba("E", "allowed%d" % i, [128, 128]) for i in range(2)]
            eT = [P.sba("E", "eT%d" % i, [128, 512]) for i in range(2)]
            rT = [P.sba("E", "rT%d" % i, [128, 512]) for i in range(1)] * 2
            pTb = [P.sba("E", "pTb%d" % i, [128, 512], BF16) for i in range(2)]
            impT = [P.sba("E", "impT%d" % i, [128, 128]) for i in range(2)]
            cand = [P.sba("E", "cand%d" % i, [128, 128]) for i in range(2)]
            cand2 = [P.sba("E", "cand2%d" % i, [128, 128]) for i in range(2)]
            m8 = [P.sba("E", "m8%d" % i, [128, 16]) for i in range(2)]
            selbT = [P.sba("E", "selbT%d" % i, [128, 512], BF16) for i in range(2)]
            eTb = [P.sba("E", "eTb%d" % i, [128, 512], BF16) for i in range(3)]
            ocmp = [P.sba("E", "ocmp%d" % i, [128, 256]) for i in range(2)]
            nacc = [P.sba("E", "nacc%d" % i, [128, 256]) for i in range(2)]
            self.erb = 0
            for c in range(2 if "hb" not in self.skip else 0):
                ps = P.pp()
                for hf in range(2):
                    w = P.wload(V(w1d_d.ap[c, hf].rearrange("p (a b) -> p a b", a=32), w1d_d.bufs), [128, 32, 128])
                    for s_ in range(32):
                        ss = hf * 32 + s_
                        P.mm(ps[:, 0:1], V(w.ap[:, s_, :], w.bufs), peT[:, c * 64 + ss:c * 64 + ss + 1],
                             start=(ss == 0), stop=(ss == 63))
                P.copy(hb[:, c:c + 1], ps[:, 0:1])

        def nsa_compress(bt):
            if "compress" in self.skip:
                return
            n0 = bt * 8
            for c, (rows, hid) in enumerate(((KcrT, hidK), (VcrT, hidV))):
                ps = P.pp()
                for hf in range(2):
                    w = P.wload(V(w1d_d.ap[c, hf].rearrange("p (a b) -> p a b", a=32), w1d_d.bufs), [128, 32, 128])
                    for s_ in range(32):
                        ss = hf * 32 + s_
                        P.mm(ps[:, 0:8], V(w.ap[:, s_, :], w.bufs), V(rows.t[:, ss:512:64], [rows.buf]),
                             start=(ss == 0), stop=(ss == 63))
                P.act(hid[:, n0:n0 + 8], ps[:, 0:8], AF.Gelu_apprx_tanh, bias=hb[:, c:c + 1])
            psk = P.pp()
            P.mm(psk[:, 0:128], w2s[:, 0:128], hidK[:], True, True)
            P.evac(KcT[:], psk[:, 0:128])
            psv = P.pp()
            P.mm(psv[:, 0:128], hidV[:], w2s[:, 128:256], True, True)
            P.evac(Vc[:], V(psv.t[:, 0:128].rearrange("p (k d) -> p k d", k=2), [psv.buf]))

        def nsa_subtile(bt, s):
            if self.stage < 2:
                return
            qi = bt * 4 + s
            i2 = qi % 2
            P.dma(cmpb[i2][:], V(cmpb_d.ap[qi], cmpb_d.bufs))
            P.dma(candb[i2][:], V(candb_d.ap[qi], candb_d.bufs))
            P.dma(forced[i2][:], V(forced_d.ap[qi], forced_d.bufs))
            P.dma(allowed[i2][:], V(allowed_d.ap[qi], allowed_d.bufs))
            for k in range(2):
                kk = (qi * 2 + k) % 2
                qv = V(nqT.t[k * 64:(k + 1) * 64, :, s * 128:(s + 1) * 128], [nqT.buf])
                ps = P.pp()
                P.mm(ps[:, :], KcT[k * 64:(k + 1) * 64, :], qv, True, False)
                for g in range(4):
                    P.mm(ps[:, g * 128:(g + 1) * 128], ident[:], cmpb[i2][:], False, g == 3)
                e_ = eT[kk]
                P.act(e_[:], ps[:, :], AF.Exp)
                ps2 = P.pp()
                P.mm(ps2[:, :], onesf[:], e_[:], True, True)
                r_ = rT[kk]
                P.ts(r_[:], ps2[:, :], 1e-30, None, ALU.max)
                P.recip(r_[:], r_[:])
                P.tt(e_[:], e_[:], r_[:], ALU.mult)
                P.copy(pTb[kk][:], e_[:], eng="pool")
                im = impT[kk]
                P.tt(im[:], e_[:, 0:128], e_[:, 128:256], ALU.add, eng="pool")
                P.tt(im[:], im[:], e_[:, 256:384], ALU.add, eng="pool")
                P.tt(im[:], im[:], e_[:, 384:512], ALU.add, eng="pool")
                psi = P.pp()
                P.tr(psi[:, 0:128], im[:], identf[:])
                pso = P.pp()
                for g in range(4):
                    P.mm(pso[:, g * 64:(g + 1) * 64], pTb[kk][:, g * 128:(g + 1) * 128], Vc[:, k, :], True, True)
                P.act(ocmp[kk][:], pso[:, 0:256], AF.Copy)
                if self.stage < 3:
                    continue
                cd, cd2, m_ = cand[kk], cand2[kk], m8[kk]
                P.tt(cd[:], psi[:, 0:128], candb[i2][:], ALU.add)
                cda, cd2a, m1a, m2a = cd.t[:], cd2.t[:], m_.t[:, 0:8], m_.t[:, 8:16]
                P.R.op("dve", lambda e, o=m1a, i=cda: e.max(out=o, in_=i), reads=[cd.buf], writes=[m_.buf])
                P.R.op("dve", lambda e, o=cd2a, r=m1a, i=cda: e.match_replace(out=o, in_to_replace=r, in_values=i, imm_value=-1e30),
                       reads=[cd.buf, m_.buf], writes=[cd2.buf])
                P.R.op("dve", lambda e, o=m2a, i=cd2a: e.max(out=o, in_=i), reads=[cd2.buf], writes=[m_.buf])
                P.ts(cd2[:], cd[:], m_[:, 12:13], None, ALU.is_ge)
                P.tt(cd2[:], cd2[:], forced[i2][:], ALU.max)
                P.tt(cd2[:], cd2[:], allowed[i2][:], ALU.mult)
                P.ts(cd2[:], cd2[:], BIG, -BIG, ALU.mult, ALU.add)
                pst = P.pp()
                P.tr(pst[:, 0:128], cd2[:], identf[:])
                sb_ = selbT[kk]
                for g in range(4):
                    P.evac(sb_[:, g * 128:(g + 1) * 128], pst[:, 0:128])
                if self.stage < 4:
                    continue
                po = P.pa()
                for kt in range(qi + 1):
                    ps = P.pp()
                    P.mm(ps[:, :], KselT[k * 64:(k + 1) * 64, kt * 128:(kt + 1) * 128], qv, True, False)
                    P.mm(ps[:, :], eall[:, kt, :], sb_[:], False, kt != qi)
                    if kt == qi:
                        P.mm(ps[:, :], ident[:], causal[:], False, True)
                    self.erb += 1
                    eb = eTb[self.erb % 3]
                    P.act(eb[:], ps[:, :], AF.Exp)
                    for g in range(4):
                        P.mm(po[:, g * 65:(g + 1) * 65], eb[:, g * 128:(g + 1) * 128], Vsel[:, kt, k, :],
                             start=(kt == 0 and g == 0), stop=(kt == qi))
                c = newcol()
                na = nacc[kk]
                pov = V(po.t[:, 0:260].rearrange("p (g d) -> p g d", g=4), [po.buf])
                P.recip(c[:, 0:4], V(pov.ap[:, :, 64], pov.bufs))
                for g in range(4):
                    gc = k * 12 + g * 3
                    P.tt(c[:, 4 + g:5 + g], c[:, g:g + 1], gt[:, s, gc + 1:gc + 2], ALU.mult)
                    P.ts(na[:, g * 64:(g + 1) * 64], ocmp[kk][:, g * 64:(g + 1) * 64], gt[:, s, gc:gc + 1], None, ALU.mult)
                    P.stt(na[:, g * 64:(g + 1) * 64], V(pov.ap[:, g, 0:64], pov.bufs), c[:, 4 + g:5 + g],
                          na[:, g * 64:(g + 1) * 64], ALU.mult, ALU.add)
                if self.stage < 5:
                    continue
                pw = P.pa()
                k0 = max(0, qi - 4)
                for kt in range(k0, qi + 1):
                    ps = P.pp()
                    last = (kt == qi)
                    low = (kt == qi - 4)
                    P.mm(ps[:, :], KwinT[k * 64:(k + 1) * 64, (kt % 8) * 128:(kt % 8 + 1) * 128], qv, True, not (last or low))
                    if last:
                        P.mm(ps[:, :], ident[:], causal[:], False, True)
                    if low:
                        P.mm(ps[:, :], ident[:], winlow[:], False, True)
                    self.erb += 1
                    eb = eTb[self.erb % 3]
                    P.act(eb[:], ps[:, :], AF.Exp)
                    for g in range(4):
                        P.mm(pw[:, g * 65:(g + 1) * 65], eb[:, g * 128:(g + 1) * 128], Vwin[:, kt % 8, k, :],
                             start=(kt == k0 and g == 0), stop=(kt == qi))
                c = newcol()
                pwv = V(pw.t[:, 0:260].rearrange("p (g d) -> p g d", g=4), [pw.buf])
                P.recip(c[:, 0:4], V(pwv.ap[:, :, 64], pwv.bufs))
                for g in range(4):
                    gc = k * 12 + g * 3
                    P.tt(c[:, 4 + g:5 + g], c[:, g:g + 1], gt[:, s, gc + 2:gc + 3], ALU.mult)
                    f0 = 512 + (k * 4 + g) * 64
                    P.stt(mix[:, s, f0:f0 + 64], V(pwv.ap[:, g, 0:64], pwv.bufs), c[:, 4 + g:5 + g],
                          na[:, g * 64:(g + 1) * 64], ALU.mult, ALU.add)

        if "mlstm" in self.parts:
            mnorm = P.sb("mnorm_s", [128, D], BF16)
            P.dma(mnorm[:], mnorm_d, eng="pool")
            bif = P.sb("bif_s", [128, 8])
            P.dma(bif[:], bif_d)
            bicol = P.sb("bicol_s", [4, 1])
            P.dma(bicol[:], bicol_d)
            nbf = P.sb("nbf_s", [128, 4])
            P.ts(nbf[:], bif[:, 4:8], -1.0, None, ALU.mult)
            sel127 = P.sb("sel127_s", [128, 128])
            P.dma(sel127[:], sel127_d)
            trilb = P.sb("trilb_s", [128, 128])
            P.dma(trilb[:], trilb_d)
            onehot = P.sb("onehot_s", [4, 512])
            P.dma(onehot[:], onehot_d)
            C32 = P.sb("C32", [128, 4, 257])
            Cbf = P.sb("Cbf", [128, 4, 257], BF16)
            P.memset(C32[:], 0.0)
            P.memset(Cbf[:], 0.0)
            m_bc = P.sb("m_bc", [128, 4])
            P.memset(m_bc[:], 0.0)
            mqT = P.sba("O", "mqT", [128, 4, 512], BF16)
            mkT = P.sba("O", "mkT", [128, 4, 512], BF16)
            mk_tm = P.sba("O", "mk_tm", [128, 4, 512], BF16)
            v_aug = P.sba("O", "v_aug", [128, 4, 4, 257], BF16)
            og = P.sba("O", "og", [128, 4, D], BF16)
            icT = P.sba("O", "icT", [4, 512])
            if_tm = P.sba("O", "if_tm", [128, 4, 8])
            sc = [P.sba("O", "sc%d" % i, [128, 64]) for i in range(2)]
            a_row = [P.sba("O", "a_row%d" % i, [4, 128]) for i in range(2)]
            swT = [P.sba("O", "swT%d" % i, [128, 128], BF16) for i in range(2)]
            t1b = [P.sba("O", "t1b%d" % i, [128, 257]) for i in range(2)]
            numb = [P.sba("O", "numb%d" % i, [128, 257]) for i in range(2)]
            hhb = [P.sba("O", "hhb%d" % i, [128, 256]) for i in range(2)]
            kwb = [P.sba("O", "kwb%d" % i, [128, 128], BF16) for i in range(2)]

        def mlstm_subtile(s):
            i2 = s % 2
            c_ = sc[i2]
            sl = slice(s * 128, (s + 1) * 128)
            P.tt(c_[:, 0:4], if_tm[:, s, 0:4], bif[:, 0:4], ALU.add)
            P.act(c_[:, 4:8], if_tm[:, s, 4:8], AF.Exp, scale=-1.0)
            P.act(c_[:, 44:48], nbf[:], AF.Exp)
            P.tt(c_[:, 4:8], c_[:, 4:8], c_[:, 44:48], ALU.mult)
            P.act(c_[:, 4:8], c_[:, 4:8], AF.Ln, bias=1.0)
            psc = P.pp()
            P.mm(psc[:, 0:4], mtri[:], c_[:, 4:8], True, True)
            psr = P.pp()
            P.mm(psr[0:4, 0:128], c_[:, 4:8], mtri[:], True, True)
            P.copy(c_[:, 8:12], psc[:, 0:4])
            P.tt(c_[:, 12:16], c_[:, 0:4], c_[:, 8:12], ALU.add)
            ar = a_row[i2]
            P.tt(ar[:], psr[0:4, 0:128], icT[:, sl], ALU.add)
            for h in range(4):
                psm = P.pp()
                P.mm(psm[:, 0:128], onehot[:, h * 128:(h + 1) * 128], ar[:], True, False)
                P.mm(psm[:, 0:128], identf[:], trilb[:], False, True)
                P.R.op("dve", lambda e, o=c_.t[:, 16 + h:17 + h], i=psm.t[:, 0:128]: e.reduce_max(o, i, AX.X),
                       reads=[], writes=[c_.buf, psm.buf])
            P.tt(c_[:, 16:20], c_[:, 16:20], m_bc[:], ALU.max)
            P.act(c_[:, 20:24], c_[:, 12:16], AF.Exp)
            P.act(c_[:, 24:28], c_[:, 16:20], AF.Exp, scale=-1.0)
            P.tt(c_[:, 44:48], m_bc[:], c_[:, 16:20], ALU.subtract)
            P.act(c_[:, 28:32], c_[:, 44:48], AF.Exp)
            P.tt(c_[:, 44:48], c_[:, 8:12], c_[:, 16:20], ALU.subtract)
            P.act(c_[:, 32:36], c_[:, 44:48], AF.Exp)
            psb = P.pp()
            P.mm(psb[:, 0:12], sel127[:], c_[:, 8:20], True, True)
            P.copy(c_[:, 48:60], psb[:, 0:12])
            P.tt(c_[:, 44:48], c_[:, 12:16], c_[:, 56:60], ALU.subtract)
            P.act(c_[:, 36:40], c_[:, 44:48], AF.Exp)
            P.tt(c_[:, 44:48], m_bc[:], c_[:, 56:60], ALU.subtract)
            P.act(c_[:, 40:44], c_[:, 44:48], AF.Exp)
            P.tt(m_bc[:], c_[:, 56:60], c_[:, 48:52], ALU.subtract)
            for h in range(4):
                hi = (s * 4 + h) % 2
                ps = P.pp()
                P.mm(ps[:, 0:128], mkT[:, h, sl], mqT[:, h, sl], True, True)
                sw = swT[hi]
                P.stt(sw[:], ps[:, 0:128], c_[:, 20 + h:21 + h], mtri[:], ALU.mult, ALU.mult)
                p1 = P.pp()
                P.mm(p1[:, 0:257], sw[:], v_aug[:, s, h, :], True, True)
                p2 = P.pp()
                P.mm(p2[:, 0:257], mqT[:, h, sl], Cbf[:, h, :], True, True)
                t1, nu, hh = t1b[hi], numb[hi], hhb[hi]
                P.act(t1[:], p2[:, 0:257], AF.Copy, scale=c_[:, 28 + h:29 + h])
                P.stt(nu[:], p1[:, 0:257], c_[:, 24 + h:25 + h], t1[:], ALU.mult, ALU.add)
                cc = newcol()
                P.ts(cc[:, 3:4], nu[:, 256:257], -1.0, None, ALU.mult)
                P.tt(cc[:, 0:1], nu[:, 256:257], cc[:, 3:4], ALU.max)
                P.tt(cc[:, 0:1], cc[:, 0:1], c_[:, 32 + h:33 + h], ALU.max)
                P.recip(cc[:, 0:1], cc[:, 0:1])
                P.ts(hh[:], nu[:, 0:256], cc[:, 0:1], None, ALU.mult)
                P.act(junk[:, 0:256], hh[:], AF.Square, accum=cc[:, 1:2])
                P.ts(cc[:, 2:3], cc[:, 1:2], 1.0 / 256, EPS, ALU.mult, ALU.add)
                P.rsqrt(cc[:, 2:3], cc[:, 2:3])
                P.stt(mix[:, s, h * 256:(h + 1) * 256], hh[:], cc[:, 2:3], og[:, s, h * 256:(h + 1) * 256], ALU.mult, ALU.mult)
                kw = kwb[hi]
                P.ts(kw[:], mk_tm[:, s, h * 128:(h + 1) * 128], c_[:, 36 + h:37 + h], None, ALU.mult)
                pc = P.pp()
                P.mm(pc[:, 0:257], kw[:], v_aug[:, s, h, :], True, True)
                P.stt(C32[:, h, :], C32[:, h, :], c_[:, 40 + h:41 + h], pc[:, 0:257], ALU.mult, ALU.add)
                P.copy(Cbf[:, h, :], C32[:, h, :], eng="pool")

        def odd_layer(bt):
            P.phase("O")
            rmsnorm_T(g_mix[1])
            P.memset(V(v_aug.t[:, :, :, 256:257], [v_aug.buf]), 1.0)
            w = P.wload(wsrc(wO, 0, 512), [128, 8, 512])
            for h in range(4):
                proj_fm(w, h * 128, 128, lambda ps, h=h: P.evac(mqT[:, h, :], ps[:, :]))
            w = P.wload(wsrc(wO, 512, 512), [128, 8, 512])
            for h in range(4):
                proj_fm(w, h * 128, 128, lambda ps, h=h: P.evac(mkT[:, h, :], ps[:, :], scale=128 ** -0.5))
            w = P.wload(wsrc(wO, 1024, 512), [128, 8, 512])
            for s in range(4):
                proj_tm(w, 0, 512, s, lambda ps, s=s: P.evac(mk_tm[:, s, :], ps[:, :], scale=128 ** -0.5))
            for hf in range(2):
                w = P.wload(wsrc(wO, 1536 + hf * 512, 512), [128, 8, 512])
                for s in range(4):
                    proj_tm(w, 0, 512, s, lambda ps, s=s, hf=hf: P.evac(
                        V(v_aug.t[:, s, 2 * hf:2 * hf + 2, 0:256], [v_aug.buf]),
                        V(ps.t[:, :].rearrange("p (h e) -> p h e", h=2), [ps.buf])))
            for hf in range(2):
                w = P.wload(wsrc(wO, 2560 + hf * 512, 512), [128, 8, 512])
                for s in range(4):
                    def f(ps, s=s, hf=hf):
                        dst = og[:, s, hf * 512:(hf + 1) * 512]
                        P.act(dst, ps[:, :], AF.Sigmoid)
                        P.tt(dst, dst, mnorm[:, hf * 512:(hf + 1) * 512], ALU.mult, eng="pool")
                    proj_tm(w, 0, 512, s, f)
            w = P.wload(wsrc(wO, 3584, 8), [128, 8, 8])
            proj_fm(w, 0, 4, lambda ps: P.act(icT[:], ps[0:4, :], AF.Identity, bias=bicol[:, 0:1]))
            for s in range(4):
                proj_tm(w, 0, 8, s, lambda ps, s=s: P.copy(if_tm[:, s, :], ps[:, 0:8]))
            for s in range(4):
                mlstm_subtile(s)
            out_proj(wOo, 1)

        def rmsnorm_T(gain, bt_rows=4):
            for s in range(bt_rows):
                c = newcol()
                P.act(junk[:], x[:, s, :], AF.Square, accum=c[:, 0:1])
                P.ts(c[:, 1:2], c[:, 0:1], 1.0 / D, EPS, ALU.mult, ALU.add)
                P.rsqrt(c[:, 2:3], c[:, 1:2])
                xs = xn[s % 2]
                P.ts(xs[:], x[:, s, :], c[:, 2:3], None, ALU.mult)
                pt = P.ppb()
                for k in range(8):
                    P.tr(pt[:, k * 128:(k + 1) * 128], xs[:, k * 128:(k + 1) * 128], ident[:])
                for k in range(8):
                    if k % 2 == 0:
                        P.act(xnT[:, k, s * 128:(s + 1) * 128], pt[:, k * 128:(k + 1) * 128], AF.Copy, scale=gain[:, k:k + 1])
                    else:
                        P.ts(xnT[:, k, s * 128:(s + 1) * 128], pt[:, k * 128:(k + 1) * 128], gain[:, k:k + 1], None, ALU.mult)

        def proj_fm(w, c0, n, out_fn):
            ps = P.pp()
            for k in range(8):
                P.mm(ps[0:n, :], V(w.ap[:, k, c0:c0 + n], w.bufs), xnT[:, k, :], start=(k == 0), stop=(k == 7))
            out_fn(ps)

        def proj_tm(w, c0, n, s, out_fn):
            ps = P.pp()
            for k in range(8):
                P.mm(ps[:, 0:n], xnT[:, k, s * 128:(s + 1) * 128], V(w.ap[:, k, c0:c0 + n], w.bufs),
                     start=(k == 0), stop=(k == 7))
            out_fn(ps)

        def wsrc(wd, c0, n):
            return V(wd.ap.rearrange("(k p) c -> p k c", p=128)[:, :, c0:c0 + n], wd.bufs)

        def out_proj(wd, l):
            for s in range(4):
                pt = P.ppb()
                for k in range(8):
                    P.tr(pt[:, k * 128:(k + 1) * 128], mix[:, s, k * 128:(k + 1) * 128], ident[:])
                P.evac(mixT[:, :, s * 128:(s + 1) * 128], V(pt.t[:, :].rearrange("p (k t) -> p k t", k=8), [pt.buf]))
            for half in range(2):
                w = P.wload(wsrc(wd, half * 512, 512), [128, 8, 512])
                for s in range(4):
                    ps = P.pp()
                    for k in range(8):
                        P.mm(ps[:, :], mixT[:, k, s * 128:(s + 1) * 128], V(w.ap[:, k, :], w.bufs),
                             start=(k == 0), stop=(k == 7))
                    P.tt(x[:, s, half * 512:(half + 1) * 512], x[:, s, half * 512:(half + 1) * 512], ps[:, :], ALU.add)

        def ffn(l, last):
            P.phase("F")
            rmsnorm_T(g_ffn[l])
            for j in range(11):
                w = P.wload(wsrc(wU[l], j * 512, 512), [128, 8, 512])
                for i in range(2):
                    c = 2 * j + i
                    psg = P.pp()
                    for k in range(8):
                        P.mm(psg[:, :], V(w.ap[:, k, i * 128:(i + 1) * 128], w.bufs), xnT[:, k, :], start=(k == 0), stop=(k == 7))
                    psu = P.pp()
                    for k in range(8):
                        P.mm(psu[:, :], V(w.ap[:, k, 256 + i * 128:256 + (i + 1) * 128], w.bufs), xnT[:, k, :],
                             start=(k == 0), stop=(k == 7))
                    gb = gbuf[c % 2]
                    ac = acc[c % 2]
                    P.act(gb[:, 2:514], psg[:, :], AF.Copy)
                    P.copy(gb[:, 0:2], halo[l][:, c, :], eng="pool")
                    P.copy(halo[l][:, c, :], gb[:, 512:514], eng="pool")
                    P.ts(ac[:], gb[:, 2:514], cwt[l][:, 2, c:c + 1], cwt[l][:, 3, c:c + 1], ALU.mult, ALU.add)
                    P.stt(ac[:], gb[:, 1:513], cwt[l][:, 1, c:c + 1], ac[:], ALU.mult, ALU.add)
                    P.stt(ac[:], gb[:, 0:512], cwt[l][:, 0, c:c + 1], ac[:], ALU.mult, ALU.add)
                    P.act(ac[:], ac[:], AF.Gelu_apprx_tanh)
                    P.tt(hT[:, c, :], ac[:], psu[:, :], ALU.mult)
            for q in range(8):
                w = P.wload(V(wD[l].ap.rearrange("(c p) n -> p c n", p=128)[:, :, q * 128:(q + 1) * 128], wD[l].bufs),
                            [128, NCH, 128])
                for s in range(4):
                    ps = P.pp()
                    for c in range(NCH):
                        P.mm(ps[:, 0:128], hT[:, c, s * 128:(s + 1) * 128], V(w.ap[:, c, :], w.bufs),
                             start=(c == 0), stop=(c == NCH - 1))
                    P.tt(x[:, s, q * 128:(q + 1) * 128], x[:, s, q * 128:(q + 1) * 128], ps[:, 0:128], ALU.add)
            if last:
                for r in range(2):
                    P.dma(V(cv_out[l].ap[r, :].rearrange("(c p) -> p c", p=128), cv_out[l].bufs), halo[l][:, :, r], nc_ok=True)

        def gla_subtile(s):
            i2 = s % 2
            ps = P.pp()
            P.mm(ps[:, 0:256], gaT[0:16, s * 128:(s + 1) * 128], wa2[:], True, True)
            sp_ = spl[i2]
            P.tt(sp_[:], ps[:, 0:256], ba[:], ALU.add)
            P.act(sp_[:], sp_[:], AF.Exp, scale=-1.0)
            P.act(sp_[:], sp_[:], AF.Ln, bias=1.0)
            psc = P.pp()
            for p in range(2):
                P.mm(psc[:, p * 128:(p + 1) * 128], sp_[:, p * 128:(p + 1) * 128], mtri[:], True, True)
            psr = P.pp()
            P.mm(psr[:, 0:256], mstr[:], sp_[:], True, True)
            a_, b_ = AT[i2], BT[i2]
            pv = V(psc.t[:, 0:256].rearrange("p (c t) -> p c t", c=2), [psc.buf])
            P.act(a_[:], pv, AF.Exp, scale=-1.0 / 16.0)
            P.act(b_[:], pv, AF.Exp, scale=1.0 / 16.0)
            P.act(erev[i2][:], psr[:, 0:256], AF.Exp, scale=-1.0 / 16.0)
            q_, k_ = qd[i2], kd[i2]
            P.stt(q_[:], gqT[:, :, s * 128:(s + 1) * 128], 0.125, a_[:], ALU.mult, ALU.mult)
            P.tt(k_[:], gkT[:, :, s * 128:(s + 1) * 128], b_[:], ALU.mult)
            P.tt(k2[i2][:], gk_tm[:, s, :], erev[i2][:], ALU.mult)
            pso = P.pa()
            for h in range(4):
                p, off = h // 2, (h % 2) * 64
                psa = P.pp()
                P.mm(psa[:, 0:128], k_[off:off + 64, p, :], q_[off:off + 64, p, :], True, True)
                at = attT[h % 2]
                P.tt(at[:], psa[:, 0:128], mtri[:], ALU.mult)
                P.mm(pso[:, h * 128:(h + 1) * 128], at[:], gv_tm[:, s, h * 128:(h + 1) * 128], True, False)
                P.mm(pso[:, h * 128:(h + 1) * 128], q_[off:off + 64, p, :], Sbf[p][off:off + 64, :], False, True)
            o_ = osb[i2]
            c = newcol()
            for h in range(4):
                P.act(o_[:, h * 128:(h + 1) * 128], pso[:, h * 128:(h + 1) * 128], AF.Square, accum=c[:, h:h + 1])
            P.ts(c[:, 4:8], c[:, 0:4], 1.0 / 128, EPS, ALU.mult, ALU.add)
            P.rsqrt(c[:, 4:8], c[:, 4:8])
            for h in range(4):
                P.stt(mix[:, s, h * 128:(h + 1) * 128], pso[:, h * 128:(h + 1) * 128], c[:, 4 + h:5 + h],
                      gsg[:, s, h * 128:(h + 1) * 128], ALU.mult, ALU.mult)
            for p in range(2):
                pss = P.pp()
                P.mm(pss[:, 0:256], k2[i2][:, p * 128:(p + 1) * 128], gv_tm[:, s, p * 256:(p + 1) * 256], True, True)
                for hh in range(2):
                    r0 = hh * 64
                    P.stt(S32[p][r0:r0 + 64, :], S32[p][r0:r0 + 64, :], a_[r0:r0 + 64, p, 127:128],
                          pss[r0:r0 + 64, hh * 128:(hh + 1) * 128], ALU.mult, ALU.add)
                P.copy(Sbf[p][:], S32[p][:], eng="pool")

        def even_layer(bt):
            nsa = "nsa" in self.parts
            P.phase("E")
            rmsnorm_T(g_mix[0])
            w = P.wload(wsrc(wE, 0, 512), [128, 8, 512])
            for c in range(2):
                proj_fm(w, c * 128, 128, lambda ps, c=c: P.evac(gqT[:, c, :], ps[:, :]))
                proj_fm(w, 256 + c * 128, 128, lambda ps, c=c: P.evac(gkT[:, c, :], ps[:, :]))
            if nsa and "g12" not in self.skip:
                w = P.wload(wsrc(wE, 512, 512), [128, 8, 512])
                for g in range(4):
                    proj_fm(w, g * 128, 128, lambda ps, g=g: P.evac(nqT[:, g, :], ps[:, :], scale=0.125))
                w = P.wload(wsrc(wE, 1024, 512), [128, 8, 512])
                proj_fm(w, 0, 128, lambda ps: P.evac(KcrT[:], ps[:, :]))
                proj_fm(w, 128, 128, lambda ps: P.evac(KselT[:, bt * 512:(bt + 1) * 512], ps[:, :]))
                proj_fm(w, 256, 128, lambda ps: P.evac(KwinT[:, (bt % 2) * 512:(bt % 2 + 1) * 512], ps[:, :]))
                proj_fm(w, 384, 128, lambda ps: P.evac(VcrT[:], ps[:, :]))
            w = P.wload(wsrc(wE, 1536, 296), [128, 8, 296])
            proj_fm(w, 280, 16, lambda ps: P.evac(gaT[:, :], ps[0:16, :]))
            for s in range(4):
                def f(ps, s=s):
                    P.evac(gk_tm[:, s, :], ps[:, 0:256])
                    if nsa and "gate" not in self.skip:
                        P.tt(gt[:, s, :], ps[:, 256:280], gateb[:], ALU.add)
                        P.act(gt[:, s, :], gt[:, s, :], AF.Exp, scale=-1.0)
                        P.ts(gt[:, s, :], gt[:, s, :], 1.0, None, ALU.add)
                        P.recip(gt[:, s, :], gt[:, s, :])
                proj_tm(w, 0, 280, s, f)
            w = P.wload(wsrc(wE, 1832, 512), [128, 8, 512])
            for s in range(4):
                proj_tm(w, 0, 512, s, lambda ps, s=s: P.evac(gv_tm[:, s, :], ps[:, :]))
            w = P.wload(wsrc(wE, 2344, 512), [128, 8, 512])
            for s in range(4):
                def f(ps, s=s):
                    P.act(gsg[:, s, :], ps[:, :], AF.Silu)
                    P.tt(gsg[:, s, :], gsg[:, s, :], gnorm[:], ALU.mult, eng="pool")
                proj_tm(w, 0, 512, s, f)
            w = P.wload(wsrc(wE, 2856, 512), [128, 8, 512])
            w2 = P.wload(wsrc(wE, 3368, 256), [128, 8, 256])
            for s in range(4):
                t0 = bt * 512 + s * 128
                qi = bt * 4 + s
                def f(ps, t0=t0, s=s, qi=qi):
                    o_ = osb[s % 2]
                    P.evac(o_[:], ps[:, :])
                    P.dma(V(kvc_out.ap[t0:t0 + 128, :], kvc_out.bufs), o_[:, 0:256])
                    P.dma(V(kvs_out.ap[t0:t0 + 128, :], kvs_out.bufs), o_[:, 256:512])
                    if nsa and "vcopy" not in self.skip:
                        P.copy(Vsel[:, qi, :, 0:64], V(o_.t[:, 384:512].rearrange("p (k d) -> p k d", k=2), [o_.buf]), eng="pool")
                proj_tm(w, 0, 512, s, f)
                if (nsa and "g7all" not in self.skip) or t0 >= T - 512:
                    def f2(ps, t0=t0, s=s, qi=qi):
                        e_ = erev[s % 2]
                        P.evac(e_[:], ps[:, 0:256])
                        if t0 >= T - 512:
                            r0 = t0 - (T - 512)
                            P.dma(V(kvw_out.ap[r0:r0 + 128, :], kvw_out.bufs), e_[:])
                        if nsa and "vcopy" not in self.skip:
                            P.copy(Vwin[:, qi % 8, :, 0:64], V(e_.t[:, 128:256].rearrange("p (k d) -> p k d", k=2), [e_.buf]), eng="pool")
                    proj_tm(w2, 0, 256, s, f2)
            if nsa:
                nsa_compress(bt)
                for s in range(4):
                    nsa_subtile(bt, s)
            if "gla" in self.parts:
                for s in range(4):
                    gla_subtile(s)
            if self.debug:
                for s in range(4):
                    t0 = bt * 512 + s * 128
                    P.dma(V(dbg2_out.ap[t0:t0 + 128, :], dbg2_out.bufs), mix[:, s, :], eng="pool")
            out_proj(wEo, 0)
            if self.debug:
                for s in range(4):
                    t0 = bt * 512 + s * 128
                    P.dma(V(dbg_out.ap[t0:t0 + 128, :], dbg_out.bufs), x[:, s, :])


        if self.samp:
            I32 = mybir.dt.int32
            NPG = 128
            xs_d = P.din("xs", [4, D])
            sgla_d = P.din("sgla", [4, 4, 64, 128])
            smc_d = P.din("smc", [4, 4, 128, 256])
            smn_d = P.din("smn", [4, 4, 128])
            smm_d = P.din("smm", [4, 4])
            scv_d = P.din("scv", [2, 4, 2, DFF])
            swin_d = P.din("swin", [4, 512, 256])
            cpool_d = P.din("cpool", [self.NPHYS * 128, 256])
            spool_d = P.din("spool", [self.NPHYS * 128, 256])
            ptT_d = P.din("ptT", [128, 4], I32)
            ptrep_d = P.din("ptrep", [128, 4, 128], I32)
            riota_d = P.din("riota", [128, 128])
            jcol_d = P.din("jcol", [128, 1])
            eye4_d = P.din("eye4", [4, 4])
            cmask_d = P.din("cmask", [128, 16])
            hc_d = P.din("hc", [2, 512], BF16)
            colsel_d = P.din("colsel", [4, 16], BF16)
            mnew_d = P.din("mnew", [128, 8], BF16)
            candbs_d = P.din("candbs", [2, 256])
            forceds_d = P.din("forceds", [2, 256])
            cwrep_d = [P.din("cwrep%d" % l, [4, 4, DFF]) for l in range(2)]
            gfin4_d = gfin
            ys_out = P.dout("ys", [4, D])
            kvcs_out = P.dout("kvcs", [4, 256])
            kvss_out = P.dout("kvss", [4, 256])
            kvws_out = P.dout("kvws", [4, 512, 256])
            glas_out = P.dout("glas", [4, 4, 64, 128])
            mlcs_out = P.dout("mlcs", [4, 4, 128, 256])
            mlns_out = P.dout("mlns", [4, 4, 128])
            mlms_out = P.dout("mlms", [4, 4])
            cvs_out = P.dout("cvs", [2, 4, 2, DFF])

            SA = lambda name, shape, dt=F32: P.sba("S", name, shape, dt)
            SB = lambda name, shape, dt=F32: P.sba("KS", name, shape, dt)
            eye4 = SA("eye4s", [4, 4])
            cmask = SA("cmasks", [128, 4, 4])
            colsel = SA("colsels", [4, 4, 4], BF16)
            mnew = SA("mnews", [128, 8], BF16)
            s_hT = SA("s_hT", [128, NCH, 4], BF16)
            s_gfin = SA("s_gfin", [4, D])
            hc = SA("hcs", [2, 4, 128], BF16)
            candbs = SA("candbss", [2, 256])
            forceds = SA("forcedss", [2, 256])
            riota = SA("riotas", [128, 128])
            jcol = SA("jcols", [128, 1])
            ptT = SA("ptTs", [128, 4], I32)
            ptTf = SA("ptTf", [128, 8])
            ptrep = SA("ptreps", [128, 4, 128], I32)
            ptrf = SA("ptrf", [128, 128])
            idxCf = SA("idxCf", [128, 128])
            idxC = SA("idxC", [128, 4, 128], I32)
            idxS = SA("idxS", [128, 4, 128], I32)
            s_gq = SA("s_gq", [4, 256])
            s_gk = SA("s_gk", [4, 256])
            s_gv = SA("s_gv", [4, 512], BF16)
            s_gsg = SA("s_gsg", [4, 512])
            s_nkv = SA("s_nkv", [4, 768])
            s_gt = SA("s_gt", [4, 24])
            s_sp = SA("s_sp", [4, 256])
            s_A = SA("s_A", [4, 256])
            s_AT = SA("s_AT", [128, 2, 4])
            s_gqT = SA("s_gqT", [128, 2, 4])
            s_qd = SA("s_qd", [128, 2, 4], BF16)
            s_qm = SA("s_qm", [128, 4, 2, 4], BF16)
            s_gaT = SA("s_gaT", [16, 4], BF16)
            s_nqT = SA("s_nqT", [128, 4, 4], BF16)
            S0f = SB("S0f", [128, 4, 2, 128])
            S0b = SB("S0b", [128, 4, 2, 128], BF16)
            s_km = SA("s_km", [4, 256], BF16)
            s_o = SA("s_o", [4, 512])
            s_tmp = SA("s_tmp", [4, 512])
            Gb = [SA("Gb%d" % i, [128, 256], BF16) for i in range(4)]
            XTb = [SA("XTb%d" % i, [128, 256], BF16) for i in range(3)]
            hidKs = SA("hidKs", [128, 128, 2], BF16)
            hidVs = SA("hidVs", [128, 128, 2], BF16)
            KcTs = SA("KcTs", [128, 256], BF16)
            Vcs = SA("Vcs", [128, 2, 128], BF16)
            eTs = SA("eTs", [128, 2, 8])
            rbs = SA("rbs", [128, 8])
            pTbs = SA("pTbs", [128, 2, 8], BF16)
            impTs = SA("impTs", [128, 2, 2])
            crow = SA("crow", [2, 256])
            crow2 = SA("crow2", [2, 256])
            m8s = SA("m8s", [2, 16])
            selbb = SA("selbb", [2, 256], BF16)
            mcol4 = SA("mcol4", [128, 2, 128, 4], BF16)
            Vaug = [SA("Vaug%d" % i, [128, 2, 65], BF16) for i in range(10)]
            ebs = [SA("ebs%d" % i, [128, 32], BF16) for i in range(4)]
            s_nkvb = SA("s_nkvb", [4, 768], BF16)
            Gnew = SA("Gnew", [128, 256], BF16)
            ounb = SA("ounb", [4, 3, 2, 65])
            Rexp = SB("Rexp", [4, 3, 2, 256], BF16)
            Rtm = SB("Rtm", [4, 3, 512])
            cst = SA("cst", [4, 2, 256])
            cwj = SA("cwj", [4, 4, 256])
            gpre = SA("gpre", [4, 256])
            accs = SA("accs", [4, 256])
            hs = SA("hs", [4, DFF], BF16)
            SB = lambda name, shape, dt=F32: P.sba("KS", name, shape, dt)
            C0f = [SB("C0f%d" % i, [128, 4, 257]) for i in range(2)]
            C0b = [SB("C0b%d" % i, [128, 4, 257], BF16) for i in range(2)]
            s_mq = SA("s_mq", [4, 512])
            s_mk = SA("s_mk", [4, 512])
            s_mv = SA("s_mv", [4, 4, 257], BF16)
            s_og = SB("s_og", [4, D])
            s_if = SA("s_if", [4, 8])
            s_mqT = SA("s_mqT", [128, 4, 4], BF16)
            s_mqm = SA("s_mqm", [128, 4, 4, 4], BF16)
            s_sc = SA("s_sc", [4, 64])
            s_m0 = SA("s_m0", [4, 4])
            s_dg = SA("s_dg", [4, 4, 8])
            s_bc = SA("s_bc", [128, 16])
            s_kmb = SA("s_kmb", [4, 512], BF16)
            s_kw = SA("s_kw", [4, 512], BF16)
            s_num = SB("s_num", [4, 4, 257])

        def s_rmsnorm(gain):
            c = newcol()
            P.act(junk[0:4, :], x[0:4, 0, :], AF.Square, accum=c[0:4, 0:1])
            P.ts(c[0:4, 1:2], c[0:4, 0:1], 1.0 / D, EPS, ALU.mult, ALU.add)
            P.rsqrt(c[0:4, 2:3], c[0:4, 1:2])
            xs_ = xn[0]
            P.ts(xs_[0:4, :], x[0:4, 0, :], c[0:4, 2:3], None, ALU.mult)
            pt = P.ppb()
            for k in range(8):
                P.tr(pt[:, k * 128:k * 128 + 4], xs_[0:4, k * 128:(k + 1) * 128], ident[0:4, 0:4])
            for k in range(8):
                P.act(xnT[:, k, 0:4], pt[:, k * 128:k * 128 + 4], AF.Copy, scale=gain[:, k:k + 1])

        def s_fm(w, c0, n, out_fn):
            ps = P.pp()
            for k in range(8):
                P.mm(ps[0:n, 0:4], V(w.ap[:, k, c0:c0 + n], w.bufs), xnT[:, k, 0:4], start=(k == 0), stop=(k == 7))
            out_fn(ps)

        def s_tm(w, c0, n, out_fn):
            ps = P.pp()
            for k in range(8):
                P.mm(ps[0:4, 0:n], xnT[:, k, 0:4], V(w.ap[:, k, c0:c0 + n], w.bufs), start=(k == 0), stop=(k == 7))
            out_fn(ps)

        def s_out_proj(wd):
            pt = P.ppb()
            for k in range(8):
                P.tr(pt[:, k * 128:k * 128 + 4], mix[0:4, 0, k * 128:(k + 1) * 128], ident[0:4, 0:4])
            for k in range(8):
                P.evac(mixT[:, k, 0:4], pt[:, k * 128:k * 128 + 4])
            for half in range(2):
                w = P.wload(wsrc(wd, half * 512, 512), [128, 8, 512])
                ps = P.pp()
                for k in range(8):
                    P.mm(ps[0:4, :], mixT[:, k, 0:4], V(w.ap[:, k, :], w.bufs), start=(k == 0), stop=(k == 7))
                P.tt(x[0:4, 0, half * 512:(half + 1) * 512], x[0:4, 0, half * 512:(half + 1) * 512], ps[0:4, :], ALU.add)

        def s_ffn(l):
            s_rmsnorm(g_ffn[l])
            for j in range(11):
                w = P.wload(wsrc(wU[l], j * 512, 512), [128, 8, 512])
                js = slice(j * 256, (j + 1) * 256)
                P.dma(cst[:], V(scv_d.ap[l, :, :, js], scv_d.bufs))
                P.dma(cwj[:], V(cwrep_d[l].ap[:, :, js], cwrep_d[l].bufs))
                ps = P.pp()
                for k in range(8):
                    P.mm(ps[0:4, :], xnT[:, k, 0:4], V(w.ap[:, k, :], w.bufs), start=(k == 0), stop=(k == 7))
                P.copy(gpre[:], ps[0:4, 0:256])
                P.dma(V(cvs_out.ap[l, :, 0, js], cvs_out.bufs), cst[:, 1, :])
                P.dma(V(cvs_out.ap[l, :, 1, js], cvs_out.bufs), gpre[:])
                P.tt(accs[:], gpre[:], cwj[:, 2, :], ALU.mult)
                P.tt(accs[:], accs[:], cwj[:, 3, :], ALU.add)
                P.tt(cst[:, 0, :], cst[:, 0, :], cwj[:, 0, :], ALU.mult)
                P.tt(accs[:], accs[:], cst[:, 0, :], ALU.add)
                P.tt(cst[:, 1, :], cst[:, 1, :], cwj[:, 1, :], ALU.mult)
                P.tt(accs[:], accs[:], cst[:, 1, :], ALU.add)
                P.act(accs[:], accs[:], AF.Gelu_apprx_tanh)
                P.tt(hs[:, js], accs[:], ps[0:4, 256:512], ALU.mult)
            pt = P.ppb()
            for c in range(NCH):
                P.tr(pt[:, c * 4:(c + 1) * 4], hs[0:4, c * 128:(c + 1) * 128], ident[0:4, 0:4])
            P.evac(s_hT[:, :, :], V(pt.t[:, 0:NCH * 4].rearrange("p (c t) -> p c t", c=NCH), [pt.buf]))
            for q in range(8):
                w = P.wload(V(wD[l].ap.rearrange("(c p) n -> p c n", p=128)[:, :, q * 128:(q + 1) * 128], wD[l].bufs),
                            [128, NCH, 128])
                ps = P.pp()
                for c in range(NCH):
                    P.mm(ps[0:4, 0:128], s_hT[:, c, :], V(w.ap[:, c, :], w.bufs), start=(c == 0), stop=(c == NCH - 1))
                P.tt(x[0:4, 0, q * 128:(q + 1) * 128], x[0:4, 0, q * 128:(q + 1) * 128], ps[0:4, 0:128], ALU.add)

        def s_gather(pool_d, idx, b, r):
            self.grr = getattr(self, "grr", 0) + 1
            g = Gb[self.grr % 4]
            o_, i_, ix = g.t[:], pool_d.ap, idx.t[:, b, r:r + 1]
            self.R.op("pool", lambda e: e.indirect_dma_start(out=o_, out_offset=None, in_=i_,
                                                             in_offset=bass.IndirectOffsetOnAxis(ap=ix, axis=0)),
                      reads=[idx.buf] + pool_d.bufs, writes=[g.buf], dma=True)
            return g

        def s_even():
            s_rmsnorm(g_mix[0])
            w = P.wload(wsrc(wE, 0, 512), [128, 8, 512])
            for c in range(2):
                s_fm(w, c * 128, 128, lambda ps, c=c: P.copy(s_gqT[:, c, :], ps[:, 0:4]))
            s_tm(w, 0, 512, lambda ps: (P.copy(s_gq[:], ps[0:4, 0:256]), P.act(s_gk[:], ps[0:4, 256:512], AF.Copy)))
            w = P.wload(wsrc(wE, 512, 512), [128, 8, 512])
            for g in range(4):
                s_fm(w, g * 128, 128, lambda ps, g=g: P.evac(s_nqT[:, g, :], ps[:, 0:4], scale=0.125))
            w = P.wload(wsrc(wE, 1536, 296), [128, 8, 296])
            s_fm(w, 280, 16, lambda ps: P.evac(s_gaT[:, :], ps[0:16, 0:4]))
            def fg_(ps):
                P.tt(s_gt[:], ps[0:4, 256:280], gateb[0:4, :], ALU.add)
                P.act(s_gt[:], s_gt[:], AF.Exp, scale=-1.0)
                P.ts(s_gt[:], s_gt[:], 1.0, None, ALU.add)
                P.recip(s_gt[:], s_gt[:])
            s_tm(w, 0, 280, fg_)
            w = P.wload(wsrc(wE, 1832, 512), [128, 8, 512])
            s_tm(w, 0, 512, lambda ps: P.evac(s_gv[:], ps[0:4, :]))
            w = P.wload(wsrc(wE, 2344, 512), [128, 8, 512])
            def fs_(ps):
                P.act(s_gsg[:], ps[0:4, :], AF.Silu)
                P.tt(s_gsg[:], s_gsg[:], gnorm[0:4, :], ALU.mult)
            s_tm(w, 0, 512, fs_)
            w = P.wload(wsrc(wE, 2856, 512), [128, 8, 512])
            s_tm(w, 0, 512, lambda ps: P.copy(s_nkv[:, 0:512], ps[0:4, :]))
            w = P.wload(wsrc(wE, 3368, 256), [128, 8, 256])
            s_tm(w, 0, 256, lambda ps: P.copy(s_nkv[:, 512:768], ps[0:4, 0:256]))
            P.dma(kvcs_out, s_nkv[:, 0:256])
            P.dma(kvss_out, s_nkv[:, 256:512])
            for b in range(4):
                P.dma(V(kvws_out.ap[b, 0:511, :], kvws_out.bufs), V(swin_d.ap[b, 1:512, :], swin_d.bufs))
                P.dma(V(kvws_out.ap[b, 511:512, :], kvws_out.bufs), s_nkv[b:b + 1, 512:768])
            if "s_gla" not in self.skip:
                s_gla()
            if "s_nsa" not in self.skip:
                s_nsa()
            s_out_proj(wEo)

        def s_gla():
            ps = P.pp()
            P.mm(ps[0:4, 0:256], s_gaT[0:16, 0:4], wa2[:], True, True)
            P.tt(s_sp[:], ps[0:4, 0:256], ba[0:4, :], ALU.add)
            P.act(s_sp[:], s_sp[:], AF.Exp, scale=-1.0)
            P.act(s_sp[:], s_sp[:], AF.Ln, bias=1.0)
            P.act(s_A[:], s_sp[:], AF.Exp, scale=-1.0 / 16.0)
            pst = P.pp()
            for p in range(2):
                P.tr(pst[:, p * 4:(p + 1) * 4], s_A[0:4, p * 128:(p + 1) * 128], identf[0:4, 0:4])
            P.copy(s_AT[:], V(pst.t[:, 0:8].rearrange("p (c t) -> p c t", c=2), [pst.buf]))
            P.stt(s_qd[:], s_gqT[:], 0.125, s_AT[:], ALU.mult, ALU.mult)
            for b in range(4):
                for p in range(2):
                    P.tt(s_qm[:, b, p, :], s_qd[:, p, :], cmask[:, b, :], ALU.mult)
                    P.dma(S0f[:, b, p, :], V(sgla_d.ap[b, 2 * p:2 * p + 2].rearrange("h d e -> (h d) e"), sgla_d.bufs))
            P.copy(S0b[:], S0f[:], eng="pool")
            psoh = [P.pa(), P.pa()]
            for h in range(4):
                p, off = h // 2, (h % 2) * 64
                for b in range(4):
                    P.mm(psoh[h % 2][0:4, p * 128:(p + 1) * 128], s_qm[off:off + 64, b, p, :], S0b[off:off + 64, b, p, :],
                         start=(b == 0), stop=(b == 3))
            P.tt(s_tmp[:, 0:256], s_gq[:], s_gk[:], ALU.mult)
            c = newcol()
            tv = s_tmp.t[:, 0:256].rearrange("p (h d) -> p h d", h=4)
            co = c.t[0:4, 0:4]
            P.R.op("dve", lambda e: e.reduce_sum(co, tv, AX.X), reads=[s_tmp.buf], writes=[c.buf])
            P.ts(c[0:4, 0:4], c[0:4, 0:4], 0.125, None, ALU.mult)
            for h in range(4):
                P.stt(s_o[:, h * 128:(h + 1) * 128], s_gv[:, h * 128:(h + 1) * 128], c[0:4, h:h + 1],
                      psoh[h % 2][0:4, (h // 2) * 128:(h // 2 + 1) * 128], ALU.mult, ALU.add)
            c2 = newcol()
            for h in range(4):
                P.act(s_tmp[:, h * 128:(h + 1) * 128], s_o[:, h * 128:(h + 1) * 128], AF.Square, accum=c2[0:4, h:h + 1])
            P.ts(c2[0:4, 4:8], c2[0:4, 0:4], 1.0 / 128, EPS, ALU.mult, ALU.add)
            P.rsqrt(c2[0:4, 4:8], c2[0:4, 4:8])
            for h in range(4):
                P.stt(mix[0:4, 0, h * 128:(h + 1) * 128], s_o[:, h * 128:(h + 1) * 128], c2[0:4, 4 + h:5 + h],
                      s_gsg[:, h * 128:(h + 1) * 128], ALU.mult, ALU.mult)
            for b in range(4):
                P.ts(s_km[:], s_gk[:], eye4[:, b:b + 1], None, ALU.mult)
                for p in range(2):
                    pss = P.pp()
                    P.mm(pss[:, 0:256], s_km[0:4, p * 128:(p + 1) * 128], s_gv[0:4, p * 256:(p + 1) * 256], True, True)
                    for hh in range(2):
                        r0 = hh * 64
                        P.stt(S0f[r0:r0 + 64, b, p, :], S0f[r0:r0 + 64, b, p, :], s_AT[r0:r0 + 64, p, b:b + 1],
                              pss[r0:r0 + 64, hh * 128:(hh + 1) * 128], ALU.mult, ALU.add)
                    P.dma(V(glas_out.ap[b, 2 * p:2 * p + 2].rearrange("h d e -> (h d) e"), glas_out.bufs), S0f[:, b, p, :])

        def s_nsa():
            P.copy(ptTf[:, 0:4], ptT[:])
            P.ts(ptTf[:, 4:8], ptTf[:, 0:4], 128.0, None, ALU.mult)
            P.copy(ptrf[:], V(ptrep.t[:, 0, :], [ptrep.buf]))
            for b in range(4):
                P.ts(idxCf[:], riota[:], ptTf[:, 4 + b:5 + b], None, ALU.add)
                P.copy(idxC[:, b, :], idxCf[:])
                P.copy(ptrf[:], V(ptrep.t[:, b, :], [ptrep.buf]))
                P.ts(idxCf[:], ptrf[:], 128.0, jcol[:, 0:1], ALU.mult, ALU.add)
                P.copy(idxS[:, b, :], idxCf[:])
            P.memset(Rtm[:], 0.0)
            P.copy(s_nkvb[:], s_nkv[:])
            for va in Vaug:
                P.memset(va[:], 1.0, eng="pool")
            for b in range(4):
                psC = P.pa()
                first = True
                for hf in range(2):
                    wk = P.wload(V(w1d_d.ap[0, hf].rearrange("p (a b) -> p a b", a=32), w1d_d.bufs), [128, 32, 128])
                    wv = P.wload(V(w1d_d.ap[1, hf].rearrange("p (a b) -> p a b", a=32), w1d_d.bufs), [128, 32, 128])
                    for s_ in range(32):
                        ss = hf * 32 + s_
                        for nl in range(2):
                            g = s_gather(cpool_d, idxC, b, nl * 64 + ss)
                            pt = P.ppb()
                            P.tr(pt[:, 0:128], g[:, 0:128], ident[:])
                            P.tr(pt[:, 128:256], g[:, 128:256], ident[:])
                            self.xrr = getattr(self, "xrr", 0) + 1
                            xt = XTb[self.xrr % 3]
                            P.evac(xt[:], pt[:, 0:256])
                            last = (ss == 63)
                            P.mm(psC[:, nl * 128:(nl + 1) * 128], V(wk.ap[:, s_, :], wk.bufs), xt[:, 0:128], first, last)
                            first = False
                            P.mm(psC[:, (2 + nl) * 128:(3 + nl) * 128], V(wv.ap[:, s_, :], wv.bufs), xt[:, 128:256], False, last)
                for nl in range(2):
                    P.act(V(hidKs.t[:, :, nl], [hidKs.buf]), psC[:, nl * 128:(nl + 1) * 128], AF.Gelu_apprx_tanh, bias=hb[:, 0:1])
                    P.act(V(hidVs.t[:, :, nl], [hidVs.buf]), psC[:, (2 + nl) * 128:(3 + nl) * 128], AF.Gelu_apprx_tanh, bias=hb[:, 1:2])
                hk2 = V(hidKs.t[:, :, :].rearrange("p a b -> p (a b)"), [hidKs.buf])
                hv2 = hidVs.t[:, :, :].rearrange("p a b -> p (a b)")
                psk = P.pp()
                P.mm(psk[:, 0:256], w2s[:, 0:128], hk2, True, True)
                P.evac(KcTs[:], psk[:, 0:256])
                for t_ in range(2):
                    psv = P.pp()
                    P.mm(psv[:, 0:128], V(hv2[:, t_ * 128:(t_ + 1) * 128], [hidVs.buf]), w2s[:, 128:256], True, True)
                    P.evac(Vcs[:, t_, :], psv[:, 0:128])
                for k in range(2):
                    pse = P.pp()
                    for t_ in range(2):
                        P.mm(pse[:, t_ * 4:(t_ + 1) * 4], KcTs[k * 64:(k + 1) * 64, t_ * 128:(t_ + 1) * 128],
                             V(s_nqT.t[k * 64:(k + 1) * 64, :, b], [s_nqT.buf]), True, True)
                    P.act(eTs[:, :, k * 4:(k + 1) * 4], V(pse.t[:, 0:8].rearrange("p (t c) -> p t c", t=2), [pse.buf]), AF.Exp)
                ps2 = P.pp()
                for t_ in range(2):
                    P.mm(ps2[:, 0:8], onesf[:], eTs[:, t_, :], t_ == 0, t_ == 1)
                P.ts(rbs[:], ps2[:, 0:8], 1e-30, None, ALU.max)
                P.recip(rbs[:], rbs[:])
                for t_ in range(2):
                    P.tt(eTs[:, t_, :], eTs[:, t_, :], rbs[:], ALU.mult)
                P.copy(pTbs[:], eTs[:], eng="pool")
                iv = eTs.t[:, :, :].rearrange("p t (k g) -> p (t k) g", k=2)
                io = impTs.t[:, :, :].rearrange("p t k -> p (t k)")
                P.R.op("dve", lambda e, io=io, iv=iv: e.reduce_sum(io, iv, AX.X), reads=[eTs.buf], writes=[impTs.buf])
                pso = P.pp()
                for k in range(2):
                    for t_ in range(2):
                        P.mm(pso[0:4, k * 64:(k + 1) * 64], pTbs[:, t_, k * 4:(k + 1) * 4], Vcs[:, t_, k * 64:(k + 1) * 64],
                             t_ == 0, t_ == 1)
                c = newcol()
                for k in range(2):
                    for g in range(4):
                        P.ts(Rexp[:, 0, k, g * 64:(g + 1) * 64], pso[0:4, k * 64:(k + 1) * 64], eye4[:, g:g + 1], None, ALU.mult)
                psr = P.pp()
                for t_ in range(2):
                    P.tr(psr[0:2, t_ * 128:(t_ + 1) * 128], impTs[:, t_, :], identf[:])
                P.tt(crow[:], psr[0:2, 0:256], candbs[:], ALU.add)
                ca, c2a, m1a, m2a = crow.t[:], crow2.t[:], m8s.t[:, 0:8], m8s.t[:, 8:16]
                P.R.op("dve", lambda e, o=m1a, i=ca: e.max(out=o, in_=i), reads=[crow.buf], writes=[m8s.buf])
                P.R.op("dve", lambda e, o=c2a, r=m1a, i=ca: e.match_replace(out=o, in_to_replace=r, in_values=i, imm_value=-1e30),
                       reads=[crow.buf, m8s.buf], writes=[crow2.buf])
                P.R.op("dve", lambda e, o=m2a, i=c2a: e.max(out=o, in_=i), reads=[crow2.buf], writes=[m8s.buf])
                P.ts(crow2[:], crow[:], m8s[:, 12:13], None, ALU.is_ge)
                P.tt(crow2[:], crow2[:], forceds[:], ALU.max)
                P.ts(crow2[:], crow2[:], BIG, -BIG, ALU.mult, ALU.add)
                P.copy(selbb[:], crow2[:])
                for k in range(2):
                    psm = P.pp()
                    for nl in range(2):
                        P.mm(psm[:, 0:128], hc[:, k * 2 + nl, :], V(selbb.t[:, nl:256:2], [selbb.buf]), nl == 0, nl == 1)
                    for g in range(4):
                        P.evac(V(mcol4.t[:, k, :, g], [mcol4.buf]), psm[:, 0:128])
                for br in (1, 2):
                    npg = 128 if br == 1 else 4
                    po = P.pa()
                    firstv = True
                    GRP = 8
                    ntile = npg + 1
                    for t0_ in range(0, ntile, GRP):
                        tl = list(range(t0_, min(ntile, t0_ + GRP)))
                        psk_ = [P.pp(), P.pp()]
                        tiles = []
                        for ti, pg in enumerate(tl):
                            if pg < npg:
                                if br == 1:
                                    g = s_gather(spool_d, idxS, b, pg)
                                else:
                                    self.grr += 1
                                    g = Gb[self.grr % 4]
                                    P.dma(g[:], V(swin_d.ap[b, pg * 128:(pg + 1) * 128, :], swin_d.bufs), eng="pool")
                            else:
                                g = Gnew
                                P.memset(Gnew[:], 0.0)
                                P.dma(Gnew[0:1, :], s_nkvb[b:b + 1, br * 256:(br + 1) * 256])
                            pt = P.ppb()
                            P.tr(pt[:, 0:128], g[:, 0:128], ident[:])
                            self.xrr += 1
                            xt = XTb[self.xrr % 3]
                            P.evac(xt[:, 0:128], pt[:, 0:128])
                            self.vrr = getattr(self, "vrr", 0) + 1
                            va = Vaug[self.vrr % len(Vaug)]
                            P.copy(va[:, :, 0:64], V(g.t[:, 128:256].rearrange("p (k d) -> p k d", k=2), [g.buf]), eng="pool")
                            tiles.append(va)
                            for k in range(2):
                                cs = slice(ti * 4, ti * 4 + 4)
                                qv = V(s_nqT.t[k * 64:(k + 1) * 64, :, b], [s_nqT.buf])
                                extra = []
                                if br == 1 and pg < npg:
                                    extra.append((ident[:], V(mcol4.t[:, k, pg, :], [mcol4.buf])))
                                if pg == npg:
                                    extra.append((ident[:], mnew[:, 0:4]))
                                if br == 2 and pg == 0:
                                    extra.append((ident[:], mnew[:, 4:8]))
                                P.mm(psk_[k][:, cs], xt[k * 64:(k + 1) * 64, 0:128], qv, True, len(extra) == 0)
                                for ei, (l_, r_) in enumerate(extra):
                                    P.mm(psk_[k][:, cs], l_, r_, False, ei == len(extra) - 1)
                        nt_ = len(tl)
                        ebk = []
                        for k in range(2):
                            self.err = getattr(self, "err", 0) + 1
                            eb = ebs[self.err % len(ebs)]
                            P.act(eb[:, 0:nt_ * 4], psk_[k][:, 0:nt_ * 4], AF.Exp)
                            ebk.append(eb)
                        for ti, pg in enumerate(tl):
                            for k in range(2):
                                cs = slice(ti * 4, ti * 4 + 4)
                                lastv = (pg == npg)
                                P.mm(po[0:4, k * 65:(k + 1) * 65], ebk[k][:, cs], tiles[ti][:, k, :], firstv, lastv)
                                firstv = False
                    P.copy(ounb[:, br, :, :], V(po.t[0:4, 0:130].rearrange("p (k d) -> p k d", k=2), [po.buf]))
                    c = newcol()
                    P.recip(c[0:4, 0:2], V(ounb.t[:, br, :, 64], [ounb.buf]))
                    for k in range(2):
                        for g in range(4):
                            P.ts(Rexp[:, br, k, g * 64:(g + 1) * 64], ounb[:, br, k, 0:64], c[0:4, k:k + 1], eye4[:, g:g + 1],
                                 ALU.mult, ALU.mult)
                for br in range(3):
                    psR = P.pp()
                    for k in range(2):
                        P.mm(psR[0:4, k * 256:(k + 1) * 256], colsel[:, b, :], Rexp[:, br, k, :], True, True)
                    P.tt(Rtm[:, br, :], Rtm[:, br, :], psR[0:4, :], ALU.add)
            for k in range(2):
                for g in range(4):
                    gc = k * 12 + g * 3
                    f0 = (k * 4 + g) * 64
                    P.ts(s_tmp[:, f0:f0 + 64], Rtm[:, 0, f0:f0 + 64], s_gt[:, gc:gc + 1], None, ALU.mult)
                    P.stt(s_tmp[:, f0:f0 + 64], Rtm[:, 1, f0:f0 + 64], s_gt[:, gc + 1:gc + 2], s_tmp[:, f0:f0 + 64], ALU.mult, ALU.add)
                    P.stt(mix[0:4, 0, 512 + f0:512 + f0 + 64], Rtm[:, 2, f0:f0 + 64], s_gt[:, gc + 2:gc + 3], s_tmp[:, f0:f0 + 64],
                          ALU.mult, ALU.add)

        def s_odd():
            s_rmsnorm(g_mix[1])
            w = P.wload(wsrc(wO, 0, 512), [128, 8, 512])
            s_tm(w, 0, 512, lambda ps: P.copy(s_mq[:], ps[0:4, :]))
            for h in range(4):
                s_fm(w, h * 128, 128, lambda ps, h=h: P.evac(s_mqT[:, h, :], ps[:, 0:4]))
            w = P.wload(wsrc(wO, 512, 512), [128, 8, 512])
            s_tm(w, 0, 512, lambda ps: P.act(s_mk[:], ps[0:4, :], AF.Copy, scale=128 ** -0.5))
            for hf in range(2):
                w = P.wload(wsrc(wO, 1536 + hf * 512, 512), [128, 8, 512])
                s_tm(w, 0, 512, lambda ps, hf=hf: P.evac(s_mv[:, 2 * hf:2 * hf + 2, 0:256],
                                                         V(ps.t[0:4, :].rearrange("p (h e) -> p h e", h=2), [ps.buf])))
            P.memset(V(s_mv.t[:, :, 256:257], [s_mv.buf]), 1.0)
            for hf in range(2):
                w = P.wload(wsrc(wO, 2560 + hf * 512, 512), [128, 8, 512])
                def f(ps, hf=hf):
                    dst = s_og[:, hf * 512:(hf + 1) * 512]
                    P.act(dst, ps[0:4, :], AF.Sigmoid)
                    P.tt(dst, dst, mnorm[0:4, hf * 512:(hf + 1) * 512], ALU.mult)
                s_tm(w, 0, 512, f)
            w = P.wload(wsrc(wO, 3584, 8), [128, 8, 8])
            s_tm(w, 0, 8, lambda ps: P.copy(s_if[:], ps[0:4, 0:8]))
            P.dma(s_m0[:], smm_d)
            c_ = s_sc
            P.tt(c_[:, 0:4], s_if[:, 0:4], bif[0:4, 0:4], ALU.add)
            P.act(c_[:, 4:8], s_if[:, 4:8], AF.Exp, scale=-1.0)
            P.act(c_[:, 36:40], nbf[0:4, :], AF.Exp)
            P.tt(c_[:, 4:8], c_[:, 4:8], c_[:, 36:40], ALU.mult)
            P.act(c_[:, 4:8], c_[:, 4:8], AF.Ln, bias=1.0)
            P.tt(c_[:, 8:12], s_m0[:], c_[:, 4:8], ALU.subtract)
            P.tt(c_[:, 12:16], c_[:, 8:12], c_[:, 0:4], ALU.max)
            P.tt(c_[:, 36:40], c_[:, 0:4], c_[:, 12:16], ALU.subtract)
            P.act(c_[:, 16:20], c_[:, 36:40], AF.Exp)
            P.tt(c_[:, 36:40], c_[:, 8:12], c_[:, 12:16], ALU.subtract)
            P.act(c_[:, 20:24], c_[:, 36:40], AF.Exp)
            P.act(c_[:, 24:28], c_[:, 12:16], AF.Exp, scale=-1.0)
            P.dma(mlms_out, c_[:, 12:16])
            P.tt(s_tmp[:], s_mq[:], s_mk[:], ALU.mult)
            tv = s_tmp.t[:, :].rearrange("p (h d) -> p h d", h=4)
            co = c_.t[:, 28:32]
            P.R.op("dve", lambda e: e.reduce_sum(co, tv, AX.X), reads=[s_tmp.buf], writes=[c_.buf])
            P.tt(c_[:, 32:36], c_[:, 28:32], c_[:, 16:20], ALU.mult)
            P.memset(s_num[:], 0.0)
            for h in range(4):
                P.ts(s_kw[:, h * 128:(h + 1) * 128], s_mk[:, h * 128:(h + 1) * 128], c_[:, 16 + h:17 + h], None, ALU.mult)
            for b in range(4):
                P.ts(s_dg[:, b, 0:4], c_[:, 20:24], eye4[:, b:b + 1], None, ALU.mult)
            psb = P.pp()
            P.mm(psb[:, 0:16], onesf[0:4, :], V(s_dg.t[:, :, 0:4], [s_dg.buf]), True, True)
            P.copy(s_bc[:], psb[:, 0:16])
            for b in range(4):
                cf, cb = C0f[b % 2], C0b[b % 2]
                for h in range(4):
                    P.dma(cf[:, h, 0:256], V(smc_d.ap[b, h], smc_d.bufs))
                    P.dma(cf[:, h, 256:257], V(smn_d.ap[b, h, :].rearrange("(d o) -> d o", o=1), smn_d.bufs), nc_ok=True)
                P.copy(cb[:], cf[:], eng="pool")
                P.ts(s_kmb[:], s_kw[:], eye4[:, b:b + 1], None, ALU.mult)
                for h in range(4):
                    P.tt(s_mqm[:, b, h, :], s_mqT[:, h, :], cmask[:, b, :], ALU.mult)
                    pn = P.pp()
                    P.mm(pn[0:4, 0:257], s_mqm[:, b, h, :], cb[:, h, :], True, True)
                    P.tt(s_num[:, h, :], s_num[:, h, :], pn[0:4, 0:257], ALU.add)
                    pc = P.pp()
                    P.mm(pc[:, 0:257], s_kmb[0:4, h * 128:(h + 1) * 128], s_mv[0:4, h, :], True, True)
                    P.stt(cf[:, h, :], cf[:, h, :], s_bc[:, b * 4 + h:b * 4 + h + 1], pc[:, 0:257], ALU.mult, ALU.add)
                    P.dma(V(mlcs_out.ap[b, h], mlcs_out.bufs), cf[:, h, 0:256])
                    P.dma(V(mlns_out.ap[b, h, :].rearrange("(d o) -> d o", o=1), mlns_out.bufs), cf[:, h, 256:257], nc_ok=True)
            for h in range(4):
                nu = V(s_num.t[:, h, :], [s_num.buf])
                P.ts(nu, nu, c_[:, 20 + h:21 + h], None, ALU.mult)
                P.stt(nu, s_mv[:, h, :], c_[:, 32 + h:33 + h], nu, ALU.mult, ALU.add)
                cc = newcol()
                P.ts(cc[0:4, 3:4], s_num[:, h, 256:257], -1.0, None, ALU.mult)
                P.tt(cc[0:4, 0:1], s_num[:, h, 256:257], cc[0:4, 3:4], ALU.max)
                P.tt(cc[0:4, 0:1], cc[0:4, 0:1], c_[:, 24 + h:25 + h], ALU.max)
                P.recip(cc[0:4, 0:1], cc[0:4, 0:1])
                P.ts(s_o[:, 0:256], s_num[:, h, 0:256], cc[0:4, 0:1], None, ALU.mult)
                P.act(s_tmp[:, 0:256], s_o[:, 0:256], AF.Square, accum=cc[0:4, 1:2])
                P.ts(cc[0:4, 2:3], cc[0:4, 1:2], 1.0 / 256, EPS, ALU.mult, ALU.add)
                P.rsqrt(cc[0:4, 2:3], cc[0:4, 2:3])
                P.stt(mix[0:4, 0, h * 256:(h + 1) * 256], s_o[:, 0:256], cc[0:4, 2:3], s_og[:, h * 256:(h + 1) * 256], ALU.mult, ALU.mult)
            s_out_proj(wOo)

        def sample_pass():
            P.phase("S")
            P.phase("KS")
            P.dma(eye4[:], eye4_d)
            P.dma(cmask[:], V(cmask_d.ap.rearrange("p (a b) -> p a b", a=4), cmask_d.bufs))
            P.dma(colsel[:], V(colsel_d.ap.rearrange("p (a b) -> p a b", a=4), colsel_d.bufs))
            P.dma(mnew[:], mnew_d)
            P.dma(hc[:], V(hc_d.ap.rearrange("p (a b) -> p a b", a=4), hc_d.bufs))
            P.dma(candbs[:], candbs_d)
            P.dma(forceds[:], forceds_d)
            P.dma(riota[:], riota_d)
            P.dma(jcol[:], jcol_d)
            P.dma(ptT[:], ptT_d)
            P.dma(ptrep[:], ptrep_d)
            P.dma(s_gfin[:], V(gfin.ap[0:4, :], gfin.bufs))
            P.dma(x[0:4, 0, :], xs_d)
            s_even()
            if "s_ffn" not in self.skip:
                s_ffn(0)
            if "s_odd" not in self.skip:
                s_odd()
            if "s_ffn" not in self.skip:
                s_ffn(1)
            c = newcol()
            P.act(junk[0:4, :], x[0:4, 0, :], AF.Square, accum=c[0:4, 0:1])
            P.ts(c[0:4, 1:2], c[0:4, 0:1], 1.0 / D, EPS, ALU.mult, ALU.add)
            P.rsqrt(c[0:4, 2:3], c[0:4, 1:2])
            P.stt(s_og[:], x[0:4, 0, :], c[0:4, 2:3], s_gfin[:], ALU.mult, ALU.mult)
            P.dma(ys_out, s_og[:])

        for bt in range(NB):
            for s in range(4):
                t0 = bt * 512 + s * 128
                P.dma(x[:, s, :], V(xp.ap[t0:t0 + 128, :], xp.bufs))
            even_layer(bt)
            ffn(0, bt == NB - 1)
            if "mlstm" in self.parts:
                odd_layer(bt)
                if self.debug and bt == NB - 1:
                    pass
            ffn(1, bt == NB - 1)
            P.dma(g_fin[:], gfin)
            for s in range(4):
                t0 = bt * 512 + s * 128
                c = newcol()
                P.act(junk[:], x[:, s, :], AF.Square, accum=c[:, 0:1])
                P.ts(c[:, 1:2], c[:, 0:1], 1.0 / D, EPS, ALU.mult, ALU.add)
                P.rsqrt(c[:, 2:3], c[:, 1:2])
                o_ = acc[s % 2]
                for hf in range(2):
                    P.stt(o_[:], x[:, s, hf * 512:(hf + 1) * 512], c[:, 2:3], g_fin[:, hf * 512:(hf + 1) * 512], ALU.mult, ALU.mult)
                    P.dma(V(y_out.ap[t0:t0 + 128, hf * 512:(hf + 1) * 512], y_out.bufs), o_[:])
        if "gla" in self.parts:
            for p in range(2):
                P.dma(V(gla_out.ap[p * 128:(p + 1) * 128, :], gla_out.bufs), S32[p][:])
        if "mlstm" in self.parts:
            for h in range(4):
                P.dma(V(mc_out.ap[h], mc_out.bufs), C32[:, h, 0:256])
            P.dma(mn_out, V(C32.t[:, :, 256], [C32.buf]), nc_ok=True)
            P.dma(mm_out, m_bc[0:1, :])
        if self.samp:
            sample_pass()
        with self.st:
            self.R.emit()
        return self.nc


def _even_cols():
    gq, gk, gv, gg, ga = 0, 256, 512, 1024, 1536
    nq, nkv, ng = 1552, 2064, 2832
    cols = []
    cols += list(range(gq, gq + 256)) + list(range(gk, gk + 256))
    for g in range(4):
        for k in range(2):
            h = k * 4 + g
            cols += list(range(nq + h * 64, nq + h * 64 + 64))
    for br in range(3):
        cols += list(range(nkv + br * 256, nkv + br * 256 + 128))
    cols += list(range(nkv + 128, nkv + 256))
    cols += list(range(gk, gk + 256)) + list(range(ng, ng + 24)) + list(range(ga, ga + 16))
    cols += list(range(gv, gv + 512))
    cols += list(range(gg, gg + 512))
    cols += list(range(nkv, nkv + 512))
    cols += list(range(nkv + 512, nkv + 768))
    return np.array(cols)


def nsa_consts(T):
    import ml_dtypes
    bf = ml_dtypes.bfloat16
    NT = T // 128
    n = np.arange(128)
    tl = np.arange(128)
    cmpb = np.zeros((NT, 128, 128), np.float32)
    candb = np.zeros((NT, 128, 128), np.float32)
    forced = np.zeros((NT, 128, 128), np.float32)
    allowed = np.zeros((NT, 128, 128), np.float32)
    for qi in range(NT):
        t = 128 * qi + tl
        cur = t // 64
        cmpb[qi] = np.where(64 * n[:, None] + 63 <= t[None, :], 0.0, -BIG)
        candb[qi] = np.where((n[None, :] >= 1) & (n[None, :] <= cur[:, None] - 2), 0.0, -BIG)
        forced[qi] = ((n[None, :] == 0) | (n[None, :] == cur[:, None]) | (n[None, :] == cur[:, None] - 1)).astype(np.float32)
        allowed[qi] = (n[None, :] <= cur[:, None]).astype(np.float32)
    eall = np.zeros((128, NT, 128), np.float32)
    for kt in range(NT):
        if 2 * kt < 128:
            eall[2 * kt, kt, 0:64] = 1.0
        if 2 * kt + 1 < 128:
            eall[2 * kt + 1, kt, 64:128] = 1.0
    j = np.arange(128)
    causal = np.where(j[:, None] <= tl[None, :], 0.0, -BIG).astype(np.float32)
    winlow = np.where(j[:, None] > tl[None, :], 0.0, -BIG).astype(np.float32)
    return {
        "identb": np.eye(128, dtype=np.float32).astype(bf),
        "cmpb": cmpb.astype(bf), "eall": eall.astype(bf),
        "causal": np.tile(causal, (1, 4)).astype(bf), "winlow": np.tile(winlow, (1, 4)).astype(bf),
        "candb": candb, "forced": forced, "allowed": allowed,
    }


def _up_cols():
    cols = []
    for j in range(11):
        cols += list(range(256 * j, 256 * j + 256)) + list(range(DFF + 256 * j, DFF + 256 * j + 256))
    return np.array(cols)


def _rep(v, n=128):
    v = np.asarray(v, np.float32).reshape(1, -1)
    return np.ascontiguousarray(np.broadcast_to(v, (n, v.shape[1])))


def prep_shared(inp, T):
    f = lambda a: np.ascontiguousarray(np.asarray(a, np.float32))
    sh = {}
    for l in range(2):
        sh["gmix%d" % l] = f(np.asarray(inp["norm_mix"][l]).reshape(8, 128).T)
        sh["gffn%d" % l] = f(np.asarray(inp["norm_ffn"][l]).reshape(8, 128).T)
        sh["wU%d" % l] = f(np.asarray(inp["ffn_w_up"][l])[:, _up_cols()])
        sh["wD%d" % l] = f(inp["ffn_w_down"][l])
        cwl = np.concatenate([np.asarray(inp["ffn_conv_w"][l]), np.asarray(inp["ffn_conv_b"][l])[None]], 0)
        sh["cw%d" % l] = f(cwl.reshape(4, NCH, 128).transpose(2, 0, 1))
    sh["gfin"] = _rep(inp["norm_final"])
    sh["wE"] = f(np.asarray(inp["even_w_in"][0])[:, _even_cols()])
    sh["wEo"] = f(inp["even_w_out"][0])
    wo = np.asarray(inp["odd_w_in"][0], np.float32)
    ocols = list(range(0, 512)) + list(range(512, 1024)) + list(range(512, 1024)) + list(range(1024, 3072)) + list(range(3072, 3080))
    sh["wO"] = f(wo[:, ocols])
    sh["mnorm"] = _rep(inp["mlstm_norm"][0])
    sh["bif"] = _rep(np.concatenate([np.asarray(inp["mlstm_b_i"][0]), np.asarray(inp["mlstm_b_f"][0])]))
    sh["bicol"] = f(np.asarray(inp["mlstm_b_i"][0]).reshape(4, 1))
    s127 = np.zeros((128, 128), np.float32); s127[127, :] = 1.0
    sh["sel127"] = s127
    ii = np.arange(128)
    sh["trilb"] = np.where(ii[None, :] <= ii[:, None], 0.0, -BIG).astype(np.float32)
    oh = np.zeros((4, 4, 128), np.float32)
    for h in range(4):
        oh[h, h, :] = 1.0
    sh["onehot"] = f(oh.reshape(4, 512))
    sh["wOo"] = f(inp["odd_w_out"][0])
    sh["ident"] = np.eye(128, dtype=np.float32)
    i = np.arange(128)
    sh["mtri"] = (i[:, None] <= i[None, :]).astype(np.float32)
    sh["mstr"] = (i[:, None] > i[None, :]).astype(np.float32)
    sh["wa2"] = f(inp["gla_w_a2"][0])
    sh["ba"] = _rep(inp["gla_b_a"][0])
    sh["gnorm"] = _rep(np.tile(np.asarray(inp["gla_norm"][0]), 4))
    w1 = np.asarray(inp["nsa_cmp_w1"][0], np.float32)
    w1bd = np.zeros((2, 128, 64, 128), np.float32)
    for k in range(2):
        w1bd[:, k * 64:(k + 1) * 64, :, k * 64:(k + 1) * 64] = w1.transpose(0, 2, 1, 3)
    sh["w1d"] = f(w1bd.reshape(2, 128, 2, 32 * 128).transpose(0, 2, 1, 3))
    w2 = np.asarray(inp["nsa_cmp_w2"][0], np.float32)
    w2bd = np.zeros((128, 2, 128), np.float32)
    for k in range(2):
        w2bd[k * 64:(k + 1) * 64, :, k * 64:(k + 1) * 64] = w2.transpose(1, 0, 2)
    sh["w2"] = f(w2bd.reshape(128, 256))
    pe = np.asarray(inp["nsa_cmp_pe"][0], np.float32)
    peT = np.concatenate([pe[0].T, pe[1].T], axis=1)
    sh["peT"] = f(np.concatenate([peT, peT], axis=0))
    sh["gateb"] = _rep(inp["nsa_gate_b"][0])
    sh.update(nsa_consts(T))
    return sh


def sample_consts():
    import ml_dtypes
    bf = ml_dtypes.bfloat16
    c = {}
    c["riota"] = np.ascontiguousarray(np.broadcast_to(np.arange(128, dtype=np.float32)[None, :], (128, 128)))
    c["jcol"] = np.arange(128, dtype=np.float32).reshape(128, 1)
    c["eye4"] = np.eye(4, dtype=np.float32)
    cm = np.zeros((128, 4, 4), np.float32)
    for b in range(4):
        cm[:, b, b] = 1.0
    c["cmask"] = cm.reshape(128, 16)
    hc = np.zeros((2, 2, 2, 128), np.float32)
    for k in range(2):
        hc[k, k, 0, 0:64] = 1.0
        hc[k, k, 1, 64:128] = 1.0
    c["hc"] = hc.reshape(2, 512).astype(bf)
    cs = np.zeros((4, 4, 4), np.float32)
    for b in range(4):
        cs[:, b, b] = 1.0
    c["colsel"] = cs.reshape(4, 16).astype(bf)
    mn = np.zeros((128, 8), np.float32)
    mn[1:, 0:4] = -BIG
    mn[0, 4:8] = -BIG
    c["mnew"] = mn.astype(bf)
    cb = np.full((2, 256), -BIG, np.float32)
    cb[:, 1:255] = 0.0
    c["candbs"] = cb
    fo = np.zeros((2, 256), np.float32)
    fo[:, 0] = 1.0
    fo[:, 255] = 1.0
    c["forceds"] = fo
    return c


def prep_sample(inp, c):
    f = lambda a: np.ascontiguousarray(np.asarray(a, np.float32))
    b0 = 4 * c
    m = {}
    m["xs"] = f(np.asarray(inp["x_sample"])[b0:b0 + 4, 0, :])
    m["sgla"] = f(np.asarray(inp["state_gla"])[0, b0:b0 + 4])
    m["smc"] = f(np.asarray(inp["state_mlstm_c"])[0, b0:b0 + 4])
    m["smn"] = f(np.asarray(inp["state_mlstm_n"])[0, b0:b0 + 4])
    m["smm"] = f(np.asarray(inp["state_mlstm_m"])[0, b0:b0 + 4])
    m["scv"] = f(np.asarray(inp["state_ffn_conv"])[:, b0:b0 + 4])
    m["swin"] = f(np.asarray(inp["cache_win_kv"])[0, b0:b0 + 4].reshape(4, 512, 256))
    pt = np.asarray(inp["page_table"])[b0:b0 + 4].astype(np.int32)
    m["ptT"] = np.ascontiguousarray(pt.T)
    m["ptrep"] = np.ascontiguousarray(np.broadcast_to(pt[None, :, :], (128, 4, 128)))
    return m


def prep_sample_shared(inp):
    f = lambda a: np.ascontiguousarray(np.asarray(a, np.float32))
    sh = {}
    cp = np.asarray(inp["cache_cmp_kv"])
    sp = np.asarray(inp["cache_sel_kv"])
    nphys = cp.shape[1]
    sh["cpool"] = f(cp[0].reshape(nphys * 128, 256))
    sh["spool"] = f(sp[0].reshape(nphys * 128, 256))
    for l in range(2):
        cwl = np.concatenate([np.asarray(inp["ffn_conv_w"][l]), np.asarray(inp["ffn_conv_b"][l])[None]], 0)
        sh["cwrep%d" % l] = f(np.broadcast_to(cwl[None], (4, 4, DFF)))
    sh.update(sample_consts())
    return sh, nphys


_CACHE = {}
_TRACE = [False, None]


def get_prog(T, parts, debug=False, samp=False, nphys=5120):
    key = (T, tuple(parts), debug, samp, nphys)
    if key not in _CACHE:
        p = Prog(T, parts, debug, samp, nphys)
        p.build()
        _CACHE[key] = p
    return _CACHE[key]


def run(inp, T, parts=("gla", "nsa", "mlstm"), n_cores=8, debug=False, samp=False):
    sh = prep_shared(inp, T)
    nphys = 5120
    if samp:
        ssh, nphys = prep_sample_shared(inp)
        sh.update(ssh)
    prog = get_prog(T, parts, debug, samp, nphys)
    xpr = np.asarray(inp["x_prompt"], np.float32)
    in_maps = []
    for c in range(n_cores):
        m = dict(sh)
        m["xp"] = np.ascontiguousarray(xpr[c % xpr.shape[0], :T])
        if samp:
            m.update(prep_sample(inp, c))
        m = {k: v for k, v in m.items() if k in prog.dram}
        in_maps.append(m)
    if _TRACE[0]:
        res = run_bass_kernel_spmd(prog.nc, in_maps, core_ids=list(range(n_cores)), trace=True)
        _TRACE[1] = res.exec_time_ns
    else:
        res = run_bass_kernel_spmd(prog.nc, in_maps, core_ids=list(range(n_cores)))
    return res.results


def kernel(**inputs):
    T = 8192
    r = run(inputs, T, samp=True)
    f = np.float32
    cat = lambda key, cores: np.concatenate([np.asarray(r[c][key], f) for c in cores], 0)
    P4, A8 = range(4), range(8)
    y_p = np.stack([r[c]["y"] for c in P4]).astype(f)
    y_s = cat("ys", A8).reshape(32, 1, 1024)
    kvshape = lambda a, n: a.reshape(1, a.shape[0], n, 2, 2, 64)
    cmp_p = kvshape(np.stack([r[c]["kvc"] for c in P4]).astype(f), T)
    sel_p = kvshape(np.stack([r[c]["kvs"] for c in P4]).astype(f), T)
    win_p = kvshape(np.stack([r[c]["kvw"] for c in P4]).astype(f), 512)
    cmp_s = kvshape(cat("kvcs", A8), 1)
    sel_s = kvshape(cat("kvss", A8), 1)
    win_s = kvshape(cat("kvws", A8), 512)
    gla_p = np.stack([r[c]["glast"].reshape(4, 64, 128) for c in P4]).astype(f)[None]
    gla_s = cat("glas", A8)[None]
    mc_p = np.stack([r[c]["mlc"] for c in P4]).astype(f)[None]
    mc_s = cat("mlcs", A8)[None]
    mn_p = np.stack([r[c]["mln"].T for c in P4]).astype(f)[None]
    mn_s = cat("mlns", A8)[None]
    mm_p = np.stack([r[c]["mlm"][0] for c in P4]).astype(f)[None]
    mm_s = cat("mlms", A8)[None]
    cv_p = np.stack([np.stack([r[c]["cv%d" % l] for c in P4]) for l in range(2)]).astype(f)
    cv_s = np.concatenate([np.asarray(r[c]["cvs"], f) for c in A8], 1)
    return (y_p, y_s, cmp_p, cmp_s, sel_p, sel_s, win_p, win_s, gla_p, gla_s,
            mc_p, mc_s, mn_p, mn_s, mm_p, mm_s, cv_p, cv_s)
```
